# Optimizing a Trainium2 kernel written in Bass

```python
import jax
import jax.numpy as jnp
from jax import lax
import numpy as np

D_MODEL = 4096
BATCH = 4
SEQ = 2048
DEPTH = 1

CHUNK = 64
EPS = 1e-6
FFN_RES = 0.5
D_FF = 11008
D_CONV = 2048
CONV_W = 3
GLA_HEADS = 4
D_GLA_K = 1024
D_GLA_V = 2048
HEAD_K = D_GLA_K // GLA_HEADS
HEAD_V = D_GLA_V // GLA_HEADS
GATE_RANK = 16
GATE_TEMP = 16.0
N_BRANCH = 2
IN_SPLIT_SIZES = (D_CONV, D_CONV, D_CONV, D_GLA_K, D_GLA_K, D_GLA_V, D_GLA_V, GATE_RANK, D_MODEL, D_MODEL)
D_IN = 3 * D_CONV + 2 * D_GLA_K + 2 * D_GLA_V + GATE_RANK + N_BRANCH * D_MODEL

kernel_name = "hybrid_shortconv_gla_macaron_block"


def _rmsnorm(x, g):
    xf = x.astype(jnp.float32)
    y = xf * lax.rsqrt(jnp.mean(xf * xf, axis=-1, keepdims=True) + EPS)
    return (y * g.astype(jnp.float32)).astype(x.dtype)


def _swiglu(h, w_gate, w_up, w_down):
    return (jax.nn.silu(h @ w_gate) * (h @ w_up)) @ w_down


def _causal_short_conv(u, w, b):
    s = u.shape[1]
    up = jnp.pad(u, ((0, 0), (CONV_W - 1, 0), (0, 0)))
    out = b
    for tap in range(CONV_W):
        out = out + w[tap] * up[:, tap:tap + s, :]
    return out


def _gla_chunk_step(state, chunk):
    q, k, v, lcum = chunk
    decay = jnp.exp(-jnp.abs(lcum[:, :, :, None, :] - lcum[:, :, None, :, :]))
    scores = jnp.einsum('bhid,bhjd,bhijd->bhij', q, k, decay)
    o = jnp.einsum('bhij,bhjv->bhiv', scores, v) + jnp.einsum('bhid,bhdv->bhiv', q * jnp.exp(lcum), state)
    l_last = lcum[:, :, -1, :]
    k_dec = k * jnp.exp(l_last[:, :, None, :] - lcum)
    new_state = jnp.exp(l_last)[..., None] * state + jnp.einsum('bhjd,bhjv->bhdv', k_dec, v)
    return new_state, o


def _gla(q, k, v, log_alpha):
    b, s, _ = q.shape
    n = s // CHUNK

    def to_chunks(t, hd):
        return t.reshape(b, n, CHUNK, GLA_HEADS, hd).transpose(1, 0, 3, 2, 4).astype(jnp.float32)

    qc = to_chunks(q, HEAD_K) * (HEAD_K ** -0.5)
    kc = to_chunks(k, HEAD_K)
    vc = to_chunks(v, HEAD_V)
    lc = jnp.cumsum(to_chunks(log_alpha, HEAD_K), axis=3)
    s0 = jnp.zeros((b, GLA_HEADS, HEAD_K, HEAD_V), jnp.float32)
    _, o = lax.scan(_gla_chunk_step, s0, (qc, kc, vc, lc))
    return o.transpose(1, 0, 3, 2, 4).reshape(b, s, GLA_HEADS, HEAD_V)


def _mixer(h, w_in, conv_w, conv_b, w_conv_out, w_alpha_up, b_alpha, gla_norm_g, w_gla_out, b_merge, w_mix_out):
    b, s, _ = h.shape
    split_points = tuple(np.cumsum(IN_SPLIT_SIZES)[:-1].tolist())
    proj = h @ w_in
    cb, cc, cu, q, k, v, r, a_low, g_a, g_b = jnp.split(proj, split_points, axis=-1)
    y_a = (cb * _causal_short_conv(cc * cu, conv_w, conv_b)) @ w_conv_out
    log_alpha = jax.nn.log_sigmoid((a_low @ w_alpha_up + b_alpha).astype(jnp.float32)) / GATE_TEMP
    o = _rmsnorm(_gla(q, k, v, log_alpha), gla_norm_g)
    o = o.astype(h.dtype).reshape(b, s, D_GLA_V) * jax.nn.silu(r)
    y_b = o @ w_gla_out
    merged = jax.nn.sigmoid(g_a + b_merge[0]) * y_a + jax.nn.sigmoid(g_b + b_merge[1]) * y_b
    return merged @ w_mix_out


def setup_inputs(seed: int = 0) -> dict:
    key = jax.random.key(seed)
    ks = jax.random.split(key, 22)

    def nrm(k, shape, scale):
        return jax.random.normal(k, shape, jnp.float32) * scale

    def gain(k, shape):
        return 1.0 + 0.02 * jax.random.normal(k, shape, jnp.float32)

    L = DEPTH
    return {
        "x": nrm(ks[0], (BATCH, SEQ, D_MODEL), 1.0),
        "ffn1_norm_g": gain(ks[1], (L, D_MODEL)),
        "ffn1_w_gate": nrm(ks[2], (L, D_MODEL, D_FF), D_MODEL ** -0.5),
        "ffn1_w_up": nrm(ks[3], (L, D_MODEL, D_FF), D_MODEL ** -0.5),
        "ffn1_w_down": nrm(ks[4], (L, D_FF, D_MODEL), D_FF ** -0.5),
        "mix_norm_g": gain(ks[5], (L, D_MODEL)),
        "w_in": nrm(ks[6], (L, D_MODEL, D_IN), D_MODEL ** -0.5),
        "conv_w": nrm(ks[7], (L, CONV_W, D_CONV), CONV_W ** -0.5),
        "conv_b": nrm(ks[8], (L, D_CONV), 0.02),
        "w_conv_out": nrm(ks[9], (L, D_CONV, D_MODEL), D_CONV ** -0.5),
        "w_alpha_up": nrm(ks[10], (L, GATE_RANK, D_GLA_K), GATE_RANK ** -0.5),
        "b_alpha": nrm(ks[11], (L, D_GLA_K), 0.02),
        "gla_norm_g": gain(ks[12], (L, HEAD_V)),
        "w_gla_out": nrm(ks[13], (L, D_GLA_V, D_MODEL), D_GLA_V ** -0.5),
        "b_merge": nrm(ks[14], (L, N_BRANCH, D_MODEL), 0.02),
        "w_mix_out": nrm(ks[15], (L, D_MODEL, D_MODEL), D_MODEL ** -0.5),
        "ffn2_norm_g": gain(ks[16], (L, D_MODEL)),
        "ffn2_w_gate": nrm(ks[17], (L, D_MODEL, D_FF), D_MODEL ** -0.5),
        "ffn2_w_up": nrm(ks[18], (L, D_MODEL, D_FF), D_MODEL ** -0.5),
        "ffn2_w_down": nrm(ks[19], (L, D_FF, D_MODEL), D_FF ** -0.5),
        "final_norm_g": gain(ks[20], (D_MODEL,)),
    }


def reference(x, ffn1_norm_g, ffn1_w_gate, ffn1_w_up, ffn1_w_down, mix_norm_g, w_in, conv_w, conv_b,
              w_conv_out, w_alpha_up, b_alpha, gla_norm_g, w_gla_out, b_merge, w_mix_out,
              ffn2_norm_g, ffn2_w_gate, ffn2_w_up, ffn2_w_down, final_norm_g):
    for l in range(DEPTH):
        x = x + FFN_RES * _swiglu(_rmsnorm(x, ffn1_norm_g[l]), ffn1_w_gate[l], ffn1_w_up[l], ffn1_w_down[l])
        x = x + _mixer(_rmsnorm(x, mix_norm_g[l]), w_in[l], conv_w[l], conv_b[l], w_conv_out[l],
                       w_alpha_up[l], b_alpha[l], gla_norm_g[l], w_gla_out[l], b_merge[l], w_mix_out[l])
        x = x + FFN_RES * _swiglu(_rmsnorm(x, ffn2_norm_g[l]), ffn2_w_gate[l], ffn2_w_up[l], ffn2_w_down[l])
    return _rmsnorm(x, final_norm_g)
```

```python
import math
from contextlib import ExitStack

import numpy as np
import concourse.bass as bass
import concourse.mybir as mybir
from concourse.bass_utils import run_bass_kernel_spmd

F32 = mybir.dt.float32
BF16 = mybir.dt.bfloat16
I32 = mybir.dt.int32
ALU = mybir.AluOpType
AF = mybir.ActivationFunctionType

D = 4096
KC = D // 128
DFF = 11008
FC = DFF // 128
DCONV = 2048
CC = DCONV // 128
HEADS = 4
DK = 1024
DV = 2048
HK = 256
HV = 512
RANK = 16
EPS = 1e-6
T = 512
NS = 4
SEQ = 2048
O_CB, O_CC, O_CU, O_Q, O_K, O_V, O_R, O_A, O_GA, O_GB = 0, 2048, 4096, 6144, 7168, 8192, 10240, 12288, 12304, 16400
DIN = 20496
NSLOT = 4
FGRP = 16


class _Op:
    __slots__ = ("eng", "emit", "reads", "writes", "dma_key", "deps", "signal", "sigval", "reqs", "custom")

    def __init__(self, eng, emit, reads, writes, dma_key):
        self.eng = eng
        self.emit = emit
        self.reads = reads
        self.writes = writes
        self.dma_key = dma_key
        self.deps = ()
        self.signal = False
        self.sigval = 0
        self.reqs = ()
        self.custom = False


class Prog:
    ENGS = ("pe", "act", "dve", "pool", "sp")

    def __init__(self, nc):
        self.nc = nc
        self.ops = []
        self.last_w = {}
        self.readers = {}
        self.final_dma = []

    def op(self, eng, emit, r=(), w=(), custom=False):
        o = _Op(eng, emit, tuple(r), tuple(w), None)
        o.custom = custom
        self._track(o)
        return o

    def dma(self, queue, emit, r=(), w=(), key=None, final=False, custom=False):
        if key is None:
            assert len(w) == 1
            key = w[0]
        o = _Op(queue, emit, tuple(r), tuple(w), key)
        o.custom = custom
        self._track(o)
        if final:
            self.final_dma.append(o)
        return o

    def _track(self, o):
        idx = len(self.ops)
        deps = set()
        for a in o.reads:
            lw = self.last_w.get(a)
            if lw is not None:
                deps.add(lw)
        for a in o.writes:
            lw = self.last_w.get(a)
            if lw is not None:
                deps.add(lw)
            rl = self.readers.get(a)
            if rl:
                deps.update(rl.values())
        deps.discard(idx)
        o.deps = deps
        for a in o.reads:
            self.readers.setdefault(a, {})[(o.eng, o.dma_key)] = idx
        for a in o.writes:
            self.last_w[a] = idx
            self.readers[a] = {}
        self.ops.append(o)

    def build(self, stack):
        nc = self.nc
        ops = self.ops

        def skip(do, o):
            if do.dma_key is not None:
                return False
            if do.eng == "pe":
                return o.eng == "pe" and o.dma_key is None
            return do.eng in ("sp", "pool") and o.eng == do.eng

        for o in ops:
            for d in o.deps:
                do = ops[d]
                if do.dma_key is None and not skip(do, o):
                    do.signal = True
        cnt = {e: 0 for e in self.ENGS}
        dcnt = {}
        for o in ops:
            if o.dma_key is not None:
                dcnt[o.dma_key] = dcnt.get(o.dma_key, 0) + 1
                o.sigval = 16 * dcnt[o.dma_key]
            elif o.signal:
                cnt[o.eng] += 1
                o.sigval = cnt[o.eng]
        esem = {e: stack.enter_context(nc.semaphore("s_" + e)) for e in ("pe", "act", "dve", "pool")}
        dsem = {}
        for i, k in enumerate(dcnt):
            dsem[k] = stack.enter_context(nc.semaphore("d%d" % i))
        self.n_sems = 4 + len(dsem)
        self.counts = dict(cnt)
        for o in ops:
            req = {}
            for d in o.deps:
                do = ops[d]
                if do.dma_key is not None:
                    s = dsem[do.dma_key]
                    k = ("d", do.dma_key)
                else:
                    if skip(do, o):
                        continue
                    s = esem[do.eng]
                    k = ("e", do.eng)
                if k not in req or req[k][1] < do.sigval:
                    req[k] = (s, do.sigval)
            o.reqs = req
        block = stack.enter_context(nc.Block())
        by_eng = {e: [o for o in ops if o.eng == e] for e in self.ENGS}
        final_dma = self.final_dma

        def run(eng_name, eng):
            known = {}
            for o in by_eng[eng_name]:
                for k, (s, v) in o.reqs.items():
                    if known.get(k, 0) < v:
                        eng.wait_ge(s, v)
                        known[k] = v
                def finish(ins, o=o):
                    if o.dma_key is not None:
                        ins.then_inc(dsem[o.dma_key], 16)
                    elif o.signal:
                        ins.then_inc(esem[eng_name], 1)

                if o.custom:
                    o.emit(eng, finish)
                else:
                    finish(o.emit(eng))
            if eng_name == "sp":
                for o in final_dma:
                    eng.wait_ge(dsem[o.dma_key], o.sigval)

        @block.tensor
        def _(e):
            run("pe", e)

        @block.scalar
        def _(e):
            run("act", e)

        @block.vector
        def _(e):
            run("dve", e)

        @block.gpsimd
        def _(e):
            run("pool", e)

        @block.sync
        def _(e):
            run("sp", e)


def build_program(n_tiles, tiles_per_seq, dbg=None, exchange=True):
    nc = bass.Bass("TRN2", target_bir_lowering=False)
    NTOK = n_tiles * T

    def din(name, shape, dt=F32):
        return nc.dram_tensor(name, list(shape), dt, kind="ExternalInput").ap()

    x_in = din("x", [NTOK, D])
    out_d = nc.dram_tensor("out", [NTOK, D], F32, kind="ExternalOutput").ap()
    xs_d = nc.dram_tensor("xspill", [NS, 128, D], F32).ap()
    g_norm = {k: din("g_" + k, [D]) for k in ("ffn1", "mix", "ffn2", "final")}
    ffn_w = {}
    for k in ("ffn1", "ffn2"):
        ffn_w[k] = (din(k + "_wg", [FC, 128, KC * 128]), din(k + "_wu", [FC, 128, KC * 128]),
                    din(k + "_wd", [8, FC, 128, 512]))
    win_f = din("win_f", [128, 128, KC * 128])
    win_t = din("win_t", [8, 4, 128, 8 * 512])
    win_a = din("win_a", [128, KC * RANK])
    wco = din("wco", [KC, 128, CC * 128])
    wgo = din("wgo", [KC, 128, CC * 128])
    wmo = din("wmo", [8, 4, 128, 8 * 512])
    wup_d = din("wup", [RANK, DK])
    gng_d = din("gng", [HV])
    NPC = 3 * CC + CC + 8 + 2 * KC
    pcol_d = din("pcol", [128, NPC])
    cmat_d = din("cmat", [128, 128 * 3 + 512])
    role_d = din("role", [128, 2])
    par_d = din("par", [1, 1], I32)
    nonce_d = din("nonce", [1, 1], I32)
    NR = max(n_tiles, 1)
    sh_S = nc.dram_tensor("sh_S", [2, NR, 128, HEADS * 2 * HV], F32, addr_space="Shared").ap()
    sh_C = nc.dram_tensor("sh_C", [2, NR, 128, HEADS * 2 * HV], F32, addr_space="Shared").ap()
    sh_D = nc.dram_tensor("sh_D", [2, NR, 128, 8], F32, addr_space="Shared").ap()
    sh_H = nc.dram_tensor("sh_H", [2, NR, 128, 2 * CC], F32, addr_space="Shared").ap()
    sh_F = nc.dram_tensor("sh_F", [2, 8], I32, addr_space="Shared").ap()
    dbg_t = {}
    if dbg:
        for name, (shape, dt_) in dbg.items():
            dbg_t[name] = nc.dram_tensor("dbg_" + name, list(shape), dt_, kind="ExternalOutput").ap()

    with ExitStack() as st:
        P = Prog(nc)
        sb = lambda n, shp, dt: st.enter_context(nc.sbuf_tensor("sb_" + n, shp, dt))
        ring = [sb("ring%d" % i, [128, 4096], BF16) for i in range(NSLOT)]
        hT = sb("hT", [128, KC, T], BF16)
        xr = sb("xr", [128, 32768], BF16)
        bq = sb("bq", [128, 16384], BF16)
        st_f = sb("st_f", [128, HEADS, 2, HV], F32)
        cmat = sb("cmat", [128, 128 * 3 + 512], F32)
        ident = sb("ident", [128, 128], BF16)
        pcol = sb("pcol", [128, NPC], F32)
        negb = sb("negb", [128, 8], F32)
        gng = sb("gng", [128, HV], F32)
        wup = sb("wup", [RANK, DK], F32)
        alT = sb("alT", [RANK, T], F32)
        halo = sb("halo", [128, CC, 2], F32)
        hin = sb("hin", [128, CC, 2], F32)
        hs1 = sb("hs1", [128, CC, 2], F32)
        acc0 = sb("acc0", [128, CC, 2], F32)
        cb0 = sb("cb0", [128, CC, 2], F32)
        fx = sb("fx", [128, 4, CC], F32)
        role = sb("role", [128, 2], F32)
        dloc = sb("dloc", [128, 8], F32)
        coef = sb("coef", [128, 8], F32)
        stat = sb("stat", [128, 16], F32)
        tmpf = [sb("tmpf%d" % i, [128, T], F32) for i in range(2)]
        ps = st.enter_context(nc.psum_tensor("ps", [128, 8, 512], F32))

        x_sb = xr[:, :].bitcast(F32).rearrange("p (s d) -> p s d", s=NS)
        gb = bq[:, 0:8192].bitcast(F32)
        hp = bq[:, 8192:12288]
        hp2 = [bq[:, 8192:12288], bq[:, 12288:16384]]
        actT = bq[:, 0:FGRP * T].rearrange("p (f t) -> p f t", f=FGRP)
        mergedT = bq[:, :].rearrange("p (c t) -> p c t", c=KC)
        maskL = cmat[:, 128:256]
        maskU = cmat[:, 256:384]
        rmask = cmat[:, 384:896]

        def xr_bf(lo, hi):
            return xr[:, lo // 2:hi // 2]

        def xr_f32(lo, hi):
            return xr[:, lo // 2:hi // 2].bitcast(F32)

        def XA(lo, hi):
            return [("XR", p) for p in range(lo // 2048, (hi + 2047) // 2048)]

        def BA(lo, hi):
            return [("BQ", p) for p in range(lo // 2048, (hi + 2047) // 2048)]

        K = 1024
        OGT = (0, 16 * K)
        PRD = (16 * K, 32 * K)
        Q1, Q2, K1, K2 = [(32 * K + i * 2 * K, 34 * K + i * 2 * K) for i in range(4)]
        K1T = (40 * K, 42 * K)
        VH = (42 * K, 46 * K)
        SR = (46 * K, 50 * K)
        OG = (50 * K, 54 * K)
        TA = (54 * K, 56 * K)
        TB = (56 * K, 58 * K)
        CUM = (58 * K, 60 * K)
        SC1 = (60 * K, 60 * K + 512)
        SC2 = (60 * K + 512, 61 * K)
        SCT = (61 * K, 61 * K + 256)
        T1 = (62 * K, 64 * K)
        A_STB = BA(0, 8 * K)
        A_EL = BA(8 * K, 12 * K)
        A_ENL = BA(12 * K, 16 * K)
        ogT = xr_bf(*OGT).rearrange("p (c t) -> p c t", c=16)
        stb = bq[:, 0:2048].rearrange("p (h c v) -> p h c v", h=2, c=2)
        prodT = xr_bf(*PRD).rearrange("p (c t) -> p c t", c=CC)

        state = {"slot": 0, "bank": 0}

        def next_slot():
            i = state["slot"] % NSLOT
            state["slot"] += 1
            return i

        def next_bank():
            i = state["bank"] % 8
            state["bank"] += 1
            return i

        def wload(src_ap, nelem, view=None):
            i = next_slot()
            dst = ring[i][:, 0:nelem]
            if view is not None:
                dst = view(dst)
            P.dma("pool", lambda e, dst=dst, src=src_ap: e.dma_start(out=dst, in_=src), w=[("ring", i)])
            return i

        def mm(bank, lhsT, rhs, start, stop, r, out=None):
            o = ps[:, bank, :] if out is None else out
            P.op("pe", lambda e, o=o, lhsT=lhsT, rhs=rhs, start=start, stop=stop:
                 e.matmul(o, lhsT, rhs, start=start, stop=stop), r=r, w=[("ps", bank)])

        def dbg_dump(name, src_ap, atoms):
            if name in dbg_t:
                P.dma("sp", lambda e, d=dbg_t[name], s=src_ap: e.dma_start(out=d, in_=s), r=atoms,
                      w=[("dbg", name)], final=True)

        P.dma("sp", lambda e: e.dma_start(out=cmat[:], in_=cmat_d), w=["cmat"])
        P.dma("sp", lambda e: e.dma_start(out=pcol[:], in_=pcol_d), w=["pcol"])
        P.dma("sp", lambda e: e.dma_start(out=wup[:], in_=wup_d), w=["wup"])
        P.dma("sp", lambda e: e.dma_start(out=gng[:], in_=gng_d.partition_broadcast(128)), w=["gng"])
        P.op("dve", lambda e: e.tensor_copy(ident[:], cmat[:, 0:128]), r=["cmat"], w=["ident"])
        CW0 = 0
        CB0 = 3 * CC
        BA0 = 4 * CC
        BM0 = 4 * CC + 8
        P.op("dve", lambda e: e.tensor_scalar(negb[:], pcol[:, BA0:BA0 + 8], -1.0, None, ALU.mult),
             r=["pcol"], w=["negb"])

        P.op("dve", lambda e: e.memset(stat[:, 10:11], 1.0), w=[("stat", 10)])
        P.op("dve", lambda e: e.memset(stat[:, 11:12], EPS), w=[("stat", 11)])

        def rstd_ops(ss, rs, n, a_ss, a_rs):
            P.op("act", lambda e: e.activation(rs, ss, AF.Sqrt, bias=stat[:, 11:12], scale=1.0 / n),
                 r=[a_ss, ("stat", 11)], w=[a_rs])
            P.op("dve", lambda e: e.reciprocal(rs, rs), r=[a_rs], w=[a_rs])

        XS = [[("XR", s * 8 + c) for c in range(8)] for s in range(NS)]

        junk = hT[:, 24:32, :].rearrange("p a b -> p (a b)")
        A_junk = [("hT", kc, s) for kc in range(24, 32) for s in range(NS)]

        def norm_stats():
            for s in range(NS):
                ss = stat[:, s:s + 1]
                rs = stat[:, 4 + s:5 + s]
                P.op("dve", lambda e, ss=ss: e.memset(ss, 0.0), w=[("stat", s)])
                P.op("act", lambda e, s=s, ss=ss: e.activation(junk, x_sb[:, s, :], AF.Square, accum_out=ss),
                     r=XS[s], w=A_junk + [("stat", s)])
                rstd_ops(ss, rs, D, ("stat", s), ("stat", 4 + s))

        def norm_to_hT(gkey):
            P.dma("sp", lambda e: e.dma_start(out=gb, in_=g_norm[gkey].partition_broadcast(128)),
                  w=BA(0, 16 * K), key=("gbload",))
            norm_stats()
            for s in range(NS):
                rs = stat[:, 4 + s:5 + s]
                hp = hp2[s % 2]
                A_hp = BA(16 * K + (s % 2) * 8 * K, 24 * K + (s % 2) * 8 * K)
                P.op("dve", lambda e, s=s, rs=rs, hp=hp: e.scalar_tensor_tensor(hp, x_sb[:, s, :], rs, gb, ALU.mult, ALU.mult),
                     r=XS[s] + BA(0, 16 * K) + [("stat", 4 + s)], w=A_hp)
                for k4 in range(KC // 4):
                    b = next_bank()
                    pb = ps[:, b, :].bitcast(BF16)
                    for j in range(4):
                        kc = k4 * 4 + j
                        P.op("pe", lambda e, pb=pb, j=j, kc=kc, hp=hp: e.transpose(pb[:, j * 128:(j + 1) * 128],
                                                                            hp[:, kc * 128:(kc + 1) * 128], ident[:]),
                             r=A_hp + ["ident"], w=[("ps", b)])
                    eng = "act" if k4 % 2 == 0 else "dve"
                    dst = hT[:, k4 * 4:k4 * 4 + 4, s * 128:(s + 1) * 128]
                    src = pb[:, 0:512].rearrange("p (a b) -> p a b", a=4)
                    if eng == "act":
                        P.op("act", lambda e, dst=dst, src=src: e.copy(dst, src), r=[("ps", b)],
                             w=[("hT", kc, s) for kc in range(k4 * 4, k4 * 4 + 4)])
                    else:
                        P.op("dve", lambda e, dst=dst, src=src: e.tensor_copy(dst, src), r=[("ps", b)],
                             w=[("hT", kc, s) for kc in range(k4 * 4, k4 * 4 + 4)])

        def hT_atoms(kc):
            return [("hT", kc, s) for s in range(NS)]

        def ffn(key):
            wg_d, wu_d, wd_d = ffn_w[key]
            norm_to_hT(key)
            groups = [(f0, min(FGRP, FC - f0)) for f0 in range(0, FC, FGRP)]
            for (f0, nf) in groups:
                for j in range(nf):
                    fc = f0 + j
                    sg_i = wload(wg_d[fc], 4096)
                    su_i = wload(wu_d[fc], 4096)
                    bg, bu = next_bank(), next_bank()
                    for kc in range(KC):
                        mm(bg, ring[sg_i][:, kc * 128:(kc + 1) * 128], hT[:, kc, :], kc == 0, kc == KC - 1,
                           r=[("ring", sg_i)] + hT_atoms(kc))
                    for kc in range(KC):
                        mm(bu, ring[su_i][:, kc * 128:(kc + 1) * 128], hT[:, kc, :], kc == 0, kc == KC - 1,
                           r=[("ring", su_i)] + hT_atoms(kc))
                    tf = tmpf[fc % 2]
                    P.op("act", lambda e, tf=tf, bg=bg: e.activation(tf[:], ps[:, bg, :], AF.Silu),
                         r=[("ps", bg)], w=[("tmpf", fc % 2)])
                    P.op("dve", lambda e, tf=tf, bu=bu, j=j: e.tensor_tensor(actT[:, j, :], tf[:], ps[:, bu, :], ALU.mult),
                         r=[("tmpf", fc % 2), ("ps", bu)], w=[("BQ", j // 2)])
                for cg in range(8):
                    banks = [next_bank() for _ in range(NS)]
                    for j0 in range(0, nf, 8):
                        n = min(8, nf - j0)
                        si = wload(wd_d[cg, f0 + j0:f0 + j0 + n].rearrange("f p c -> p f c"), n * 512,
                                   view=lambda a, n=n: a.rearrange("p (f c) -> p f c", f=n))
                        for jj in range(n):
                            j = j0 + jj
                            for s in range(NS):
                                mm(banks[s], actT[:, j, s * 128:(s + 1) * 128], ring[si][:, jj * 512:(jj + 1) * 512],
                                   j == 0, j == nf - 1, r=[("ring", si), ("BQ", j // 2)])
                    for s in range(NS):
                        xs = x_sb[:, s, cg * 512:(cg + 1) * 512]
                        P.op("dve", lambda e, xs=xs, b=banks[s]: e.scalar_tensor_tensor(xs, ps[:, b, :], 0.5, xs,
                                                                                         ALU.mult, ALU.add),
                             r=[("ps", banks[s]), ("XR", s * 8 + cg)], w=[("XR", s * 8 + cg)])

        def fchunk(col_chunk, bank):
            si = wload(win_f[col_chunk], 4096)
            for kc in range(KC):
                mm(bank, ring[si][:, kc * 128:(kc + 1) * 128], hT[:, kc, :], kc == 0, kc == KC - 1,
                   r=[("ring", si)] + hT_atoms(kc))

        def tgroup(grp):
            banks = [next_bank() for _ in range(NS)]
            for kq in range(4):
                si = wload(win_t[grp, kq], 4096)
                for k8 in range(8):
                    kc = kq * 8 + k8
                    for s in range(NS):
                        mm(banks[s], hT[:, kc, s * 128:(s + 1) * 128], ring[si][:, k8 * 512:(k8 + 1) * 512],
                           kc == 0, kc == KC - 1, r=[("ring", si), ("hT", kc, s)])
            return banks

        regs = {}

        def sp_init(e, finish):
            regs["par"] = e.alloc_register("r_par")
            regs["nonce"] = e.alloc_register("r_nonce")
            regs["fv"] = e.alloc_register("r_fv")
            regs["pr"] = e.alloc_register("r_pr")
            e.reg_load(regs["par"], par_d[0:1, 0:1])
            e.reg_load(regs["nonce"], nonce_d[0:1, 0:1])

        P.op("sp", sp_init, custom=True)
        P.dma("sp", lambda e: e.dma_start(out=role[:], in_=role_d), w=["role"])
        isA = role[:, 0:1]
        isB = role[:, 1:2]

        def own_slot_dma(dst_of_slot, src, r, w, key):
            def emit(e, finish):
                with e.If_eq(regs["par"], 0):
                    finish(e.dma_start(out=dst_of_slot(0), in_=src))
                with e.Else():
                    finish(e.dma_start(out=dst_of_slot(1), in_=src))
            P.dma("sp", emit, r=r, w=w, key=key, custom=True)

        def mixer(ti):
            first = (ti % tiles_per_seq) == 0
            norm_to_hT("mix")
            for s in range(NS):
                P.dma("sp", lambda e, s=s: e.dma_start(out=xs_d[s], in_=x_sb[:, s, :]), r=XS[s], w=[("xsd", s)])
            si = wload(win_a, KC * RANK)
            ba = next_bank()
            for kc in range(KC):
                mm(ba, ring[si][:, kc * RANK:(kc + 1) * RANK], hT[:, kc, :], kc == 0, kc == KC - 1,
                   r=[("ring", si)] + hT_atoms(kc), out=ps[0:RANK, ba, :])
            P.op("act", lambda e: e.copy(alT[:], ps[0:RANK, ba, :]), r=[("ps", ba)], w=["alT"])

            TCC = (40 * K, 42 * K)
            CCU = (42 * K, 44 * K + 16)
            ACC = (46 * K, 48 * K)
            tcc = xr_f32(*TCC)
            ccu = xr_f32(42 * K, 44 * K + 8)
            acc = xr_f32(*ACC)
            for c in range(CC):
                b_cc, b_cu, b_cb = next_bank(), next_bank(), next_bank()
                fchunk(O_CC // 128 + c, b_cc)
                fchunk(O_CU // 128 + c, b_cu)
                fchunk(O_CB // 128 + c, b_cb)
                P.op("act", lambda e, b=b_cc: e.copy(tcc, ps[:, b, :]), r=[("ps", b_cc)], w=XA(*TCC))
                P.op("dve", lambda e: e.memset(ccu[:, 0:2], 0.0), w=XA(*CCU))
                P.op("dve", lambda e, b=b_cu: e.tensor_tensor(ccu[:, 2:514], tcc, ps[:, b, :], ALU.mult),
                     r=[("ps", b_cu)] + XA(*TCC), w=XA(*CCU))
                P.op("dve", lambda e, c=c: e.tensor_copy(halo[:, c, :], ccu[:, 512:514]), r=XA(*CCU), w=["halo"])
                w0 = pcol[:, CW0 + c * 3 + 0:CW0 + c * 3 + 1]
                w1 = pcol[:, CW0 + c * 3 + 1:CW0 + c * 3 + 2]
                w2 = pcol[:, CW0 + c * 3 + 2:CW0 + c * 3 + 3]
                cb_ = pcol[:, CB0 + c:CB0 + c + 1]
                P.op("dve", lambda e, w2=w2, cb_=cb_: e.tensor_scalar(acc, ccu[:, 2:514], w2, cb_, ALU.mult, ALU.add),
                     r=XA(*CCU) + ["pcol"], w=XA(*ACC))
                P.op("dve", lambda e, w1=w1: e.scalar_tensor_tensor(acc, ccu[:, 1:513], w1, acc, ALU.mult, ALU.add),
                     r=XA(*CCU) + XA(*ACC) + ["pcol"], w=XA(*ACC))
                P.op("dve", lambda e, w0=w0: e.scalar_tensor_tensor(acc, ccu[:, 0:512], w0, acc, ALU.mult, ALU.add),
                     r=XA(*CCU) + XA(*ACC) + ["pcol"], w=XA(*ACC))
                P.op("dve", lambda e, c=c: e.tensor_copy(acc0[:, c, :], acc[:, 0:2]), r=XA(*ACC), w=["acc0"])
                P.op("dve", lambda e, c=c, b=b_cb: e.tensor_copy(cb0[:, c, :], ps[:, b, 0:2]), r=[("ps", b_cb)], w=["cb0"])
                P.op("dve", lambda e, c=c, b=b_cb: e.tensor_tensor(prodT[:, c, :], acc, ps[:, b, :], ALU.mult),
                     r=XA(*ACC) + [("ps", b_cb)], w=XA(16 * K + c * K, 17 * K + c * K))

            q1 = xr_bf(*Q1).rearrange("p (c t) -> p c t", c=2)
            q2 = xr_bf(*Q2).rearrange("p (c t) -> p c t", c=2)
            k1 = xr_bf(*K1).rearrange("p (c t) -> p c t", c=2)
            k2 = xr_bf(*K2).rearrange("p (c t) -> p c t", c=2)
            k1t = xr_bf(*K1T).rearrange("p (b d) -> p b d", b=4)
            vh_all = bq[:, 8192:16384].rearrange("p (h s v) -> p h s v", h=HEADS, s=NS)
            sr = xr_bf(*SR).rearrange("p (s v) -> p s v", s=NS)
            og = xr_bf(*OG).rearrange("p (s v) -> p s v", s=NS)
            eL = bq[:, 4096:6144].bitcast(F32).rearrange("p (c t) -> p c t", c=2)
            eNL = bq[:, 6144:8192].bitcast(F32).rearrange("p (c t) -> p c t", c=2)
            tA = xr_f32(*TA)
            tB = xr_f32(*TB)
            cum = xr_f32(*CUM)
            sc1 = xr_f32(*SC1)
            sc2 = xr_f32(*SC2)
            scT = xr_bf(*SCT)
            t1 = xr_f32(*T1)

            def head_prep(h, full):
                for dc in range(2):
                    c = 2 * h + dc
                    bz = next_bank()
                    P.op("pe", lambda e, bz=bz, c=c: e.matmul(ps[:, bz, :], wup[:, c * 128:(c + 1) * 128], alT[:],
                                                               start=True, stop=True),
                         r=["wup", "alT"], w=[("ps", bz)])
                    P.op("act", lambda e, bz=bz, c=c: e.activation(tA, ps[:, bz, :], AF.Exp, bias=negb[:, c:c + 1], scale=-1.0),
                         r=[("ps", bz), "negb"], w=XA(*TA))
                    P.op("act", lambda e: e.activation(tB, tA, AF.Ln, bias=stat[:, 10:11], scale=1.0),
                         r=XA(*TA) + [("stat", 10)], w=XA(*TB))
                    P.op("dve", lambda e: e.tensor_tensor_scan(cum, rmask, tB, 0.0, ALU.mult, ALU.add),
                         r=XA(*TB) + ["cmat"], w=XA(*CUM))
                    P.op("act", lambda e, dc=dc: e.activation(eL[:, dc, :], cum, AF.Exp, scale=-1.0 / 16.0),
                         r=XA(*CUM), w=A_EL)
                    P.op("act", lambda e, dc=dc: e.activation(eNL[:, dc, :], cum, AF.Exp, scale=1.0 / 16.0),
                         r=XA(*CUM), w=A_ENL)
                    if full:
                        bqk = next_bank()
                        fchunk(O_Q // 128 + c, bqk)
                        P.op("dve", lambda e, b=bqk, dc=dc: e.scalar_tensor_tensor(q1[:, dc, :], ps[:, b, :], HK ** -0.5,
                                                                                     eL[:, dc, :], ALU.mult, ALU.mult),
                             r=[("ps", bqk)] + A_EL, w=XA(*Q1))
                        P.op("dve", lambda e, b=bqk, dc=dc: e.scalar_tensor_tensor(q2[:, dc, :], ps[:, b, :], HK ** -0.5,
                                                                                     eNL[:, dc, :], ALU.mult, ALU.mult),
                             r=[("ps", bqk)] + A_ENL, w=XA(*Q2))
                    bkk = next_bank()
                    fchunk(O_K // 128 + c, bkk)
                    P.op("dve", lambda e, b=bkk, dc=dc: e.tensor_tensor(k1[:, dc, :], ps[:, b, :], eNL[:, dc, :], ALU.mult),
                         r=[("ps", bkk)] + A_ENL, w=XA(*K1))
                    if full:
                        P.op("dve", lambda e, b=bkk, dc=dc: e.tensor_tensor(k2[:, dc, :], ps[:, b, :], eL[:, dc, :], ALU.mult),
                             r=[("ps", bkk)] + A_EL, w=XA(*K2))
                    else:
                        dl = dloc[:, c:c + 1]
                        P.op("dve", lambda e, dl=dl, dc=dc: e.tensor_copy(dl, eL[:, dc, 127:128]), r=A_EL, w=["dloc"])
                        for blk in range(1, 4):
                            P.op("dve", lambda e, dl=dl, dc=dc, blk=blk: e.tensor_tensor(
                                dl, dl, eL[:, dc, blk * 128 + 127:blk * 128 + 128], ALU.mult), r=A_EL + ["dloc"], w=["dloc"])
                for blk in range(4):
                    b = next_bank()
                    pb = ps[:, b, :].bitcast(BF16)
                    for dc in range(2):
                        P.op("pe", lambda e, pb=pb, dc=dc, blk=blk: e.transpose(pb[:, dc * 128:(dc + 1) * 128],
                                                                                 k1[:, dc, blk * 128:(blk + 1) * 128], ident[:]),
                             r=XA(*K1) + ["ident"], w=[("ps", b)])
                    P.op("act", lambda e, pb=pb, blk=blk: e.copy(k1t[:, blk, :], pb[:, 0:256]), r=[("ps", b)], w=XA(*K1T))
                vh = vh_all[:, h]
                A_VH = BA(16 * K + h * 4 * K, 16 * K + (h + 1) * 4 * K)
                if (not full) or (not exchange):
                    banks = tgroup(h)
                    for s in range(NS):
                        P.op("act", lambda e, s=s, b=banks[s]: e.copy(vh[:, s, :], ps[:, b, :]), r=[("ps", banks[s])], w=A_VH)
                if full:
                    banks = tgroup(4 + h)
                    for s in range(NS):
                        P.op("act", lambda e, s=s, b=banks[s]: e.activation(sr[:, s, :], ps[:, b, :], AF.Silu),
                             r=[("ps", banks[s])], w=XA(*SR))

            def head_blocks(h, full):
                vh = vh_all[:, h]
                A_VH = BA(16 * K + h * 4 * K, 16 * K + (h + 1) * 4 * K)
                if full:
                    P.op("act", lambda e, h=h: e.copy(stb[:, 0], st_f[:, h]), r=["st_f"], w=[("BQ", 0)])
                for blk in range(4):
                    tk = slice(blk * 128, (blk + 1) * 128)
                    bs = [next_bank(), next_bank()]
                    for dc in range(2):
                        mm(bs[dc], k1t[:, blk, dc * 128:(dc + 1) * 128], vh[:, blk, :], True, True, r=XA(*K1T) + A_VH)
                    if full:
                        b1 = next_bank()
                        for dc in range(2):
                            mm(b1, k1[:, dc, tk], q1[:, dc, tk], dc == 0, dc == 1, r=XA(*K1) + XA(*Q1), out=ps[:, b1, 0:128])
                        b2 = next_bank()
                        for dc in range(2):
                            mm(b2, k2[:, dc, tk], q2[:, dc, tk], dc == 0, dc == 1, r=XA(*K2) + XA(*Q2), out=ps[:, b2, 0:128])
                        P.op("dve", lambda e, b1=b1: e.tensor_tensor(sc1, ps[:, b1, 0:128], maskL, ALU.mult),
                             r=[("ps", b1), "cmat"], w=XA(*SC1))
                        P.op("dve", lambda e, b2=b2: e.tensor_tensor(sc2, ps[:, b2, 0:128], maskU, ALU.mult),
                             r=[("ps", b2), "cmat"], w=XA(*SC2))
                        P.op("dve", lambda e: e.tensor_tensor(scT, sc1, sc2, ALU.add), r=XA(*SC1) + XA(*SC2), w=XA(*SCT))
                    for dc in range(2):
                        sl = eL[:, dc, blk * 128 + 127:blk * 128 + 128]
                        sf = st_f[:, h, dc, :]
                        P.op("dve", lambda e, b=bs[dc], sl=sl: e.tensor_scalar(tA, ps[:, b, :], sl, None, ALU.mult),
                             r=[("ps", bs[dc])] + A_EL, w=XA(*TA))
                        P.op("dve", lambda e, sf=sf, sl=sl: e.scalar_tensor_tensor(sf, sf, sl, tA, ALU.mult, ALU.add),
                             r=XA(*TA) + A_EL + ["st_f"], w=["st_f"])
                        if full and blk < 3:
                            P.op("act", lambda e, sf=sf, dc=dc, blk=blk: e.copy(stb[:, (blk + 1) % 2, dc, :], sf),
                                 r=["st_f"], w=[("BQ", (blk + 1) % 2)])
                    if full:
                        bo = next_bank()
                        mm(bo, scT, vh[:, blk, :], True, False, r=XA(*SCT) + A_VH)
                        for dc in range(2):
                            mm(bo, q1[:, dc, tk], stb[:, blk % 2, dc, :], False, dc == 1, r=XA(*Q1) + [("BQ", blk % 2)])
                        ss = stat[:, 8:9]
                        rs = stat[:, 9:10]
                        P.op("dve", lambda e, ss=ss: e.memset(ss, 0.0), w=[("stat", 8)])
                        P.op("act", lambda e, bo=bo, ss=ss: e.activation(t1, ps[:, bo, :], AF.Square, accum_out=ss),
                             r=[("ps", bo)], w=XA(*T1) + [("stat", 8)])
                        rstd_ops(ss, rs, HV, ("stat", 8), ("stat", 9))
                        P.op("dve", lambda e, bo=bo, rs=rs: e.scalar_tensor_tensor(t1, ps[:, bo, :], rs, gng[:], ALU.mult, ALU.mult),
                             r=[("ps", bo), ("stat", 9), "gng"], w=XA(*T1))
                        P.op("dve", lambda e, blk=blk: e.tensor_tensor(og[:, blk, :], t1, sr[:, blk, :], ALU.mult),
                             r=XA(*T1) + XA(*SR), w=XA(*OG))
                if full:
                    for vc in range(4):
                        b = next_bank()
                        pb = ps[:, b, :].bitcast(BF16)
                        for blk in range(4):
                            P.op("pe", lambda e, pb=pb, blk=blk, vc=vc: e.transpose(pb[:, blk * 128:(blk + 1) * 128],
                                                                                     og[:, blk, vc * 128:(vc + 1) * 128], ident[:]),
                                 r=XA(*OG) + ["ident"], w=[("ps", b)])
                        P.op("act", lambda e, pb=pb, h=h, vc=vc: e.copy(ogT[:, h * 4 + vc, :], pb[:, 0:512]),
                             r=[("ps", b)], w=XA(*OGT))

            st_flat = st_f[:].rearrange("p h c v -> p (h c v)")
            halo_flat = halo[:].rearrange("p c t -> p (c t)")
            if exchange:
                P.op("dve", lambda e: e.memset(st_f[:], 0.0), w=["st_f"])
                for h in range(HEADS):
                    head_prep(h, False)
                    head_blocks(h, False)
                own_slot_dma(lambda sl: sh_S[sl, ti], st_flat, r=["st_f"], w=[("shS", ti)], key=("shS",))
                own_slot_dma(lambda sl: sh_D[sl, ti], dloc[:], r=["dloc"], w=[("shD", ti)], key=("shD",))
                own_slot_dma(lambda sl: sh_H[sl, ti], halo_flat, r=["halo"], w=[("shH", ti)], key=("shH",))

                def handshake(e, finish, ti=ti, first=first):
                    fv, pr = regs["fv"], regs["pr"]
                    e.reg_add(fv, regs["nonce"], ti + 1)
                    with e.If_eq(regs["par"], 0):
                        e.reg_save(sh_F[0:1, ti:ti + 1], fv)
                    with e.Else():
                        e.reg_save(sh_F[1:2, ti:ti + 1], fv)
                    polls = [sh_F[0:1, ti:ti + 1]]
                    if not first:
                        polls.append(sh_F[1:2, 4 + ti - 1:4 + ti])
                    for k, fl in enumerate(polls):
                        if k == 1:
                            e.reg_add(fv, regs["nonce"], ti)
                        e.reg_mov(pr, 1)
                        with e.While(pr):
                            e.reg_load(pr, fl)
                            e.reg_sub(pr, pr, fv)

                deps = [("shS", ti), ("shD", ti), ("shH", ti)] + ([("shC", ti - 1)] if not first else [])
                P.op("sp", handshake, r=deps, w=[("hs", ti)], custom=True)
                head_prep(0, True)
                P.dma("sp", lambda e: e.dma_start(out=dloc[:], in_=sh_D[0, ti]), r=[("hs", ti), ("shD", ti)], w=["dloc"],
                      key=("ldD",))
                P.op("dve", lambda e: e.tensor_scalar(coef[:], dloc[:], isB, isA, ALU.mult, ALU.add),
                     r=["dloc", "role"], w=["coef"])
                P.dma("sp", lambda e: e.dma_start(out=hin[:].rearrange("p c t -> p (c t)"), in_=sh_H[0, ti]),
                      r=[("hs", ti), ("shH", ti)], w=["hin"], key=("ldH",))
                P.op("dve", lambda e: e.tensor_scalar(hin[:], hin[:], isB, None, ALU.mult), r=["hin", "role"], w=["hin"])
                if not first:
                    P.dma("sp", lambda e: e.dma_start(out=hs1[:].rearrange("p c t -> p (c t)"), in_=sh_H[1, ti - 1]),
                          r=[("hs", ti)], w=["hs1"], key=("ldH1",))
                    P.op("dve", lambda e: e.scalar_tensor_tensor(hin[:], hs1[:], isA, hin[:], ALU.mult, ALU.add),
                         r=["hin", "hs1", "role"], w=["hin"])
                for b8 in range(8):
                    tS = tmpf[b8 % 2]
                    tC = xr_f32(50 * K + (b8 % 2) * 2 * K, 52 * K + (b8 % 2) * 2 * K)
                    aC = XA(50 * K + (b8 % 2) * 2 * K, 52 * K + (b8 % 2) * 2 * K)
                    cs = slice(b8 * HV, (b8 + 1) * HV)
                    P.dma("sp", lambda e, tS=tS, cs=cs: e.dma_start(out=tS[:], in_=sh_S[0, ti][:, cs]),
                          r=[("hs", ti), ("shS", ti)], w=[("tmpf", b8 % 2)], key=("ldS", b8 % 2))
                    if first:
                        P.op("dve", lambda e, tS=tS, cs=cs: e.tensor_scalar(st_flat[:, cs], tS[:], isB, None, ALU.mult),
                             r=[("tmpf", b8 % 2), "role"], w=["st_f"])
                    else:
                        P.op("dve", lambda e, tS=tS: e.tensor_scalar(tS[:], tS[:], isB, None, ALU.mult),
                             r=[("tmpf", b8 % 2), "role"], w=[("tmpf", b8 % 2)])
                        P.dma("sp", lambda e, tC=tC, cs=cs: e.dma_start(out=tC, in_=sh_C[1, ti - 1][:, cs]),
                              r=[("hs", ti)], w=aC, key=("ldC", b8 % 2))
                        P.op("dve", lambda e, tS=tS, tC=tC, cs=cs, b8=b8: e.scalar_tensor_tensor(
                            st_flat[:, cs], tC, coef[:, b8:b8 + 1], tS[:], ALU.mult, ALU.add),
                            r=aC + [("tmpf", b8 % 2), "coef"], w=["st_f"])
                pw = pcol[:, CW0:CW0 + 3 * CC].rearrange("p (c t) -> p c t", t=3)
                P.op("dve", lambda e: e.tensor_tensor(fx[:, 0, :], pw[:, :, 1], hin[:, :, 1], ALU.mult), r=["pcol", "hin"], w=["fx"])
                P.op("dve", lambda e: e.tensor_tensor(fx[:, 1, :], pw[:, :, 0], hin[:, :, 0], ALU.mult), r=["pcol", "hin"], w=["fx"])
                P.op("dve", lambda e: e.tensor_tensor(fx[:, 0, :], fx[:, 0, :], fx[:, 1, :], ALU.add), r=["fx"], w=["fx"])
                P.op("dve", lambda e: e.tensor_tensor(fx[:, 1, :], pw[:, :, 0], hin[:, :, 1], ALU.mult), r=["pcol", "hin", "fx"], w=["fx"])
                for t in range(2):
                    P.op("dve", lambda e, t=t: e.tensor_tensor(fx[:, 2 + t, :], fx[:, t, :], acc0[:, :, t], ALU.add),
                         r=["fx", "acc0"], w=["fx"])
                    P.op("dve", lambda e, t=t: e.tensor_tensor(prodT[:, :, t], fx[:, 2 + t, :], cb0[:, :, t], ALU.mult),
                         r=["fx", "cb0"], w=XA(*PRD))
            else:
                if first:
                    P.op("dve", lambda e: e.memset(st_f[:], 0.0), w=["st_f"])
                head_prep(0, True)
            dbg_dump("prodT", xr_bf(*PRD), XA(*PRD))
            dbg_dump("stin", st_flat, ["st_f"])
            head_blocks(0, True)
            for h in range(1, HEADS):
                head_prep(h, True)
                head_blocks(h, True)
            if exchange and ti + 1 < n_tiles:
                own_slot_dma(lambda sl: sh_C[sl, ti], st_flat, r=["st_f"], w=[("shC", ti)], key=("shC",))

                def carry_flag(e, finish, ti=ti):
                    fv = regs["fv"]
                    e.reg_add(fv, regs["nonce"], ti + 1)
                    with e.If_eq(regs["par"], 0):
                        e.reg_save(sh_F[0:1, 4 + ti:5 + ti], fv)
                    with e.Else():
                        e.reg_save(sh_F[1:2, 4 + ti:5 + ti], fv)

                P.op("sp", carry_flag, r=[("shC", ti)], w=[("cf", ti)], custom=True)
            dbg_dump("ogT", xr_bf(*OGT), XA(*OGT))
            for dch in range(KC):
                o = 40 * K + (dch % 2) * 8 * K
                sa = xr_f32(o, o + 2 * K)
                sb_ = xr_f32(o + 2 * K, o + 4 * K)
                m1 = xr_f32(o + 4 * K, o + 6 * K)
                m2 = xr_f32(o + 6 * K, o + 8 * K)
                A_sa, A_sb, A_m1, A_m2 = [XA(o + i * 2 * K, o + (i + 1) * 2 * K) for i in range(4)]
                bga, bgb, bya, byb = next_bank(), next_bank(), next_bank(), next_bank()
                fchunk(64 + dch, bga)
                fchunk(96 + dch, bgb)
                si = wload(wco[dch], CC * 128)
                for cc in range(CC):
                    mm(bya, ring[si][:, cc * 128:(cc + 1) * 128], prodT[:, cc, :], cc == 0, cc == CC - 1,
                       r=[("ring", si)] + XA(16 * K + cc * K, 17 * K + cc * K))
                si = wload(wgo[dch], CC * 128)
                for vc in range(16):
                    mm(byb, ring[si][:, vc * 128:(vc + 1) * 128], ogT[:, vc, :], vc == 0, vc == 15,
                       r=[("ring", si)] + XA(*OGT))
                bm0 = pcol[:, BM0 + dch:BM0 + dch + 1]
                bm1 = pcol[:, BM0 + KC + dch:BM0 + KC + dch + 1]
                P.op("act", lambda e, sa=sa, b=bga, bm0=bm0: e.activation(sa, ps[:, b, :], AF.Sigmoid, bias=bm0),
                     r=[("ps", bga), "pcol"], w=A_sa)
                P.op("act", lambda e, sb_=sb_, b=bgb, bm1=bm1: e.activation(sb_, ps[:, b, :], AF.Sigmoid, bias=bm1),
                     r=[("ps", bgb), "pcol"], w=A_sb)
                P.op("dve", lambda e, m1=m1, sa=sa, b=bya: e.tensor_tensor(m1, sa, ps[:, b, :], ALU.mult),
                     r=A_sa + [("ps", bya)], w=A_m1)
                P.op("dve", lambda e, m2=m2, sb_=sb_, b=byb: e.tensor_tensor(m2, sb_, ps[:, b, :], ALU.mult),
                     r=A_sb + [("ps", byb)], w=A_m2)
                P.op("dve", lambda e, m1=m1, m2=m2, dch=dch: e.tensor_tensor(mergedT[:, dch, :], m1, m2, ALU.add),
                     r=A_m1 + A_m2, w=[("BQ", dch // 2)])
            dbg_dump("mergedT", bq[:, :], BA(0, 32 * K))
            for s in range(NS):
                P.dma("sp", lambda e, s=s: e.dma_start(out=x_sb[:, s, :], in_=xs_d[s]), r=[("xsd", s)], w=XS[s],
                      key=("xrl", s))
            for cg in range(8):
                banks = [next_bank() for _ in range(NS)]
                for kq in range(4):
                    si = wload(wmo[cg, kq], 4096)
                    for k8 in range(8):
                        kc = kq * 8 + k8
                        for s in range(NS):
                            mm(banks[s], mergedT[:, kc, s * 128:(s + 1) * 128], ring[si][:, k8 * 512:(k8 + 1) * 512],
                               kc == 0, kc == KC - 1, r=[("ring", si), ("BQ", kc // 2)])
                for s in range(NS):
                    xs = x_sb[:, s, cg * 512:(cg + 1) * 512]
                    P.op("dve", lambda e, xs=xs, b=banks[s]: e.tensor_tensor(xs, ps[:, b, :], xs, ALU.add),
                         r=[("ps", banks[s]), ("XR", s * 8 + cg)], w=[("XR", s * 8 + cg)])

        for ti in range(n_tiles):
            for s in range(NS):
                r0 = ti * T + s * 128
                P.dma("sp", lambda e, s=s, r0=r0: e.dma_start(out=x_sb[:, s, :], in_=x_in[r0:r0 + 128, :]), w=XS[s],
                      key=("xload", s))
            ffn("ffn1")
            if ti == 0:
                dbg_dump("x1", xr[:, :].bitcast(F32), [a for s in range(NS) for a in XS[s]])
            mixer(ti)
            if ti == 0:
                dbg_dump("x2", xr[:, :].bitcast(F32), [a for s in range(NS) for a in XS[s]])
            ffn("ffn2")
            P.dma("sp", lambda e: e.dma_start(out=gb, in_=g_norm["final"].partition_broadcast(128)),
                  w=BA(0, 16 * K), key=("gbload",))
            norm_stats()
            for s in range(NS):
                rs = stat[:, 4 + s:5 + s]
                P.op("dve", lambda e, s=s, rs=rs: e.scalar_tensor_tensor(x_sb[:, s, :], x_sb[:, s, :], rs, gb,
                                                                           ALU.mult, ALU.mult),
                     r=XS[s] + BA(0, 16 * K) + [("stat", 4 + s)], w=XS[s])
                r0 = ti * T + s * 128
                P.dma("sp", lambda e, s=s, r0=r0: e.dma_start(out=out_d[r0:r0 + 128, :], in_=x_sb[:, s, :]),
                      r=XS[s], w=[("out", ti, s)], key=("ostore", s), final=True)
        P.build(st)
        nc._prog_stats = (len(P.ops), P.n_sems, P.counts)
    return nc


def _f32(a):
    return np.ascontiguousarray(a, dtype=np.float32)


def prep_weights(inp):
    m = {}
    m["g_ffn1"] = _f32(inp["ffn1_norm_g"][0])
    m["g_mix"] = _f32(inp["mix_norm_g"][0])
    m["g_ffn2"] = _f32(inp["ffn2_norm_g"][0])
    m["g_final"] = _f32(inp["final_norm_g"])

    def fchunks(w):
        C = w.shape[1] // 128
        return _f32(w.reshape(KC, 128, C, 128).transpose(2, 1, 0, 3).reshape(C, 128, KC * 128))

    def rounds(w):
        R = w.shape[0] // 128
        return _f32(w.reshape(R, 128, 8, 512).transpose(2, 0, 1, 3))

    def tgroups(w):
        G = w.shape[1] // 512
        return _f32(w.reshape(4, 8, 128, G, 512).transpose(3, 0, 2, 1, 4).reshape(G, 4, 128, 8 * 512))

    for k in ("ffn1", "ffn2"):
        m[k + "_wg"] = fchunks(inp[k + "_w_gate"][0])
        m[k + "_wu"] = fchunks(inp[k + "_w_up"][0])
        m[k + "_wd"] = rounds(inp[k + "_w_down"][0])
    w_in = inp["w_in"][0]
    nat = fchunks(w_in[:, 0:O_V])
    ga = fchunks(w_in[:, O_GA:O_GA + D])
    gb_ = fchunks(w_in[:, O_GB:O_GB + D])
    m["win_f"] = np.concatenate([nat, ga, gb_], axis=0)
    m["win_t"] = np.concatenate([tgroups(w_in[:, O_V:O_V + DV]), tgroups(w_in[:, O_R:O_R + DV])], axis=0)
    wa = w_in[:, O_A:O_A + RANK]
    m["win_a"] = _f32(wa.reshape(KC, 128, RANK).transpose(1, 0, 2).reshape(128, KC * RANK))

    def ochunks(w):
        return _f32(w.reshape(CC, 128, KC, 128).transpose(2, 1, 0, 3).reshape(KC, 128, CC * 128))

    m["wco"] = ochunks(inp["w_conv_out"][0])
    m["wgo"] = ochunks(inp["w_gla_out"][0])
    wmo = inp["w_mix_out"][0]
    m["wmo"] = _f32(wmo.reshape(4, 8, 128, 8, 512).transpose(3, 0, 2, 1, 4).reshape(8, 4, 128, 8 * 512))
    m["wup"] = _f32(inp["w_alpha_up"][0])
    m["gng"] = _f32(inp["gla_norm_g"][0])
    cw = inp["conv_w"][0]
    pc = np.zeros((128, 3 * CC + CC + 8 + 2 * KC), np.float32)
    pc[:, 0:3 * CC] = cw.reshape(3, CC, 128).transpose(2, 1, 0).reshape(128, 3 * CC)
    pc[:, 3 * CC:4 * CC] = inp["conv_b"][0].reshape(CC, 128).T
    pc[:, 4 * CC:4 * CC + 8] = inp["b_alpha"][0].reshape(8, 128).T
    pc[:, 4 * CC + 8:4 * CC + 8 + KC] = inp["b_merge"][0, 0].reshape(KC, 128).T
    pc[:, 4 * CC + 8 + KC:] = inp["b_merge"][0, 1].reshape(KC, 128).T
    m["pcol"] = pc
    cm = np.zeros((128, 128 * 3 + 512), np.float32)
    cm[:, 0:128] = np.eye(128, dtype=np.float32)
    j = np.arange(128)[:, None]
    i = np.arange(128)[None, :]
    cm[:, 128:256] = (j <= i)
    cm[:, 256:384] = (j > i) & ((j // 64) == (i // 64))
    rm = np.ones(512, np.float32)
    rm[::128] = 0.0
    cm[:, 384:896] = rm[None, :]
    m["cmat"] = cm
    return m


N_CORES = 8
N_TILES = 2


def shard_x(x):
    B = x.shape[0]
    xt = x.reshape(B, SEQ // T, T, D)
    return [_f32(xt[c // 2, (c % 2)::2].reshape(N_TILES * T, D)) for c in range(2 * B)]


def unshard_out(outs, B):
    out = np.empty((B, SEQ // T, T, D), np.float32)
    for c in range(2 * B):
        out[c // 2, (c % 2)::2] = np.asarray(outs[c], dtype=np.float32).reshape(N_TILES, T, D)
    return out.reshape(B, SEQ, D)


def core_inputs(shared, xs, c, nonce):
    mp = dict(shared)
    mp["x"] = xs[c]
    role = np.zeros((128, 2), np.float32)
    role[:, c % 2] = 1.0
    mp["role"] = role
    mp["par"] = np.array([[c % 2]], np.int32)
    mp["nonce"] = np.array([[nonce]], np.int32)
    return mp


def kernel(**inputs):
    inp = {k: np.asarray(v) for k, v in inputs.items()}
    x = inp["x"]
    B = x.shape[0]
    shared = prep_weights(inp)
    nc = build_program(N_TILES, N_TILES)
    xs = shard_x(x)
    nonce = int(np.random.randint(1, 1 << 28)) * 4
    in_maps = [core_inputs(shared, xs, c, nonce) for c in range(N_CORES)]
    res = run_bass_kernel_spmd(nc, in_maps, core_ids=list(range(N_CORES)))
    return unshard_out([res.results[c]["out"] for c in range(N_CORES)], B)
```

```python
import math
from contextlib import ExitStack

import numpy as np
import concourse.bass as bass
import concourse.mybir as mybir
from concourse.bass_utils import run_bass_kernel_spmd

F32 = mybir.dt.float32
BF16 = mybir.dt.bfloat16
I32 = mybir.dt.int32
ALU = mybir.AluOpType
AF = mybir.ActivationFunctionType

D = 4096
KC = D // 128
DFF = 11008
FC = DFF // 128
DCONV = 2048
CC = DCONV // 128
HEADS = 4
DK = 1024
DV = 2048
HK = 256
HV = 512
RANK = 16
EPS = 1e-6
T = 512
NS = 4
SEQ = 2048
O_CB, O_CC, O_CU, O_Q, O_K, O_V, O_R, O_A, O_GA, O_GB = 0, 2048, 4096, 6144, 7168, 8192, 10240, 12288, 12304, 16400
DIN = 20496
NSLOT = 4
FGRP = 16


class _Op:
    __slots__ = ("eng", "emit", "reads", "writes", "dma_key", "deps", "signal", "sigval", "reqs", "custom")

    def __init__(self, eng, emit, reads, writes, dma_key):
        self.eng = eng
        self.emit = emit
        self.reads = reads
        self.writes = writes
        self.dma_key = dma_key
        self.deps = ()
        self.signal = False
        self.sigval = 0
        self.reqs = ()
        self.custom = False


class Prog:
    ENGS = ("pe", "act", "dve", "pool", "sp")

    def __init__(self, nc):
        self.nc = nc
        self.ops = []
        self.last_w = {}
        self.readers = {}
        self.final_dma = []

    def op(self, eng, emit, r=(), w=(), custom=False):
        o = _Op(eng, emit, tuple(r), tuple(w), None)
        o.custom = custom
        self._track(o)
        return o

    def dma(self, queue, emit, r=(), w=(), key=None, final=False, custom=False):
        if key is None:
            assert len(w) == 1
            key = w[0]
        o = _Op(queue, emit, tuple(r), tuple(w), key)
        o.custom = custom
        self._track(o)
        if final:
            self.final_dma.append(o)
        return o

    def _track(self, o):
        idx = len(self.ops)
        deps = set()
        for a in o.reads:
            lw = self.last_w.get(a)
            if lw is not None:
                deps.add(lw)
        for a in o.writes:
            lw = self.last_w.get(a)
            if lw is not None:
                deps.add(lw)
            rl = self.readers.get(a)
            if rl:
                deps.update(rl.values())
        deps.discard(idx)
        o.deps = deps
        for a in o.reads:
            self.readers.setdefault(a, {})[(o.eng, o.dma_key)] = idx
        for a in o.writes:
            self.last_w[a] = idx
            self.readers[a] = {}
        self.ops.append(o)

    def build(self, stack):
        nc = self.nc
        ops = self.ops

        def skip(do, o):
            if do.dma_key is not None:
                return False
            if do.eng == "pe":
                return o.eng == "pe" and o.dma_key is None
            return do.eng in ("sp", "pool") and o.eng == do.eng

        for o in ops:
            for d in o.deps:
                do = ops[d]
                if do.dma_key is None and not skip(do, o):
                    do.signal = True
        cnt = {e: 0 for e in self.ENGS}
        dcnt = {}
        for o in ops:
            if o.dma_key is not None:
                dcnt[o.dma_key] = dcnt.get(o.dma_key, 0) + 1
                o.sigval = 16 * dcnt[o.dma_key]
            elif o.signal:
                cnt[o.eng] += 1
                o.sigval = cnt[o.eng]
        esem = {e: stack.enter_context(nc.semaphore("s_" + e)) for e in ("pe", "act", "dve", "pool")}
        dsem = {}
        for i, k in enumerate(dcnt):
            dsem[k] = stack.enter_context(nc.semaphore("d%d" % i))
        self.n_sems = 4 + len(dsem)
        self.counts = dict(cnt)
        for o in ops:
            req = {}
            for d in o.deps:
                do = ops[d]
                if do.dma_key is not None:
                    s = dsem[do.dma_key]
                    k = ("d", do.dma_key)
                else:
                    if skip(do, o):
                        continue
                    s = esem[do.eng]
                    k = ("e", do.eng)
                if k not in req or req[k][1] < do.sigval:
                    req[k] = (s, do.sigval)
            o.reqs = req
        block = stack.enter_context(nc.Block())
        by_eng = {e: [o for o in ops if o.eng == e] for e in self.ENGS}
        final_dma = self.final_dma

        def run(eng_name, eng):
            known = {}
            for o in by_eng[eng_name]:
                for k, (s, v) in o.reqs.items():
                    if known.get(k, 0) < v:
                        eng.wait_ge(s, v)
                        known[k] = v
                def finish(ins, o=o):
                    if o.dma_key is not None:
                        ins.then_inc(dsem[o.dma_key], 16)
                    elif o.signal:
                        ins.then_inc(esem[eng_name], 1)

                if o.custom:
                    o.emit(eng, finish)
                else:
                    finish(o.emit(eng))
            if eng_name == "sp":
                for o in final_dma:
                    eng.wait_ge(dsem[o.dma_key], o.sigval)

        @block.tensor
        def _(e):
            run("pe", e)

        @block.scalar
        def _(e):
            run("act", e)

        @block.vector
        def _(e):
            run("dve", e)

        @block.gpsimd
        def _(e):
            run("pool", e)

        @block.sync
        def _(e):
            run("sp", e)


def build_program(n_tiles, tiles_per_seq, dbg=None, exchange=True):
    nc = bass.Bass("TRN2", target_bir_lowering=False)
    NTOK = n_tiles * T

    def din(name, shape, dt=F32):
        return nc.dram_tensor(name, list(shape), dt, kind="ExternalInput").ap()

    x_in = din("x", [NTOK, D])
    out_d = nc.dram_tensor("out", [NTOK, D], F32, kind="ExternalOutput").ap()
    xs_d = nc.dram_tensor("xspill", [NS, 128, D], F32).ap()
    g_norm = {k: din("g_" + k, [D]) for k in ("ffn1", "mix", "ffn2", "final")}
    ffn_w = {}
    for k in ("ffn1", "ffn2"):
        ffn_w[k] = (din(k + "_wg", [FC, 128, KC * 128]), din(k + "_wu", [FC, 128, KC * 128]),
                    din(k + "_wd", [8, FC, 128, 512]))
    win_f = din("win_f", [128, 128, KC * 128])
    win_t = din("win_t", [8, 4, 128, 8 * 512])
    win_a = din("win_a", [128, KC * RANK])
    wco = din("wco", [KC, 128, CC * 128])
    wgo = din("wgo", [KC, 128, CC * 128])
    wmo = din("wmo", [8, 4, 128, 8 * 512])
    wup_d = din("wup", [RANK, DK])
    gng_d = din("gng", [HV])
    NPC = 3 * CC + CC + 8 + 2 * KC
    pcol_d = din("pcol", [128, NPC])
    cmat_d = din("cmat", [128, 128 * 3 + 512])
    role_d = din("role", [128, 2])
    par_d = din("par", [1, 1], I32)
    nonce_d = din("nonce", [1, 1], I32)
    NR = max(n_tiles, 1)
    sh_S = nc.dram_tensor("sh_S", [2, NR, 128, HEADS * 2 * HV], F32, addr_space="Shared").ap()
    sh_C = nc.dram_tensor("sh_C", [2, NR, 128, HEADS * 2 * HV], F32, addr_space="Shared").ap()
    sh_D = nc.dram_tensor("sh_D", [2, NR, 128, 8], F32, addr_space="Shared").ap()
    sh_H = nc.dram_tensor("sh_H", [2, NR, 128, 2 * CC], F32, addr_space="Shared").ap()
    sh_F = nc.dram_tensor("sh_F", [2, 8], I32, addr_space="Shared").ap()
    dbg_t = {}
    if dbg:
        for name, (shape, dt_) in dbg.items():
            dbg_t[name] = nc.dram_tensor("dbg_" + name, list(shape), dt_, kind="ExternalOutput").ap()

    with ExitStack() as st:
        P = Prog(nc)
        sb = lambda n, shp, dt: st.enter_context(nc.sbuf_tensor("sb_" + n, shp, dt))
        ring = [sb("ring%d" % i, [128, 4096], BF16) for i in range(NSLOT)]
        hT = sb("hT", [128, KC, T], BF16)
        xr = sb("xr", [128, 32768], BF16)
        bq = sb("bq", [128, 16384], BF16)
        st_f = sb("st_f", [128, HEADS, 2, HV], F32)
        cmat = sb("cmat", [128, 128 * 3 + 512], F32)
        ident = sb("ident", [128, 128], BF16)
        pcol = sb("pcol", [128, NPC], F32)
        negb = sb("negb", [128, 8], F32)
        gng = sb("gng", [128, HV], F32)
        wup = sb("wup", [RANK, DK], F32)
        alT = sb("alT", [RANK, T], F32)
        halo = sb("halo", [128, CC, 2], F32)
        hin = sb("hin", [128, CC, 2], F32)
        hs1 = sb("hs1", [128, CC, 2], F32)
        acc0 = sb("acc0", [128, CC, 2], F32)
        cb0 = sb("cb0", [128, CC, 2], F32)
        fx = sb("fx", [128, 4, CC], F32)
        role = sb("role", [128, 2], F32)
        dloc = sb("dloc", [128, 8], F32)
        coef = sb("coef", [128, 8], F32)
        stat = sb("stat", [128, 16], F32)
        tmpf = [sb("tmpf%d" % i, [128, T], F32) for i in range(2)]
        ps = st.enter_context(nc.psum_tensor("ps", [128, 8, 512], F32))

        x_sb = xr[:, :].bitcast(F32).rearrange("p (s d) -> p s d", s=NS)
        gb = bq[:, 0:8192].bitcast(F32)
        hp = bq[:, 8192:12288]
        hp2 = [bq[:, 8192:12288], bq[:, 12288:16384]]
        actT = bq[:, 0:FGRP * T].rearrange("p (f t) -> p f t", f=FGRP)
        mergedT = bq[:, :].rearrange("p (c t) -> p c t", c=KC)
        maskL = cmat[:, 128:256]
        maskU = cmat[:, 256:384]
        rmask = cmat[:, 384:896]

        def xr_bf(lo, hi):
            return xr[:, lo // 2:hi // 2]

        def xr_f32(lo, hi):
            return xr[:, lo // 2:hi // 2].bitcast(F32)

        def XA(lo, hi):
            return [("XR", p) for p in range(lo // 2048, (hi + 2047) // 2048)]

        def BA(lo, hi):
            return [("BQ", p) for p in range(lo // 2048, (hi + 2047) // 2048)]

        K = 1024
        OGT = (0, 16 * K)
        PRD = (16 * K, 32 * K)
        Q1, Q2, K1, K2 = [(32 * K + i * 2 * K, 34 * K + i * 2 * K) for i in range(4)]
        K1T = (40 * K, 42 * K)
        VH = (42 * K, 46 * K)
        SR = (46 * K, 50 * K)
        OG = (50 * K, 54 * K)
        TA = (54 * K, 56 * K)
        TB = (56 * K, 58 * K)
        CUM = (58 * K, 60 * K)
        SC1 = (60 * K, 60 * K + 512)
        SC2 = (60 * K + 512, 61 * K)
        SCT = (61 * K, 61 * K + 256)
        T1 = (62 * K, 64 * K)
        A_STB = BA(0, 8 * K)
        A_EL = BA(8 * K, 12 * K)
        A_ENL = BA(12 * K, 16 * K)
        ogT = xr_bf(*OGT).rearrange("p (c t) -> p c t", c=16)
        stb = bq[:, 0:2048].rearrange("p (h c v) -> p h c v", h=2, c=2)
        prodT = xr_bf(*PRD).rearrange("p (c t) -> p c t", c=CC)

        state = {"slot": 0, "bank": 0}

        def next_slot():
            i = state["slot"] % NSLOT
            state["slot"] += 1
            return i

        def next_bank():
            i = state["bank"] % 8
            state["bank"] += 1
            return i

        def wload(src_ap, nelem, view=None):
            i = next_slot()
            dst = ring[i][:, 0:nelem]
            if view is not None:
                dst = view(dst)
            P.dma("pool", lambda e, dst=dst, src=src_ap: e.dma_start(out=dst, in_=src), w=[("ring", i)])
            return i

        def mm(bank, lhsT, rhs, start, stop, r, out=None):
            o = ps[:, bank, :] if out is None else out
            P.op("pe", lambda e, o=o, lhsT=lhsT, rhs=rhs, start=start, stop=stop:
                 e.matmul(o, lhsT, rhs, start=start, stop=stop), r=r, w=[("ps", bank)])

        def dbg_dump(name, src_ap, atoms):
            if name in dbg_t:
                P.dma("sp", lambda e, d=dbg_t[name], s=src_ap: e.dma_start(out=d, in_=s), r=atoms,
                      w=[("dbg", name)], final=True)

        P.dma("sp", lambda e: e.dma_start(out=cmat[:], in_=cmat_d), w=["cmat"])
        P.dma("sp", lambda e: e.dma_start(out=pcol[:], in_=pcol_d), w=["pcol"])
        P.dma("sp", lambda e: e.dma_start(out=wup[:], in_=wup_d), w=["wup"])
        P.dma("sp", lambda e: e.dma_start(out=gng[:], in_=gng_d.partition_broadcast(128)), w=["gng"])
        P.op("dve", lambda e: e.tensor_copy(ident[:], cmat[:, 0:128]), r=["cmat"], w=["ident"])
        CW0 = 0
        CB0 = 3 * CC
        BA0 = 4 * CC
        BM0 = 4 * CC + 8
        P.op("dve", lambda e: e.tensor_scalar(negb[:], pcol[:, BA0:BA0 + 8], -1.0, None, ALU.mult),
             r=["pcol"], w=["negb"])

        P.op("dve", lambda e: e.memset(stat[:, 10:11], 1.0), w=[("stat", 10)])
        P.op("dve", lambda e: e.memset(stat[:, 11:12], EPS), w=[("stat", 11)])

        def rstd_ops(ss, rs, n, a_ss, a_rs):
            P.op("act", lambda e: e.activation(rs, ss, AF.Sqrt, bias=stat[:, 11:12], scale=1.0 / n),
                 r=[a_ss, ("stat", 11)], w=[a_rs])
            P.op("dve", lambda e: e.reciprocal(rs, rs), r=[a_rs], w=[a_rs])

        XS = [[("XR", s * 8 + c) for c in range(8)] for s in range(NS)]

        junk = hT[:, 24:32, :].rearrange("p a b -> p (a b)")
        A_junk = [("hT", kc, s) for kc in range(24, 32) for s in range(NS)]

        def norm_stats():
            for s in range(NS):
                ss = stat[:, s:s + 1]
                rs = stat[:, 4 + s:5 + s]
                P.op("dve", lambda e, ss=ss: e.memset(ss, 0.0), w=[("stat", s)])
                P.op("act", lambda e, s=s, ss=ss: e.activation(junk, x_sb[:, s, :], AF.Square, accum_out=ss),
                     r=XS[s], w=A_junk + [("stat", s)])
                P.op("act", lambda e, ss=ss, rs=rs: e.activation(rs, ss, AF.Sqrt, bias=stat[:, 11:12], scale=1.0 / D),
                     r=[("stat", s), ("stat", 11)], w=[("stat", 4 + s)])

        def norm_recip(s):
            rs = stat[:, 4 + s:5 + s]
            P.op("dve", lambda e, rs=rs: e.reciprocal(rs, rs), r=[("stat", 4 + s)], w=[("stat", 4 + s)])

        def norm_to_hT(gkey):
            P.dma("sp", lambda e: e.dma_start(out=gb, in_=g_norm[gkey].partition_broadcast(128)),
                  w=BA(0, 16 * K), key=("gbload",))
            norm_stats()

            def scale(s):
                rs = stat[:, 4 + s:5 + s]
                hp = hp2[s % 2]
                A_hp = BA(16 * K + (s % 2) * 8 * K, 24 * K + (s % 2) * 8 * K)
                norm_recip(s)
                P.op("dve", lambda e, s=s, rs=rs, hp=hp: e.scalar_tensor_tensor(hp, x_sb[:, s, :], rs, gb, ALU.mult, ALU.mult),
                     r=XS[s] + BA(0, 16 * K) + [("stat", 4 + s)], w=A_hp)

            def transp(s):
                hp = hp2[s % 2]
                A_hp = BA(16 * K + (s % 2) * 8 * K, 24 * K + (s % 2) * 8 * K)
                for k4 in range(KC // 4):
                    b = next_bank()
                    pb = ps[:, b, :].bitcast(BF16)
                    for j in range(4):
                        kc = k4 * 4 + j
                        P.op("pe", lambda e, pb=pb, j=j, kc=kc, hp=hp: e.transpose(pb[:, j * 128:(j + 1) * 128],
                                                                            hp[:, kc * 128:(kc + 1) * 128], ident[:]),
                             r=A_hp + ["ident"], w=[("ps", b)])
                    dst = hT[:, k4 * 4:k4 * 4 + 4, s * 128:(s + 1) * 128]
                    src = pb[:, 0:512].rearrange("p (a b) -> p a b", a=4)
                    wat = [("hT", kc, s) for kc in range(k4 * 4, k4 * 4 + 4)]
                    if k4 % 2 == 0:
                        P.op("act", lambda e, dst=dst, src=src: e.copy(dst, src), r=[("ps", b)], w=wat)
                    else:
                        P.op("dve", lambda e, dst=dst, src=src: e.tensor_copy(dst, src), r=[("ps", b)], w=wat)

            scale(0)
            scale(1)
            transp(0)
            scale(2)
            transp(1)
            scale(3)
            transp(2)
            transp(3)

        def hT_atoms(kc):
            return [("hT", kc, s) for s in range(NS)]

        def ffn(key):
            wg_d, wu_d, wd_d = ffn_w[key]
            norm_to_hT(key)
            groups = [(f0, min(FGRP, FC - f0)) for f0 in range(0, FC, FGRP)]
            for (f0, nf) in groups:
                for j in range(nf):
                    fc = f0 + j
                    sg_i = wload(wg_d[fc], 4096)
                    su_i = wload(wu_d[fc], 4096)
                    bg, bu = next_bank(), next_bank()
                    for kc in range(KC):
                        mm(bg, ring[sg_i][:, kc * 128:(kc + 1) * 128], hT[:, kc, :], kc == 0, kc == KC - 1,
                           r=[("ring", sg_i)] + hT_atoms(kc))
                    for kc in range(KC):
                        mm(bu, ring[su_i][:, kc * 128:(kc + 1) * 128], hT[:, kc, :], kc == 0, kc == KC - 1,
                           r=[("ring", su_i)] + hT_atoms(kc))
                    tf = tmpf[fc % 2]
                    P.op("act", lambda e, tf=tf, bg=bg: e.activation(tf[:], ps[:, bg, :], AF.Silu),
                         r=[("ps", bg)], w=[("tmpf", fc % 2)])
                    P.op("dve", lambda e, tf=tf, bu=bu, j=j: e.tensor_tensor(actT[:, j, :], tf[:], ps[:, bu, :], ALU.mult),
                         r=[("tmpf", fc % 2), ("ps", bu)], w=[("BQ", j // 2)])
                for cg in range(8):
                    banks = [next_bank() for _ in range(NS)]
                    for j0 in range(0, nf, 8):
                        n = min(8, nf - j0)
                        si = wload(wd_d[cg, f0 + j0:f0 + j0 + n].rearrange("f p c -> p f c"), n * 512,
                                   view=lambda a, n=n: a.rearrange("p (f c) -> p f c", f=n))
                        for jj in range(n):
                            j = j0 + jj
                            for s in range(NS):
                                mm(banks[s], actT[:, j, s * 128:(s + 1) * 128], ring[si][:, jj * 512:(jj + 1) * 512],
                                   j == 0, j == nf - 1, r=[("ring", si), ("BQ", j // 2)])
                    for s in range(NS):
                        xs = x_sb[:, s, cg * 512:(cg + 1) * 512]
                        P.op("dve", lambda e, xs=xs, b=banks[s]: e.scalar_tensor_tensor(xs, ps[:, b, :], 0.5, xs,
                                                                                         ALU.mult, ALU.add),
                             r=[("ps", banks[s]), ("XR", s * 8 + cg)], w=[("XR", s * 8 + cg)])

        def fchunk(col_chunk, bank):
            si = wload(win_f[col_chunk], 4096)
            for kc in range(KC):
                mm(bank, ring[si][:, kc * 128:(kc + 1) * 128], hT[:, kc, :], kc == 0, kc == KC - 1,
                   r=[("ring", si)] + hT_atoms(kc))

        def tgroup(grp):
            banks = [next_bank() for _ in range(NS)]
            for kq in range(4):
                si = wload(win_t[grp, kq], 4096)
                for k8 in range(8):
                    kc = kq * 8 + k8
                    for s in range(NS):
                        mm(banks[s], hT[:, kc, s * 128:(s + 1) * 128], ring[si][:, k8 * 512:(k8 + 1) * 512],
                           kc == 0, kc == KC - 1, r=[("ring", si), ("hT", kc, s)])
            return banks

        regs = {}

        def sp_init(e, finish):
            regs["par"] = e.alloc_register("r_par")
            regs["nonce"] = e.alloc_register("r_nonce")
            regs["fv"] = e.alloc_register("r_fv")
            regs["pr"] = e.alloc_register("r_pr")
            e.reg_load(regs["par"], par_d[0:1, 0:1])
            e.reg_load(regs["nonce"], nonce_d[0:1, 0:1])

        P.op("sp", sp_init, custom=True)
        P.dma("sp", lambda e: e.dma_start(out=role[:], in_=role_d), w=["role"])
        isA = role[:, 0:1]
        isB = role[:, 1:2]

        def own_slot_dma(dst_of_slot, src, r, w, key):
            def emit(e, finish):
                with e.If_eq(regs["par"], 0):
                    finish(e.dma_start(out=dst_of_slot(0), in_=src))
                with e.Else():
                    finish(e.dma_start(out=dst_of_slot(1), in_=src))
            P.dma("sp", emit, r=r, w=w, key=key, custom=True)

        def mixer(ti):
            first = (ti % tiles_per_seq) == 0
            norm_to_hT("mix")
            for s in range(NS):
                P.dma("sp", lambda e, s=s: e.dma_start(out=xs_d[s], in_=x_sb[:, s, :]), r=XS[s], w=[("xsd", s)])
            si = wload(win_a, KC * RANK)
            ba = next_bank()
            for kc in range(KC):
                mm(ba, ring[si][:, kc * RANK:(kc + 1) * RANK], hT[:, kc, :], kc == 0, kc == KC - 1,
                   r=[("ring", si)] + hT_atoms(kc), out=ps[0:RANK, ba, :])
            P.op("act", lambda e: e.copy(alT[:], ps[0:RANK, ba, :]), r=[("ps", ba)], w=["alT"])

            TCC = (40 * K, 42 * K)
            CCU = (42 * K, 44 * K + 16)
            ACC = (46 * K, 48 * K)
            tcc = xr_f32(*TCC)
            ccu = xr_f32(42 * K, 44 * K + 8)
            acc = xr_f32(*ACC)
            for c in range(CC):
                b_cc, b_cu, b_cb = next_bank(), next_bank(), next_bank()
                fchunk(O_CC // 128 + c, b_cc)
                fchunk(O_CU // 128 + c, b_cu)
                fchunk(O_CB // 128 + c, b_cb)
                P.op("act", lambda e, b=b_cc: e.copy(tcc, ps[:, b, :]), r=[("ps", b_cc)], w=XA(*TCC))
                P.op("dve", lambda e: e.memset(ccu[:, 0:2], 0.0), w=XA(*CCU))
                P.op("dve", lambda e, b=b_cu: e.tensor_tensor(ccu[:, 2:514], tcc, ps[:, b, :], ALU.mult),
                     r=[("ps", b_cu)] + XA(*TCC), w=XA(*CCU))
                P.op("dve", lambda e, c=c: e.tensor_copy(halo[:, c, :], ccu[:, 512:514]), r=XA(*CCU), w=["halo"])
                w0 = pcol[:, CW0 + c * 3 + 0:CW0 + c * 3 + 1]
                w1 = pcol[:, CW0 + c * 3 + 1:CW0 + c * 3 + 2]
                w2 = pcol[:, CW0 + c * 3 + 2:CW0 + c * 3 + 3]
                cb_ = pcol[:, CB0 + c:CB0 + c + 1]
                P.op("dve", lambda e, w2=w2, cb_=cb_: e.tensor_scalar(acc, ccu[:, 2:514], w2, cb_, ALU.mult, ALU.add),
                     r=XA(*CCU) + ["pcol"], w=XA(*ACC))
                P.op("dve", lambda e, w1=w1: e.scalar_tensor_tensor(acc, ccu[:, 1:513], w1, acc, ALU.mult, ALU.add),
                     r=XA(*CCU) + XA(*ACC) + ["pcol"], w=XA(*ACC))
                P.op("dve", lambda e, w0=w0: e.scalar_tensor_tensor(acc, ccu[:, 0:512], w0, acc, ALU.mult, ALU.add),
                     r=XA(*CCU) + XA(*ACC) + ["pcol"], w=XA(*ACC))
                P.op("dve", lambda e, c=c: e.tensor_copy(acc0[:, c, :], acc[:, 0:2]), r=XA(*ACC), w=["acc0"])
                P.op("dve", lambda e, c=c, b=b_cb: e.tensor_copy(cb0[:, c, :], ps[:, b, 0:2]), r=[("ps", b_cb)], w=["cb0"])
                P.op("dve", lambda e, c=c, b=b_cb: e.tensor_tensor(prodT[:, c, :], acc, ps[:, b, :], ALU.mult),
                     r=XA(*ACC) + [("ps", b_cb)], w=XA(16 * K + c * K, 17 * K + c * K))

            q1 = xr_bf(*Q1).rearrange("p (c t) -> p c t", c=2)
            q2 = xr_bf(*Q2).rearrange("p (c t) -> p c t", c=2)
            k1 = xr_bf(*K1).rearrange("p (c t) -> p c t", c=2)
            k2 = xr_bf(*K2).rearrange("p (c t) -> p c t", c=2)
            k1t = xr_bf(*K1T).rearrange("p (b d) -> p b d", b=4)
            vh_all = bq[:, 8192:16384].rearrange("p (h s v) -> p h s v", h=HEADS, s=NS)
            sr = xr_bf(*SR).rearrange("p (s v) -> p s v", s=NS)
            og = xr_bf(*OG).rearrange("p (s v) -> p s v", s=NS)
            eL = bq[:, 4096:6144].bitcast(F32).rearrange("p (c t) -> p c t", c=2)
            eNL = bq[:, 6144:8192].bitcast(F32).rearrange("p (c t) -> p c t", c=2)
            tA = xr_f32(*TA)
            tB = xr_f32(*TB)
            cum = xr_f32(*CUM)
            sc1 = xr_f32(*SC1)
            sc2 = xr_f32(*SC2)
            scT = xr_bf(*SCT)
            t1 = xr_f32(*T1)

            def head_prep(h, full):
                for dc in range(2):
                    c = 2 * h + dc
                    bz = next_bank()
                    P.op("pe", lambda e, bz=bz, c=c: e.matmul(ps[:, bz, :], wup[:, c * 128:(c + 1) * 128], alT[:],
                                                               start=True, stop=True),
                         r=["wup", "alT"], w=[("ps", bz)])
                    P.op("act", lambda e, bz=bz, c=c: e.activation(tA, ps[:, bz, :], AF.Exp, bias=negb[:, c:c + 1], scale=-1.0),
                         r=[("ps", bz), "negb"], w=XA(*TA))
                    P.op("act", lambda e: e.activation(tB, tA, AF.Ln, bias=stat[:, 10:11], scale=1.0),
                         r=XA(*TA) + [("stat", 10)], w=XA(*TB))
                    P.op("dve", lambda e: e.tensor_tensor_scan(cum, rmask, tB, 0.0, ALU.mult, ALU.add),
                         r=XA(*TB) + ["cmat"], w=XA(*CUM))
                    P.op("act", lambda e, dc=dc: e.activation(eL[:, dc, :], cum, AF.Exp, scale=-1.0 / 16.0),
                         r=XA(*CUM), w=A_EL)
                    P.op("act", lambda e, dc=dc: e.activation(eNL[:, dc, :], cum, AF.Exp, scale=1.0 / 16.0),
                         r=XA(*CUM), w=A_ENL)
                    if full:
                        bqk = next_bank()
                        fchunk(O_Q // 128 + c, bqk)
                        P.op("dve", lambda e, b=bqk, dc=dc: e.scalar_tensor_tensor(q1[:, dc, :], ps[:, b, :], HK ** -0.5,
                                                                                     eL[:, dc, :], ALU.mult, ALU.mult),
                             r=[("ps", bqk)] + A_EL, w=XA(*Q1))
                        P.op("dve", lambda e, b=bqk, dc=dc: e.scalar_tensor_tensor(q2[:, dc, :], ps[:, b, :], HK ** -0.5,
                                                                                     eNL[:, dc, :], ALU.mult, ALU.mult),
                             r=[("ps", bqk)] + A_ENL, w=XA(*Q2))
                    bkk = next_bank()
                    fchunk(O_K // 128 + c, bkk)
                    P.op("dve", lambda e, b=bkk, dc=dc: e.tensor_tensor(k1[:, dc, :], ps[:, b, :], eNL[:, dc, :], ALU.mult),
                         r=[("ps", bkk)] + A_ENL, w=XA(*K1))
                    if full:
                        P.op("dve", lambda e, b=bkk, dc=dc: e.tensor_tensor(k2[:, dc, :], ps[:, b, :], eL[:, dc, :], ALU.mult),
                             r=[("ps", bkk)] + A_EL, w=XA(*K2))
                    else:
                        dl = dloc[:, c:c + 1]
                        P.op("dve", lambda e, dl=dl, dc=dc: e.tensor_copy(dl, eL[:, dc, 127:128]), r=A_EL, w=["dloc"])
                        for blk in range(1, 4):
                            P.op("dve", lambda e, dl=dl, dc=dc, blk=blk: e.tensor_tensor(
                                dl, dl, eL[:, dc, blk * 128 + 127:blk * 128 + 128], ALU.mult), r=A_EL + ["dloc"], w=["dloc"])
                for blk in range(4):
                    b = next_bank()
                    pb = ps[:, b, :].bitcast(BF16)
                    for dc in range(2):
                        P.op("pe", lambda e, pb=pb, dc=dc, blk=blk: e.transpose(pb[:, dc * 128:(dc + 1) * 128],
                                                                                 k1[:, dc, blk * 128:(blk + 1) * 128], ident[:]),
                             r=XA(*K1) + ["ident"], w=[("ps", b)])
                    P.op("act", lambda e, pb=pb, blk=blk: e.copy(k1t[:, blk, :], pb[:, 0:256]), r=[("ps", b)], w=XA(*K1T))
                vh = vh_all[:, h]
                A_VH = BA(16 * K + h * 4 * K, 16 * K + (h + 1) * 4 * K)
                if (not full) or (not exchange):
                    banks = tgroup(h)
                    for s in range(NS):
                        P.op("act", lambda e, s=s, b=banks[s]: e.copy(vh[:, s, :], ps[:, b, :]), r=[("ps", banks[s])], w=A_VH)
                if full:
                    banks = tgroup(4 + h)
                    for s in range(NS):
                        P.op("act", lambda e, s=s, b=banks[s]: e.activation(sr[:, s, :], ps[:, b, :], AF.Silu),
                             r=[("ps", banks[s])], w=XA(*SR))

            def head_blocks(h, full):
                vh = vh_all[:, h]
                A_VH = BA(16 * K + h * 4 * K, 16 * K + (h + 1) * 4 * K)
                if full:
                    P.op("act", lambda e, h=h: e.copy(stb[:, 0], st_f[:, h]), r=["st_f"], w=[("BQ", 0)])
                for blk in range(4):
                    tk = slice(blk * 128, (blk + 1) * 128)
                    bs = [next_bank(), next_bank()]
                    for dc in range(2):
                        mm(bs[dc], k1t[:, blk, dc * 128:(dc + 1) * 128], vh[:, blk, :], True, True, r=XA(*K1T) + A_VH)
                    if full:
                        b1 = next_bank()
                        for dc in range(2):
                            mm(b1, k1[:, dc, tk], q1[:, dc, tk], dc == 0, dc == 1, r=XA(*K1) + XA(*Q1), out=ps[:, b1, 0:128])
                        b2 = next_bank()
                        for dc in range(2):
                            mm(b2, k2[:, dc, tk], q2[:, dc, tk], dc == 0, dc == 1, r=XA(*K2) + XA(*Q2), out=ps[:, b2, 0:128])
                        P.op("dve", lambda e, b1=b1: e.tensor_tensor(sc1, ps[:, b1, 0:128], maskL, ALU.mult),
                             r=[("ps", b1), "cmat"], w=XA(*SC1))
                        P.op("dve", lambda e, b2=b2: e.tensor_tensor(sc2, ps[:, b2, 0:128], maskU, ALU.mult),
                             r=[("ps", b2), "cmat"], w=XA(*SC2))
                        P.op("dve", lambda e: e.tensor_tensor(scT, sc1, sc2, ALU.add), r=XA(*SC1) + XA(*SC2), w=XA(*SCT))
                    for dc in range(2):
                        sl = eL[:, dc, blk * 128 + 127:blk * 128 + 128]
                        sf = st_f[:, h, dc, :]
                        P.op("dve", lambda e, b=bs[dc], sl=sl: e.tensor_scalar(tA, ps[:, b, :], sl, None, ALU.mult),
                             r=[("ps", bs[dc])] + A_EL, w=XA(*TA))
                        P.op("dve", lambda e, sf=sf, sl=sl: e.scalar_tensor_tensor(sf, sf, sl, tA, ALU.mult, ALU.add),
                             r=XA(*TA) + A_EL + ["st_f"], w=["st_f"])
                        if full and blk < 3:
                            P.op("act", lambda e, sf=sf, dc=dc, blk=blk: e.copy(stb[:, (blk + 1) % 2, dc, :], sf),
                                 r=["st_f"], w=[("BQ", (blk + 1) % 2)])
                    if full:
                        bo = next_bank()
                        mm(bo, scT, vh[:, blk, :], True, False, r=XA(*SCT) + A_VH)
                        for dc in range(2):
                            mm(bo, q1[:, dc, tk], stb[:, blk % 2, dc, :], False, dc == 1, r=XA(*Q1) + [("BQ", blk % 2)])
                        ss = stat[:, 8:9]
                        rs = stat[:, 9:10]
                        P.op("dve", lambda e, ss=ss: e.memset(ss, 0.0), w=[("stat", 8)])
                        P.op("act", lambda e, bo=bo, ss=ss: e.activation(t1, ps[:, bo, :], AF.Square, accum_out=ss),
                             r=[("ps", bo)], w=XA(*T1) + [("stat", 8)])
                        rstd_ops(ss, rs, HV, ("stat", 8), ("stat", 9))
                        P.op("dve", lambda e, bo=bo, rs=rs: e.scalar_tensor_tensor(t1, ps[:, bo, :], rs, gng[:], ALU.mult, ALU.mult),
                             r=[("ps", bo), ("stat", 9), "gng"], w=XA(*T1))
                        P.op("dve", lambda e, blk=blk: e.tensor_tensor(og[:, blk, :], t1, sr[:, blk, :], ALU.mult),
                             r=XA(*T1) + XA(*SR), w=XA(*OG))
                if full:
                    for vc in range(4):
                        b = next_bank()
                        pb = ps[:, b, :].bitcast(BF16)
                        for blk in range(4):
                            P.op("pe", lambda e, pb=pb, blk=blk, vc=vc: e.transpose(pb[:, blk * 128:(blk + 1) * 128],
                                                                                     og[:, blk, vc * 128:(vc + 1) * 128], ident[:]),
                                 r=XA(*OG) + ["ident"], w=[("ps", b)])
                        P.op("act", lambda e, pb=pb, h=h, vc=vc: e.copy(ogT[:, h * 4 + vc, :], pb[:, 0:512]),
                             r=[("ps", b)], w=XA(*OGT))

            st_flat = st_f[:].rearrange("p h c v -> p (h c v)")
            halo_flat = halo[:].rearrange("p c t -> p (c t)")
            if exchange:
                P.op("dve", lambda e: e.memset(st_f[:], 0.0), w=["st_f"])
                for h in range(HEADS):
                    head_prep(h, False)
                    head_blocks(h, False)
                own_slot_dma(lambda sl: sh_S[sl, ti], st_flat, r=["st_f"], w=[("shS", ti)], key=("shS",))
                own_slot_dma(lambda sl: sh_D[sl, ti], dloc[:], r=["dloc"], w=[("shD", ti)], key=("shD",))
                own_slot_dma(lambda sl: sh_H[sl, ti], halo_flat, r=["halo"], w=[("shH", ti)], key=("shH",))

                def handshake(e, finish, ti=ti, first=first):
                    fv, pr = regs["fv"], regs["pr"]
                    e.reg_add(fv, regs["nonce"], ti + 1)
                    with e.If_eq(regs["par"], 0):
                        e.reg_save(sh_F[0:1, ti:ti + 1], fv)
                    with e.Else():
                        e.reg_save(sh_F[1:2, ti:ti + 1], fv)
                    polls = [sh_F[0:1, ti:ti + 1]]
                    if not first:
                        polls.append(sh_F[1:2, 4 + ti - 1:4 + ti])
                    for k, fl in enumerate(polls):
                        if k == 1:
                            e.reg_add(fv, regs["nonce"], ti)
                        e.reg_mov(pr, 1)
                        with e.While(pr):
                            e.reg_load(pr, fl)
                            e.reg_sub(pr, pr, fv)

                deps = [("shS", ti), ("shD", ti), ("shH", ti)] + ([("shC", ti - 1)] if not first else [])
                P.op("sp", handshake, r=deps, w=[("hs", ti)], custom=True)
                head_prep(0, True)
                P.dma("sp", lambda e: e.dma_start(out=dloc[:], in_=sh_D[0, ti]), r=[("hs", ti), ("shD", ti)], w=["dloc"],
                      key=("ldD",))
                P.op("dve", lambda e: e.tensor_scalar(coef[:], dloc[:], isB, isA, ALU.mult, ALU.add),
                     r=["dloc", "role"], w=["coef"])
                P.dma("sp", lambda e: e.dma_start(out=hin[:].rearrange("p c t -> p (c t)"), in_=sh_H[0, ti]),
                      r=[("hs", ti), ("shH", ti)], w=["hin"], key=("ldH",))
                P.op("dve", lambda e: e.tensor_scalar(hin[:], hin[:], isB, None, ALU.mult), r=["hin", "role"], w=["hin"])
                if not first:
                    P.dma("sp", lambda e: e.dma_start(out=hs1[:].rearrange("p c t -> p (c t)"), in_=sh_H[1, ti - 1]),
                          r=[("hs", ti)], w=["hs1"], key=("ldH1",))
                    P.op("dve", lambda e: e.scalar_tensor_tensor(hin[:], hs1[:], isA, hin[:], ALU.mult, ALU.add),
                         r=["hin", "hs1", "role"], w=["hin"])
                for b8 in range(8):
                    tS = tmpf[b8 % 2]
                    tC = xr_f32(50 * K + (b8 % 2) * 2 * K, 52 * K + (b8 % 2) * 2 * K)
                    aC = XA(50 * K + (b8 % 2) * 2 * K, 52 * K + (b8 % 2) * 2 * K)
                    cs = slice(b8 * HV, (b8 + 1) * HV)
                    P.dma("sp", lambda e, tS=tS, cs=cs: e.dma_start(out=tS[:], in_=sh_S[0, ti][:, cs]),
                          r=[("hs", ti), ("shS", ti)], w=[("tmpf", b8 % 2)], key=("ldS", b8 % 2))
                    if first:
                        P.op("dve", lambda e, tS=tS, cs=cs: e.tensor_scalar(st_flat[:, cs], tS[:], isB, None, ALU.mult),
                             r=[("tmpf", b8 % 2), "role"], w=["st_f"])
                    else:
                        P.op("dve", lambda e, tS=tS: e.tensor_scalar(tS[:], tS[:], isB, None, ALU.mult),
                             r=[("tmpf", b8 % 2), "role"], w=[("tmpf", b8 % 2)])
                        P.dma("sp", lambda e, tC=tC, cs=cs: e.dma_start(out=tC, in_=sh_C[1, ti - 1][:, cs]),
                              r=[("hs", ti)], w=aC, key=("ldC", b8 % 2))
                        P.op("dve", lambda e, tS=tS, tC=tC, cs=cs, b8=b8: e.scalar_tensor_tensor(
                            st_flat[:, cs], tC, coef[:, b8:b8 + 1], tS[:], ALU.mult, ALU.add),
                            r=aC + [("tmpf", b8 % 2), "coef"], w=["st_f"])
                pw = pcol[:, CW0:CW0 + 3 * CC].rearrange("p (c t) -> p c t", t=3)
                P.op("dve", lambda e: e.tensor_tensor(fx[:, 0, :], pw[:, :, 1], hin[:, :, 1], ALU.mult), r=["pcol", "hin"], w=["fx"])
                P.op("dve", lambda e: e.tensor_tensor(fx[:, 1, :], pw[:, :, 0], hin[:, :, 0], ALU.mult), r=["pcol", "hin"], w=["fx"])
                P.op("dve", lambda e: e.tensor_tensor(fx[:, 0, :], fx[:, 0, :], fx[:, 1, :], ALU.add), r=["fx"], w=["fx"])
                P.op("dve", lambda e: e.tensor_tensor(fx[:, 1, :], pw[:, :, 0], hin[:, :, 1], ALU.mult), r=["pcol", "hin", "fx"], w=["fx"])
                for t in range(2):
                    P.op("dve", lambda e, t=t: e.tensor_tensor(fx[:, 2 + t, :], fx[:, t, :], acc0[:, :, t], ALU.add),
                         r=["fx", "acc0"], w=["fx"])
                    P.op("dve", lambda e, t=t: e.tensor_tensor(prodT[:, :, t], fx[:, 2 + t, :], cb0[:, :, t], ALU.mult),
                         r=["fx", "cb0"], w=XA(*PRD))
            else:
                if first:
                    P.op("dve", lambda e: e.memset(st_f[:], 0.0), w=["st_f"])
                head_prep(0, True)
            dbg_dump("prodT", xr_bf(*PRD), XA(*PRD))
            dbg_dump("stin", st_flat, ["st_f"])
            head_blocks(0, True)
            for h in range(1, HEADS):
                head_prep(h, True)
                head_blocks(h, True)
            if exchange and ti + 1 < n_tiles:
                own_slot_dma(lambda sl: sh_C[sl, ti], st_flat, r=["st_f"], w=[("shC", ti)], key=("shC",))

                def carry_flag(e, finish, ti=ti):
                    fv = regs["fv"]
                    e.reg_add(fv, regs["nonce"], ti + 1)
                    with e.If_eq(regs["par"], 0):
                        e.reg_save(sh_F[0:1, 4 + ti:5 + ti], fv)
                    with e.Else():
                        e.reg_save(sh_F[1:2, 4 + ti:5 + ti], fv)

                P.op("sp", carry_flag, r=[("shC", ti)], w=[("cf", ti)], custom=True)
            dbg_dump("ogT", xr_bf(*OGT), XA(*OGT))
            for dch in range(KC):
                o = 40 * K + (dch % 2) * 8 * K
                sa = xr_f32(o, o + 2 * K)
                sb_ = xr_f32(o + 2 * K, o + 4 * K)
                m1 = xr_f32(o + 4 * K, o + 6 * K)
                m2 = xr_f32(o + 6 * K, o + 8 * K)
                A_sa, A_sb, A_m1, A_m2 = [XA(o + i * 2 * K, o + (i + 1) * 2 * K) for i in range(4)]
                bga, bgb, bya, byb = next_bank(), next_bank(), next_bank(), next_bank()
                fchunk(64 + dch, bga)
                fchunk(96 + dch, bgb)
                si = wload(wco[dch], CC * 128)
                for cc in range(CC):
                    mm(bya, ring[si][:, cc * 128:(cc + 1) * 128], prodT[:, cc, :], cc == 0, cc == CC - 1,
                       r=[("ring", si)] + XA(16 * K + cc * K, 17 * K + cc * K))
                si = wload(wgo[dch], CC * 128)
                for vc in range(16):
                    mm(byb, ring[si][:, vc * 128:(vc + 1) * 128], ogT[:, vc, :], vc == 0, vc == 15,
                       r=[("ring", si)] + XA(*OGT))
                bm0 = pcol[:, BM0 + dch:BM0 + dch + 1]
                bm1 = pcol[:, BM0 + KC + dch:BM0 + KC + dch + 1]
                P.op("act", lambda e, sa=sa, b=bga, bm0=bm0: e.activation(sa, ps[:, b, :], AF.Sigmoid, bias=bm0),
                     r=[("ps", bga), "pcol"], w=A_sa)
                P.op("act", lambda e, sb_=sb_, b=bgb, bm1=bm1: e.activation(sb_, ps[:, b, :], AF.Sigmoid, bias=bm1),
                     r=[("ps", bgb), "pcol"], w=A_sb)
                P.op("dve", lambda e, m1=m1, sa=sa, b=bya: e.tensor_tensor(m1, sa, ps[:, b, :], ALU.mult),
                     r=A_sa + [("ps", bya)], w=A_m1)
                P.op("dve", lambda e, m2=m2, sb_=sb_, b=byb: e.tensor_tensor(m2, sb_, ps[:, b, :], ALU.mult),
                     r=A_sb + [("ps", byb)], w=A_m2)
                P.op("dve", lambda e, m1=m1, m2=m2, dch=dch: e.tensor_tensor(mergedT[:, dch, :], m1, m2, ALU.add),
                     r=A_m1 + A_m2, w=[("BQ", dch // 2)])
            dbg_dump("mergedT", bq[:, :], BA(0, 32 * K))
            for s in range(NS):
                P.dma("sp", lambda e, s=s: e.dma_start(out=x_sb[:, s, :], in_=xs_d[s]), r=[("xsd", s)], w=XS[s],
                      key=("xrl", s))
            for cg in range(8):
                banks = [next_bank() for _ in range(NS)]
                for kq in range(4):
                    si = wload(wmo[cg, kq], 4096)
                    for k8 in range(8):
                        kc = kq * 8 + k8
                        for s in range(NS):
                            mm(banks[s], mergedT[:, kc, s * 128:(s + 1) * 128], ring[si][:, k8 * 512:(k8 + 1) * 512],
                               kc == 0, kc == KC - 1, r=[("ring", si), ("BQ", kc // 2)])
                for s in range(NS):
                    xs = x_sb[:, s, cg * 512:(cg + 1) * 512]
                    P.op("dve", lambda e, xs=xs, b=banks[s]: e.tensor_tensor(xs, ps[:, b, :], xs, ALU.add),
                         r=[("ps", banks[s]), ("XR", s * 8 + cg)], w=[("XR", s * 8 + cg)])

        for ti in range(n_tiles):
            for s in range(NS):
                r0 = ti * T + s * 128
                P.dma("sp", lambda e, s=s, r0=r0: e.dma_start(out=x_sb[:, s, :], in_=x_in[r0:r0 + 128, :]), w=XS[s],
                      key=("xload", s))
            ffn("ffn1")
            if ti == 0:
                dbg_dump("x1", xr[:, :].bitcast(F32), [a for s in range(NS) for a in XS[s]])
            mixer(ti)
            if ti == 0:
                dbg_dump("x2", xr[:, :].bitcast(F32), [a for s in range(NS) for a in XS[s]])
            ffn("ffn2")
            P.dma("sp", lambda e: e.dma_start(out=gb, in_=g_norm["final"].partition_broadcast(128)),
                  w=BA(0, 16 * K), key=("gbload",))
            norm_stats()
            for s in range(NS):
                rs = stat[:, 4 + s:5 + s]
                norm_recip(s)
                P.op("dve", lambda e, s=s, rs=rs: e.scalar_tensor_tensor(x_sb[:, s, :], x_sb[:, s, :], rs, gb,
                                                                           ALU.mult, ALU.mult),
                     r=XS[s] + BA(0, 16 * K) + [("stat", 4 + s)], w=XS[s])
                r0 = ti * T + s * 128
                P.dma("sp", lambda e, s=s, r0=r0: e.dma_start(out=out_d[r0:r0 + 128, :], in_=x_sb[:, s, :]),
                      r=XS[s], w=[("out", ti, s)], key=("ostore", s), final=True)
        P.build(st)
        nc._prog_stats = (len(P.ops), P.n_sems, P.counts)
    return nc


def _f32(a):
    return np.ascontiguousarray(a, dtype=np.float32)


def prep_weights(inp):
    m = {}
    m["g_ffn1"] = _f32(inp["ffn1_norm_g"][0])
    m["g_mix"] = _f32(inp["mix_norm_g"][0])
    m["g_ffn2"] = _f32(inp["ffn2_norm_g"][0])
    m["g_final"] = _f32(inp["final_norm_g"])

    def fchunks(w):
        C = w.shape[1] // 128
        return _f32(w.reshape(KC, 128, C, 128).transpose(2, 1, 0, 3).reshape(C, 128, KC * 128))

    def rounds(w):
        R = w.shape[0] // 128
        return _f32(w.reshape(R, 128, 8, 512).transpose(2, 0, 1, 3))

    def tgroups(w):
        G = w.shape[1] // 512
        return _f32(w.reshape(4, 8, 128, G, 512).transpose(3, 0, 2, 1, 4).reshape(G, 4, 128, 8 * 512))

    for k in ("ffn1", "ffn2"):
        m[k + "_wg"] = fchunks(inp[k + "_w_gate"][0])
        m[k + "_wu"] = fchunks(inp[k + "_w_up"][0])
        m[k + "_wd"] = rounds(inp[k + "_w_down"][0])
    w_in = inp["w_in"][0]
    nat = fchunks(w_in[:, 0:O_V])
    ga = fchunks(w_in[:, O_GA:O_GA + D])
    gb_ = fchunks(w_in[:, O_GB:O_GB + D])
    m["win_f"] = np.concatenate([nat, ga, gb_], axis=0)
    m["win_t"] = np.concatenate([tgroups(w_in[:, O_V:O_V + DV]), tgroups(w_in[:, O_R:O_R + DV])], axis=0)
    wa = w_in[:, O_A:O_A + RANK]
    m["win_a"] = _f32(wa.reshape(KC, 128, RANK).transpose(1, 0, 2).reshape(128, KC * RANK))

    def ochunks(w):
        return _f32(w.reshape(CC, 128, KC, 128).transpose(2, 1, 0, 3).reshape(KC, 128, CC * 128))

    m["wco"] = ochunks(inp["w_conv_out"][0])
    m["wgo"] = ochunks(inp["w_gla_out"][0])
    wmo = inp["w_mix_out"][0]
    m["wmo"] = _f32(wmo.reshape(4, 8, 128, 8, 512).transpose(3, 0, 2, 1, 4).reshape(8, 4, 128, 8 * 512))
    m["wup"] = _f32(inp["w_alpha_up"][0])
    m["gng"] = _f32(inp["gla_norm_g"][0])
    cw = inp["conv_w"][0]
    pc = np.zeros((128, 3 * CC + CC + 8 + 2 * KC), np.float32)
    pc[:, 0:3 * CC] = cw.reshape(3, CC, 128).transpose(2, 1, 0).reshape(128, 3 * CC)
    pc[:, 3 * CC:4 * CC] = inp["conv_b"][0].reshape(CC, 128).T
    pc[:, 4 * CC:4 * CC + 8] = inp["b_alpha"][0].reshape(8, 128).T
    pc[:, 4 * CC + 8:4 * CC + 8 + KC] = inp["b_merge"][0, 0].reshape(KC, 128).T
    pc[:, 4 * CC + 8 + KC:] = inp["b_merge"][0, 1].reshape(KC, 128).T
    m["pcol"] = pc
    cm = np.zeros((128, 128 * 3 + 512), np.float32)
    cm[:, 0:128] = np.eye(128, dtype=np.float32)
    j = np.arange(128)[:, None]
    i = np.arange(128)[None, :]
    cm[:, 128:256] = (j <= i)
    cm[:, 256:384] = (j > i) & ((j // 64) == (i // 64))
    rm = np.ones(512, np.float32)
    rm[::128] = 0.0
    cm[:, 384:896] = rm[None, :]
    m["cmat"] = cm
    return m


N_CORES = 8
N_TILES = 2


def shard_x(x):
    B = x.shape[0]
    xt = x.reshape(B, SEQ // T, T, D)
    return [_f32(xt[c // 2, (c % 2)::2].reshape(N_TILES * T, D)) for c in range(2 * B)]


def unshard_out(outs, B):
    out = np.empty((B, SEQ // T, T, D), np.float32)
    for c in range(2 * B):
        out[c // 2, (c % 2)::2] = np.asarray(outs[c], dtype=np.float32).reshape(N_TILES, T, D)
    return out.reshape(B, SEQ, D)


def core_inputs(shared, xs, c, nonce):
    mp = dict(shared)
    mp["x"] = xs[c]
    role = np.zeros((128, 2), np.float32)
    role[:, c % 2] = 1.0
    mp["role"] = role
    mp["par"] = np.array([[c % 2]], np.int32)
    mp["nonce"] = np.array([[nonce]], np.int32)
    return mp


def kernel(**inputs):
    inp = {k: np.asarray(v) for k, v in inputs.items()}
    x = inp["x"]
    B = x.shape[0]
    shared = prep_weights(inp)
    nc = build_program(N_TILES, N_TILES)
    xs = shard_x(x)
    nonce = int(np.random.randint(1, 1 << 28)) * 4
    in_maps = [core_inputs(shared, xs, c, nonce) for c in range(N_CORES)]
    res = run_bass_kernel_spmd(nc, in_maps, core_ids=list(range(N_CORES)))
    return unshard_out([res.results[c]["out"] for c in range(N_CORES)], B)
```

```python
import math
from contextlib import ExitStack

import numpy as np
import concourse.bass as bass
import concourse.mybir as mybir
from concourse.bass_utils import run_bass_kernel_spmd

F32 = mybir.dt.float32
BF16 = mybir.dt.bfloat16
I32 = mybir.dt.int32
ALU = mybir.AluOpType
AF = mybir.ActivationFunctionType

D = 4096
KC = D // 128
DFF = 11008
FC = DFF // 128
DCONV = 2048
CC = DCONV // 128
HEADS = 4
DK = 1024
DV = 2048
HK = 256
HV = 512
RANK = 16
EPS = 1e-6
T = 512
NS = 4
SEQ = 2048
O_CB, O_CC, O_CU, O_Q, O_K, O_V, O_R, O_A, O_GA, O_GB = 0, 2048, 4096, 6144, 7168, 8192, 10240, 12288, 12304, 16400
DIN = 20496
NSLOT = 4
FGRP = 16


class _Op:
    __slots__ = ("eng", "emit", "reads", "writes", "dma_key", "deps", "signal", "sigval", "reqs", "custom")

    def __init__(self, eng, emit, reads, writes, dma_key):
        self.eng = eng
        self.emit = emit
        self.reads = reads
        self.writes = writes
        self.dma_key = dma_key
        self.deps = ()
        self.signal = False
        self.sigval = 0
        self.reqs = ()
        self.custom = False


class Prog:
    ENGS = ("pe", "act", "dve", "pool", "sp")

    def __init__(self, nc):
        self.nc = nc
        self.ops = []
        self.last_w = {}
        self.readers = {}
        self.final_dma = []

    def op(self, eng, emit, r=(), w=(), custom=False):
        o = _Op(eng, emit, tuple(r), tuple(w), None)
        o.custom = custom
        self._track(o)
        return o

    def dma(self, queue, emit, r=(), w=(), key=None, final=False, custom=False):
        if key is None:
            assert len(w) == 1
            key = w[0]
        o = _Op(queue, emit, tuple(r), tuple(w), key)
        o.custom = custom
        self._track(o)
        if final:
            self.final_dma.append(o)
        return o

    def _track(self, o):
        idx = len(self.ops)
        deps = set()
        for a in o.reads:
            lw = self.last_w.get(a)
            if lw is not None:
                deps.add(lw)
        for a in o.writes:
            lw = self.last_w.get(a)
            if lw is not None:
                deps.add(lw)
            rl = self.readers.get(a)
            if rl:
                deps.update(rl.values())
        deps.discard(idx)
        o.deps = deps
        for a in o.reads:
            self.readers.setdefault(a, {})[(o.eng, o.dma_key)] = idx
        for a in o.writes:
            self.last_w[a] = idx
            self.readers[a] = {}
        self.ops.append(o)

    def build(self, stack):
        nc = self.nc
        ops = self.ops

        def skip(do, o):
            if do.dma_key is not None:
                return False
            if do.eng == "pe":
                return o.eng == "pe" and o.dma_key is None
            return do.eng in ("sp", "pool") and o.eng == do.eng

        for o in ops:
            for d in o.deps:
                do = ops[d]
                if do.dma_key is None and not skip(do, o):
                    do.signal = True
        cnt = {e: 0 for e in self.ENGS}
        dcnt = {}
        for o in ops:
            if o.dma_key is not None:
                dcnt[o.dma_key] = dcnt.get(o.dma_key, 0) + 1
                o.sigval = 16 * dcnt[o.dma_key]
            elif o.signal:
                cnt[o.eng] += 1
                o.sigval = cnt[o.eng]
        esem = {e: stack.enter_context(nc.semaphore("s_" + e)) for e in ("pe", "act", "dve", "pool")}
        dsem = {}
        for i, k in enumerate(dcnt):
            dsem[k] = stack.enter_context(nc.semaphore("d%d" % i))
        self.n_sems = 4 + len(dsem)
        self.counts = dict(cnt)
        for o in ops:
            req = {}
            for d in o.deps:
                do = ops[d]
                if do.dma_key is not None:
                    s = dsem[do.dma_key]
                    k = ("d", do.dma_key)
                else:
                    if skip(do, o):
                        continue
                    s = esem[do.eng]
                    k = ("e", do.eng)
                if k not in req or req[k][1] < do.sigval:
                    req[k] = (s, do.sigval)
            o.reqs = req
        block = stack.enter_context(nc.Block())
        by_eng = {e: [o for o in ops if o.eng == e] for e in self.ENGS}
        final_dma = self.final_dma

        def run(eng_name, eng):
            known = {}
            for o in by_eng[eng_name]:
                for k, (s, v) in o.reqs.items():
                    if known.get(k, 0) < v:
                        eng.wait_ge(s, v)
                        known[k] = v
                def finish(ins, o=o):
                    if o.dma_key is not None:
                        ins.then_inc(dsem[o.dma_key], 16)
                    elif o.signal:
                        ins.then_inc(esem[eng_name], 1)

                if o.custom:
                    o.emit(eng, finish)
                else:
                    finish(o.emit(eng))
            if eng_name == "sp":
                for o in final_dma:
                    eng.wait_ge(dsem[o.dma_key], o.sigval)

        @block.tensor
        def _(e):
            run("pe", e)

        @block.scalar
        def _(e):
            run("act", e)

        @block.vector
        def _(e):
            run("dve", e)

        @block.gpsimd
        def _(e):
            run("pool", e)

        @block.sync
        def _(e):
            run("sp", e)


def build_program(n_tiles, tiles_per_seq, dbg=None, exchange=True):
    nc = bass.Bass("TRN2", target_bir_lowering=False)
    NTOK = n_tiles * T

    def din(name, shape, dt=F32):
        return nc.dram_tensor(name, list(shape), dt, kind="ExternalInput").ap()

    x_in = din("x", [NTOK, D])
    out_d = nc.dram_tensor("out", [NTOK, D], F32, kind="ExternalOutput").ap()
    xs_d = nc.dram_tensor("xspill", [NS, 128, D], F32).ap()
    g_norm = {k: din("g_" + k, [D]) for k in ("ffn1", "mix", "ffn2", "final")}
    ffn_w = {}
    for k in ("ffn1", "ffn2"):
        ffn_w[k] = (din(k + "_wg", [FC, 128, KC * 128]), din(k + "_wu", [FC, 128, KC * 128]),
                    din(k + "_wd", [8, FC, 128, 512]))
    win_f = din("win_f", [128, 128, KC * 128])
    win_t = din("win_t", [8, 4, 128, 8 * 512])
    win_a = din("win_a", [128, KC * RANK])
    wco = din("wco", [KC, 128, CC * 128])
    wgo = din("wgo", [KC, 128, CC * 128])
    wmo = din("wmo", [8, 4, 128, 8 * 512])
    wup_d = din("wup", [RANK, DK])
    gng_d = din("gng", [HV])
    NPC = 3 * CC + CC + 8 + 2 * KC
    pcol_d = din("pcol", [128, NPC])
    cmat_d = din("cmat", [128, 128 * 3 + 512])
    role_d = din("role", [128, 2])
    par_d = din("par", [1, 1], I32)
    nonce_d = din("nonce", [1, 1], I32)
    NR = max(n_tiles, 1)
    sh_S = nc.dram_tensor("sh_S", [2, NR, 128, HEADS * 2 * HV], F32, addr_space="Shared").ap()
    sh_C = nc.dram_tensor("sh_C", [2, NR, 128, HEADS * 2 * HV], F32, addr_space="Shared").ap()
    sh_D = nc.dram_tensor("sh_D", [2, NR, 128, 8], F32, addr_space="Shared").ap()
    sh_H = nc.dram_tensor("sh_H", [2, NR, 128, 2 * CC], F32, addr_space="Shared").ap()
    sh_F = nc.dram_tensor("sh_F", [2, 8], I32, addr_space="Shared").ap()
    dbg_t = {}
    if dbg:
        for name, (shape, dt_) in dbg.items():
            dbg_t[name] = nc.dram_tensor("dbg_" + name, list(shape), dt_, kind="ExternalOutput").ap()

    with ExitStack() as st:
        P = Prog(nc)
        sb = lambda n, shp, dt: st.enter_context(nc.sbuf_tensor("sb_" + n, shp, dt))
        ring = [sb("ring%d" % i, [128, 4096], BF16) for i in range(NSLOT)]
        hT = sb("hT", [128, KC, T], BF16)
        xr = sb("xr", [128, 32768], BF16)
        bq = sb("bq", [128, 16384], BF16)
        st_f = sb("st_f", [128, HEADS, 2, HV], F32)
        cmat = sb("cmat", [128, 128 * 3 + 512], F32)
        ident = sb("ident", [128, 128], BF16)
        pcol = sb("pcol", [128, NPC], F32)
        negb = sb("negb", [128, 8], F32)
        gng = sb("gng", [128, HV], F32)
        wup = sb("wup", [RANK, DK], F32)
        alT = sb("alT", [RANK, T], F32)
        halo = sb("halo", [128, CC, 2], F32)
        hin = sb("hin", [128, CC, 2], F32)
        hs1 = sb("hs1", [128, CC, 2], F32)
        acc0 = sb("acc0", [128, CC, 2], F32)
        cb0 = sb("cb0", [128, CC, 2], F32)
        fx = sb("fx", [128, 4, CC], F32)
        role = sb("role", [128, 2], F32)
        dloc = sb("dloc", [128, 8], F32)
        coef = sb("coef", [128, 8], F32)
        stat = sb("stat", [128, 16], F32)
        tmpf = [sb("tmpf%d" % i, [128, T], F32) for i in range(2)]
        ps = st.enter_context(nc.psum_tensor("ps", [128, 8, 512], F32))

        x_sb = xr[:, :].bitcast(F32).rearrange("p (s d) -> p s d", s=NS)
        gb = bq[:, 0:8192].bitcast(F32)
        hp = bq[:, 8192:12288]
        hp2 = [bq[:, 8192:12288], bq[:, 12288:16384]]
        actT = bq[:, 0:FGRP * T].rearrange("p (f t) -> p f t", f=FGRP)
        mergedT = bq[:, :].rearrange("p (c t) -> p c t", c=KC)
        maskL = cmat[:, 128:256]
        maskU = cmat[:, 256:384]
        rmask = cmat[:, 384:896]

        def xr_bf(lo, hi):
            return xr[:, lo // 2:hi // 2]

        def xr_f32(lo, hi):
            return xr[:, lo // 2:hi // 2].bitcast(F32)

        def XA(lo, hi):
            return [("XR", p) for p in range(lo // 2048, (hi + 2047) // 2048)]

        def BA(lo, hi):
            return [("BQ", p) for p in range(lo // 2048, (hi + 2047) // 2048)]

        K = 1024
        OGT = (0, 16 * K)
        PRD = (16 * K, 32 * K)
        Q1, Q2, K1, K2 = [(32 * K + i * 2 * K, 34 * K + i * 2 * K) for i in range(4)]
        K1T = (40 * K, 42 * K)
        VH = (42 * K, 46 * K)
        SR = (46 * K, 50 * K)
        OG = (50 * K, 54 * K)
        TA = (54 * K, 56 * K)
        TB = (56 * K, 58 * K)
        CUM = (58 * K, 60 * K)
        SC1 = (60 * K, 60 * K + 512)
        SC2 = (60 * K + 512, 61 * K)
        SCT = (61 * K, 61 * K + 256)
        T1 = (62 * K, 64 * K)
        A_STB = BA(0, 8 * K)
        A_EL = BA(8 * K, 12 * K)
        A_ENL = BA(12 * K, 16 * K)
        ogT = xr_bf(*OGT).rearrange("p (c t) -> p c t", c=16)
        stb = bq[:, 0:4096].rearrange("p (h c v) -> p h c v", h=4, c=2)
        prodT = xr_bf(*PRD).rearrange("p (c t) -> p c t", c=CC)

        state = {"slot": 0, "bank": 0}

        def next_slot():
            i = state["slot"] % NSLOT
            state["slot"] += 1
            return i

        def next_bank():
            i = state["bank"] % 8
            state["bank"] += 1
            return i

        def wload(src_ap, nelem, view=None):
            i = next_slot()
            dst = ring[i][:, 0:nelem]
            if view is not None:
                dst = view(dst)
            P.dma("pool", lambda e, dst=dst, src=src_ap: e.dma_start(out=dst, in_=src), w=[("ring", i)])
            return i

        def mm(bank, lhsT, rhs, start, stop, r, out=None):
            o = ps[:, bank, :] if out is None else out
            P.op("pe", lambda e, o=o, lhsT=lhsT, rhs=rhs, start=start, stop=stop:
                 e.matmul(o, lhsT, rhs, start=start, stop=stop), r=r, w=[("ps", bank)])

        def dbg_dump(name, src_ap, atoms):
            if name in dbg_t:
                P.dma("sp", lambda e, d=dbg_t[name], s=src_ap: e.dma_start(out=d, in_=s), r=atoms,
                      w=[("dbg", name)], final=True)

        P.dma("sp", lambda e: e.dma_start(out=cmat[:], in_=cmat_d), w=["cmat"])
        P.dma("sp", lambda e: e.dma_start(out=pcol[:], in_=pcol_d), w=["pcol"])
        P.dma("sp", lambda e: e.dma_start(out=wup[:], in_=wup_d), w=["wup"])
        P.dma("sp", lambda e: e.dma_start(out=gng[:], in_=gng_d.partition_broadcast(128)), w=["gng"])
        P.op("dve", lambda e: e.tensor_copy(ident[:], cmat[:, 0:128]), r=["cmat"], w=["ident"])
        CW0 = 0
        CB0 = 3 * CC
        BA0 = 4 * CC
        BM0 = 4 * CC + 8
        P.op("dve", lambda e: e.tensor_scalar(negb[:], pcol[:, BA0:BA0 + 8], -1.0, None, ALU.mult),
             r=["pcol"], w=["negb"])

        P.op("dve", lambda e: e.memset(stat[:, 10:11], 1.0), w=[("stat", 10)])
        P.op("dve", lambda e: e.memset(stat[:, 11:12], EPS), w=[("stat", 11)])

        def rstd_ops(ss, rs, n, a_ss, a_rs):
            P.op("act", lambda e: e.activation(rs, ss, AF.Sqrt, bias=stat[:, 11:12], scale=1.0 / n),
                 r=[a_ss, ("stat", 11)], w=[a_rs])
            P.op("dve", lambda e: e.reciprocal(rs, rs), r=[a_rs], w=[a_rs])

        XS = [[("XR", s * 8 + c) for c in range(8)] for s in range(NS)]

        junk = hT[:, 24:32, :].rearrange("p a b -> p (a b)")
        A_junk = [("hT", kc, s) for kc in range(24, 32) for s in range(NS)]

        def norm_stats():
            for s in range(NS):
                ss = stat[:, s:s + 1]
                rs = stat[:, 4 + s:5 + s]
                P.op("dve", lambda e, ss=ss: e.memset(ss, 0.0), w=[("stat", s)])
                P.op("act", lambda e, s=s, ss=ss: e.activation(junk, x_sb[:, s, :], AF.Square, accum_out=ss),
                     r=XS[s], w=A_junk + [("stat", s)])
                P.op("act", lambda e, ss=ss, rs=rs: e.activation(rs, ss, AF.Sqrt, bias=stat[:, 11:12], scale=1.0 / D),
                     r=[("stat", s), ("stat", 11)], w=[("stat", 4 + s)])

        def norm_recip(s):
            rs = stat[:, 4 + s:5 + s]
            P.op("dve", lambda e, rs=rs: e.reciprocal(rs, rs), r=[("stat", 4 + s)], w=[("stat", 4 + s)])

        def norm_to_hT(gkey):
            P.dma("sp", lambda e: e.dma_start(out=gb, in_=g_norm[gkey].partition_broadcast(128)),
                  w=BA(0, 16 * K), key=("gbload",))
            norm_stats()

            def scale(s):
                rs = stat[:, 4 + s:5 + s]
                hp = hp2[s % 2]
                A_hp = BA(16 * K + (s % 2) * 8 * K, 24 * K + (s % 2) * 8 * K)
                norm_recip(s)
                P.op("dve", lambda e, s=s, rs=rs, hp=hp: e.scalar_tensor_tensor(hp, x_sb[:, s, :], rs, gb, ALU.mult, ALU.mult),
                     r=XS[s] + BA(0, 16 * K) + [("stat", 4 + s)], w=A_hp)

            def transp(s):
                hp = hp2[s % 2]
                A_hp = BA(16 * K + (s % 2) * 8 * K, 24 * K + (s % 2) * 8 * K)
                for k4 in range(KC // 4):
                    b = next_bank()
                    pb = ps[:, b, :].bitcast(BF16)
                    for j in range(4):
                        kc = k4 * 4 + j
                        P.op("pe", lambda e, pb=pb, j=j, kc=kc, hp=hp: e.transpose(pb[:, j * 128:(j + 1) * 128],
                                                                            hp[:, kc * 128:(kc + 1) * 128], ident[:]),
                             r=A_hp + ["ident"], w=[("ps", b)])
                    dst = hT[:, k4 * 4:k4 * 4 + 4, s * 128:(s + 1) * 128]
                    src = pb[:, 0:512].rearrange("p (a b) -> p a b", a=4)
                    wat = [("hT", kc, s) for kc in range(k4 * 4, k4 * 4 + 4)]
                    if k4 % 2 == 0:
                        P.op("act", lambda e, dst=dst, src=src: e.copy(dst, src), r=[("ps", b)], w=wat)
                    else:
                        P.op("dve", lambda e, dst=dst, src=src: e.tensor_copy(dst, src), r=[("ps", b)], w=wat)

            scale(0)
            scale(1)
            transp(0)
            scale(2)
            transp(1)
            scale(3)
            transp(2)
            transp(3)

        def hT_atoms(kc):
            return [("hT", kc, s) for s in range(NS)]

        def ffn(key):
            wg_d, wu_d, wd_d = ffn_w[key]
            norm_to_hT(key)
            groups = [(f0, min(FGRP, FC - f0)) for f0 in range(0, FC, FGRP)]
            for (f0, nf) in groups:
                for j in range(nf):
                    fc = f0 + j
                    sg_i = wload(wg_d[fc], 4096)
                    su_i = wload(wu_d[fc], 4096)
                    bg, bu = next_bank(), next_bank()
                    for kc in range(KC):
                        mm(bg, ring[sg_i][:, kc * 128:(kc + 1) * 128], hT[:, kc, :], kc == 0, kc == KC - 1,
                           r=[("ring", sg_i)] + hT_atoms(kc))
                    for kc in range(KC):
                        mm(bu, ring[su_i][:, kc * 128:(kc + 1) * 128], hT[:, kc, :], kc == 0, kc == KC - 1,
                           r=[("ring", su_i)] + hT_atoms(kc))
                    tf = tmpf[fc % 2]
                    P.op("act", lambda e, tf=tf, bg=bg: e.activation(tf[:], ps[:, bg, :], AF.Silu),
                         r=[("ps", bg)], w=[("tmpf", fc % 2)])
                    P.op("dve", lambda e, tf=tf, bu=bu, j=j: e.tensor_tensor(actT[:, j, :], tf[:], ps[:, bu, :], ALU.mult),
                         r=[("tmpf", fc % 2), ("ps", bu)], w=[("BQ", j // 2)])
                for cg in range(8):
                    banks = [next_bank() for _ in range(NS)]
                    for j0 in range(0, nf, 8):
                        n = min(8, nf - j0)
                        si = wload(wd_d[cg, f0 + j0:f0 + j0 + n].rearrange("f p c -> p f c"), n * 512,
                                   view=lambda a, n=n: a.rearrange("p (f c) -> p f c", f=n))
                        for jj in range(n):
                            j = j0 + jj
                            for s in range(NS):
                                mm(banks[s], actT[:, j, s * 128:(s + 1) * 128], ring[si][:, jj * 512:(jj + 1) * 512],
                                   j == 0, j == nf - 1, r=[("ring", si), ("BQ", j // 2)])
                    for s in range(NS):
                        xs = x_sb[:, s, cg * 512:(cg + 1) * 512]
                        P.op("dve", lambda e, xs=xs, b=banks[s]: e.scalar_tensor_tensor(xs, ps[:, b, :], 0.5, xs,
                                                                                         ALU.mult, ALU.add),
                             r=[("ps", banks[s]), ("XR", s * 8 + cg)], w=[("XR", s * 8 + cg)])

        def fchunk(col_chunk, bank):
            si = wload(win_f[col_chunk], 4096)
            for kc in range(KC):
                mm(bank, ring[si][:, kc * 128:(kc + 1) * 128], hT[:, kc, :], kc == 0, kc == KC - 1,
                   r=[("ring", si)] + hT_atoms(kc))

        def tgroup(grp):
            banks = [next_bank() for _ in range(NS)]
            for kq in range(4):
                si = wload(win_t[grp, kq], 4096)
                for k8 in range(8):
                    kc = kq * 8 + k8
                    for s in range(NS):
                        mm(banks[s], hT[:, kc, s * 128:(s + 1) * 128], ring[si][:, k8 * 512:(k8 + 1) * 512],
                           kc == 0, kc == KC - 1, r=[("ring", si), ("hT", kc, s)])
            return banks

        regs = {}

        def sp_init(e, finish):
            regs["par"] = e.alloc_register("r_par")
            regs["nonce"] = e.alloc_register("r_nonce")
            regs["fv"] = e.alloc_register("r_fv")
            regs["pr"] = e.alloc_register("r_pr")
            e.reg_load(regs["par"], par_d[0:1, 0:1])
            e.reg_load(regs["nonce"], nonce_d[0:1, 0:1])

        P.op("sp", sp_init, custom=True)
        P.dma("sp", lambda e: e.dma_start(out=role[:], in_=role_d), w=["role"])
        isA = role[:, 0:1]
        isB = role[:, 1:2]

        def own_slot_dma(dst_of_slot, src, r, w, key):
            def emit(e, finish):
                with e.If_eq(regs["par"], 0):
                    finish(e.dma_start(out=dst_of_slot(0), in_=src))
                with e.Else():
                    finish(e.dma_start(out=dst_of_slot(1), in_=src))
            P.dma("sp", emit, r=r, w=w, key=key, custom=True)

        def mixer(ti):
            first = (ti % tiles_per_seq) == 0
            norm_to_hT("mix")
            for s in range(NS):
                P.dma("sp", lambda e, s=s: e.dma_start(out=xs_d[s], in_=x_sb[:, s, :]), r=XS[s], w=[("xsd", s)])
            si = wload(win_a, KC * RANK)
            ba = next_bank()
            for kc in range(KC):
                mm(ba, ring[si][:, kc * RANK:(kc + 1) * RANK], hT[:, kc, :], kc == 0, kc == KC - 1,
                   r=[("ring", si)] + hT_atoms(kc), out=ps[0:RANK, ba, :])
            P.op("act", lambda e: e.copy(alT[:], ps[0:RANK, ba, :]), r=[("ps", ba)], w=["alT"])

            TCC = (40 * K, 42 * K)
            CCU = (42 * K, 44 * K + 16)
            ACC = (46 * K, 48 * K)
            tcc = xr_f32(*TCC)
            ccu = xr_f32(42 * K, 44 * K + 8)
            acc = xr_f32(*ACC)
            for c in range(CC):
                b_cc, b_cu, b_cb = next_bank(), next_bank(), next_bank()
                fchunk(O_CC // 128 + c, b_cc)
                fchunk(O_CU // 128 + c, b_cu)
                fchunk(O_CB // 128 + c, b_cb)
                P.op("act", lambda e, b=b_cc: e.copy(tcc, ps[:, b, :]), r=[("ps", b_cc)], w=XA(*TCC))
                P.op("dve", lambda e: e.memset(ccu[:, 0:2], 0.0), w=XA(*CCU))
                P.op("dve", lambda e, b=b_cu: e.tensor_tensor(ccu[:, 2:514], tcc, ps[:, b, :], ALU.mult),
                     r=[("ps", b_cu)] + XA(*TCC), w=XA(*CCU))
                P.op("dve", lambda e, c=c: e.tensor_copy(halo[:, c, :], ccu[:, 512:514]), r=XA(*CCU), w=["halo"])
                w0 = pcol[:, CW0 + c * 3 + 0:CW0 + c * 3 + 1]
                w1 = pcol[:, CW0 + c * 3 + 1:CW0 + c * 3 + 2]
                w2 = pcol[:, CW0 + c * 3 + 2:CW0 + c * 3 + 3]
                cb_ = pcol[:, CB0 + c:CB0 + c + 1]
                P.op("dve", lambda e, w2=w2, cb_=cb_: e.tensor_scalar(acc, ccu[:, 2:514], w2, cb_, ALU.mult, ALU.add),
                     r=XA(*CCU) + ["pcol"], w=XA(*ACC))
                P.op("dve", lambda e, w1=w1: e.scalar_tensor_tensor(acc, ccu[:, 1:513], w1, acc, ALU.mult, ALU.add),
                     r=XA(*CCU) + XA(*ACC) + ["pcol"], w=XA(*ACC))
                P.op("dve", lambda e, w0=w0: e.scalar_tensor_tensor(acc, ccu[:, 0:512], w0, acc, ALU.mult, ALU.add),
                     r=XA(*CCU) + XA(*ACC) + ["pcol"], w=XA(*ACC))
                P.op("dve", lambda e, c=c: e.tensor_copy(acc0[:, c, :], acc[:, 0:2]), r=XA(*ACC), w=["acc0"])
                P.op("dve", lambda e, c=c, b=b_cb: e.tensor_copy(cb0[:, c, :], ps[:, b, 0:2]), r=[("ps", b_cb)], w=["cb0"])
                P.op("dve", lambda e, c=c, b=b_cb: e.tensor_tensor(prodT[:, c, :], acc, ps[:, b, :], ALU.mult),
                     r=XA(*ACC) + [("ps", b_cb)], w=XA(16 * K + c * K, 17 * K + c * K))

            q1 = xr_bf(*Q1).rearrange("p (c t) -> p c t", c=2)
            q2 = xr_bf(*Q2).rearrange("p (c t) -> p c t", c=2)
            k1 = xr_bf(*K1).rearrange("p (c t) -> p c t", c=2)
            k2 = xr_bf(*K2).rearrange("p (c t) -> p c t", c=2)
            k1t = xr_bf(*K1T).rearrange("p (b d) -> p b d", b=4)
            vh_all = bq[:, 8192:16384].rearrange("p (h s v) -> p h s v", h=HEADS, s=NS)
            sr = xr_bf(*SR).rearrange("p (s v) -> p s v", s=NS)
            og = xr_bf(*OG).rearrange("p (s v) -> p s v", s=NS)
            eL = bq[:, 4096:6144].bitcast(F32).rearrange("p (c t) -> p c t", c=2)
            eNL = bq[:, 6144:8192].bitcast(F32).rearrange("p (c t) -> p c t", c=2)
            tA = xr_f32(*TA)
            tB = xr_f32(*TB)
            cum = xr_f32(*CUM)
            sc1s = [xr_f32(42 * K + b * 512, 42 * K + (b + 1) * 512) for b in range(4)]
            sc2s = [xr_f32(44 * K + b * 512, 44 * K + (b + 1) * 512) for b in range(4)]
            scTs = [xr_bf(60 * K + b * 256, 60 * K + (b + 1) * 256) for b in range(4)]
            A_S1, A_S2, A_ST = XA(42 * K, 44 * K), XA(44 * K, 46 * K), XA(60 * K, 61 * K)
            t1 = xr_f32(*T1)

            def head_prep(h, full):
                for dc in range(2):
                    c = 2 * h + dc
                    bz = next_bank()
                    P.op("pe", lambda e, bz=bz, c=c: e.matmul(ps[:, bz, :], wup[:, c * 128:(c + 1) * 128], alT[:],
                                                               start=True, stop=True),
                         r=["wup", "alT"], w=[("ps", bz)])
                    P.op("act", lambda e, bz=bz, c=c: e.activation(tA, ps[:, bz, :], AF.Exp, bias=negb[:, c:c + 1], scale=-1.0),
                         r=[("ps", bz), "negb"], w=XA(*TA))
                    P.op("act", lambda e: e.activation(tB, tA, AF.Ln, bias=stat[:, 10:11], scale=1.0),
                         r=XA(*TA) + [("stat", 10)], w=XA(*TB))
                    P.op("dve", lambda e: e.tensor_tensor_scan(cum, rmask, tB, 0.0, ALU.mult, ALU.add),
                         r=XA(*TB) + ["cmat"], w=XA(*CUM))
                    P.op("act", lambda e, dc=dc: e.activation(eL[:, dc, :], cum, AF.Exp, scale=-1.0 / 16.0),
                         r=XA(*CUM), w=A_EL)
                    P.op("act", lambda e, dc=dc: e.activation(eNL[:, dc, :], cum, AF.Exp, scale=1.0 / 16.0),
                         r=XA(*CUM), w=A_ENL)
                    if full:
                        bqk = next_bank()
                        fchunk(O_Q // 128 + c, bqk)
                        P.op("dve", lambda e, b=bqk, dc=dc: e.scalar_tensor_tensor(q1[:, dc, :], ps[:, b, :], HK ** -0.5,
                                                                                     eL[:, dc, :], ALU.mult, ALU.mult),
                             r=[("ps", bqk)] + A_EL, w=XA(*Q1))
                        P.op("dve", lambda e, b=bqk, dc=dc: e.scalar_tensor_tensor(q2[:, dc, :], ps[:, b, :], HK ** -0.5,
                                                                                     eNL[:, dc, :], ALU.mult, ALU.mult),
                             r=[("ps", bqk)] + A_ENL, w=XA(*Q2))
                    bkk = next_bank()
                    fchunk(O_K // 128 + c, bkk)
                    P.op("dve", lambda e, b=bkk, dc=dc: e.tensor_tensor(k1[:, dc, :], ps[:, b, :], eNL[:, dc, :], ALU.mult),
                         r=[("ps", bkk)] + A_ENL, w=XA(*K1))
                    if full:
                        P.op("dve", lambda e, b=bkk, dc=dc: e.tensor_tensor(k2[:, dc, :], ps[:, b, :], eL[:, dc, :], ALU.mult),
                             r=[("ps", bkk)] + A_EL, w=XA(*K2))
                    else:
                        dl = dloc[:, c:c + 1]
                        P.op("dve", lambda e, dl=dl, dc=dc: e.tensor_copy(dl, eL[:, dc, 127:128]), r=A_EL, w=["dloc"])
                        for blk in range(1, 4):
                            P.op("dve", lambda e, dl=dl, dc=dc, blk=blk: e.tensor_tensor(
                                dl, dl, eL[:, dc, blk * 128 + 127:blk * 128 + 128], ALU.mult), r=A_EL + ["dloc"], w=["dloc"])
                for blk in range(4):
                    b = next_bank()
                    pb = ps[:, b, :].bitcast(BF16)
                    for dc in range(2):
                        P.op("pe", lambda e, pb=pb, dc=dc, blk=blk: e.transpose(pb[:, dc * 128:(dc + 1) * 128],
                                                                                 k1[:, dc, blk * 128:(blk + 1) * 128], ident[:]),
                             r=XA(*K1) + ["ident"], w=[("ps", b)])
                    P.op("act", lambda e, pb=pb, blk=blk: e.copy(k1t[:, blk, :], pb[:, 0:256]), r=[("ps", b)], w=XA(*K1T))
                vh = vh_all[:, h]
                A_VH = BA(16 * K + h * 4 * K, 16 * K + (h + 1) * 4 * K)
                if (not full) or (not exchange):
                    banks = tgroup(h)
                    for s in range(NS):
                        P.op("act", lambda e, s=s, b=banks[s]: e.copy(vh[:, s, :], ps[:, b, :]), r=[("ps", banks[s])], w=A_VH)
                if full:
                    banks = tgroup(4 + h)
                    for s in range(NS):
                        P.op("act", lambda e, s=s, b=banks[s]: e.activation(sr[:, s, :], ps[:, b, :], AF.Silu),
                             r=[("ps", banks[s])], w=XA(*SR))

            def state_update(h, blk, bs, full):
                for dc in range(2):
                    sl = eL[:, dc, blk * 128 + 127:blk * 128 + 128]
                    sf = st_f[:, h, dc, :]
                    P.op("dve", lambda e, b=bs[dc], sl=sl: e.tensor_scalar(tA, ps[:, b, :], sl, None, ALU.mult),
                         r=[("ps", bs[dc])] + A_EL, w=XA(*TA))
                    P.op("dve", lambda e, sf=sf, sl=sl: e.scalar_tensor_tensor(sf, sf, sl, tA, ALU.mult, ALU.add),
                         r=XA(*TA) + A_EL + ["st_f"], w=["st_f"])
                    if full and blk < 3:
                        P.op("act", lambda e, sf=sf, dc=dc, blk=blk: e.copy(stb[:, blk + 1, dc, :], sf),
                             r=["st_f"], w=[("BQ", blk + 1)])

            def head_blocks(h, full):
                vh = vh_all[:, h]
                A_VH = BA(16 * K + h * 4 * K, 16 * K + (h + 1) * 4 * K)
                if not full:
                    for blk in range(4):
                        bs = [next_bank(), next_bank()]
                        for dc in range(2):
                            mm(bs[dc], k1t[:, blk, dc * 128:(dc + 1) * 128], vh[:, blk, :], True, True, r=XA(*K1T) + A_VH)
                        state_update(h, blk, bs, False)
                    return
                P.op("act", lambda e, h=h: e.copy(stb[:, 0], st_f[:, h]), r=["st_f"], w=[("BQ", 0)])
                for blk in range(4):
                    tk = slice(blk * 128, (blk + 1) * 128)
                    s1, s2, sT = sc1s[blk], sc2s[blk], scTs[blk]
                    b1 = next_bank()
                    for dc in range(2):
                        mm(b1, k1[:, dc, tk], q1[:, dc, tk], dc == 0, dc == 1, r=XA(*K1) + XA(*Q1), out=ps[:, b1, 0:128])
                    b2 = next_bank()
                    for dc in range(2):
                        mm(b2, k2[:, dc, tk], q2[:, dc, tk], dc == 0, dc == 1, r=XA(*K2) + XA(*Q2), out=ps[:, b2, 0:128])
                    P.op("dve", lambda e, b1=b1, s1=s1: e.tensor_tensor(s1, ps[:, b1, 0:128], maskL, ALU.mult),
                         r=[("ps", b1), "cmat"], w=A_S1)
                    P.op("dve", lambda e, b2=b2, s2=s2: e.tensor_tensor(s2, ps[:, b2, 0:128], maskU, ALU.mult),
                         r=[("ps", b2), "cmat"], w=A_S2)
                    P.op("dve", lambda e, s1=s1, s2=s2, sT=sT: e.tensor_tensor(sT, s1, s2, ALU.add), r=A_S1 + A_S2, w=A_ST)
                for blk in range(4):
                    bs = [next_bank(), next_bank()]
                    for dc in range(2):
                        mm(bs[dc], k1t[:, blk, dc * 128:(dc + 1) * 128], vh[:, blk, :], True, True, r=XA(*K1T) + A_VH)
                    state_update(h, blk, bs, True)
                for blk in range(4):
                    tk = slice(blk * 128, (blk + 1) * 128)
                    bo = next_bank()
                    mm(bo, scTs[blk], vh[:, blk, :], True, False, r=A_ST + A_VH)
                    for dc in range(2):
                        mm(bo, q1[:, dc, tk], stb[:, blk, dc, :], False, dc == 1, r=XA(*Q1) + [("BQ", blk)])
                    ss = stat[:, 8:9]
                    rs = stat[:, 9:10]
                    P.op("dve", lambda e, ss=ss: e.memset(ss, 0.0), w=[("stat", 8)])
                    P.op("act", lambda e, bo=bo, ss=ss: e.activation(t1, ps[:, bo, :], AF.Square, accum_out=ss),
                         r=[("ps", bo)], w=XA(*T1) + [("stat", 8)])
                    rstd_ops(ss, rs, HV, ("stat", 8), ("stat", 9))
                    P.op("dve", lambda e, bo=bo, rs=rs: e.scalar_tensor_tensor(t1, ps[:, bo, :], rs, gng[:], ALU.mult, ALU.mult),
                         r=[("ps", bo), ("stat", 9), "gng"], w=XA(*T1))
                    P.op("dve", lambda e, blk=blk: e.tensor_tensor(og[:, blk, :], t1, sr[:, blk, :], ALU.mult),
                         r=XA(*T1) + XA(*SR), w=XA(*OG))
                for vc in range(4):
                    b = next_bank()
                    pb = ps[:, b, :].bitcast(BF16)
                    for blk in range(4):
                        P.op("pe", lambda e, pb=pb, blk=blk, vc=vc: e.transpose(pb[:, blk * 128:(blk + 1) * 128],
                                                                                 og[:, blk, vc * 128:(vc + 1) * 128], ident[:]),
                             r=XA(*OG) + ["ident"], w=[("ps", b)])
                    P.op("act", lambda e, pb=pb, h=h, vc=vc: e.copy(ogT[:, h * 4 + vc, :], pb[:, 0:512]),
                         r=[("ps", b)], w=XA(*OGT))

            st_flat = st_f[:].rearrange("p h c v -> p (h c v)")
            halo_flat = halo[:].rearrange("p c t -> p (c t)")
            if exchange:
                P.op("dve", lambda e: e.memset(st_f[:], 0.0), w=["st_f"])
                for h in range(HEADS):
                    head_prep(h, False)
                    head_blocks(h, False)
                own_slot_dma(lambda sl: sh_S[sl, ti], st_flat, r=["st_f"], w=[("shS", ti)], key=("shS",))
                own_slot_dma(lambda sl: sh_D[sl, ti], dloc[:], r=["dloc"], w=[("shD", ti)], key=("shD",))
                own_slot_dma(lambda sl: sh_H[sl, ti], halo_flat, r=["halo"], w=[("shH", ti)], key=("shH",))

                def handshake(e, finish, ti=ti, first=first):
                    fv, pr = regs["fv"], regs["pr"]
                    e.reg_add(fv, regs["nonce"], ti + 1)
                    with e.If_eq(regs["par"], 0):
                        e.reg_save(sh_F[0:1, ti:ti + 1], fv)
                    with e.Else():
                        e.reg_save(sh_F[1:2, ti:ti + 1], fv)
                    polls = [sh_F[0:1, ti:ti + 1]]
                    if not first:
                        polls.append(sh_F[1:2, 4 + ti - 1:4 + ti])
                    for k, fl in enumerate(polls):
                        if k == 1:
                            e.reg_add(fv, regs["nonce"], ti)
                        e.reg_mov(pr, 1)
                        with e.While(pr):
                            e.reg_load(pr, fl)
                            e.reg_sub(pr, pr, fv)

                deps = [("shS", ti), ("shD", ti), ("shH", ti)] + ([("shC", ti - 1)] if not first else [])
                P.op("sp", handshake, r=deps, w=[("hs", ti)], custom=True)
                head_prep(0, True)
                P.dma("sp", lambda e: e.dma_start(out=dloc[:], in_=sh_D[0, ti]), r=[("hs", ti), ("shD", ti)], w=["dloc"],
                      key=("ldD",))
                P.op("dve", lambda e: e.tensor_scalar(coef[:], dloc[:], isB, isA, ALU.mult, ALU.add),
                     r=["dloc", "role"], w=["coef"])
                P.dma("sp", lambda e: e.dma_start(out=hin[:].rearrange("p c t -> p (c t)"), in_=sh_H[0, ti]),
                      r=[("hs", ti), ("shH", ti)], w=["hin"], key=("ldH",))
                P.op("dve", lambda e: e.tensor_scalar(hin[:], hin[:], isB, None, ALU.mult), r=["hin", "role"], w=["hin"])
                if not first:
                    P.dma("sp", lambda e: e.dma_start(out=hs1[:].rearrange("p c t -> p (c t)"), in_=sh_H[1, ti - 1]),
                          r=[("hs", ti)], w=["hs1"], key=("ldH1",))
                    P.op("dve", lambda e: e.scalar_tensor_tensor(hin[:], hs1[:], isA, hin[:], ALU.mult, ALU.add),
                         r=["hin", "hs1", "role"], w=["hin"])
                for b8 in range(8):
                    tS = tmpf[b8 % 2]
                    tC = xr_f32(50 * K + (b8 % 2) * 2 * K, 52 * K + (b8 % 2) * 2 * K)
                    aC = XA(50 * K + (b8 % 2) * 2 * K, 52 * K + (b8 % 2) * 2 * K)
                    cs = slice(b8 * HV, (b8 + 1) * HV)
                    P.dma("sp", lambda e, tS=tS, cs=cs: e.dma_start(out=tS[:], in_=sh_S[0, ti][:, cs]),
                          r=[("hs", ti), ("shS", ti)], w=[("tmpf", b8 % 2)], key=("ldS", b8 % 2))
                    if first:
                        P.op("dve", lambda e, tS=tS, cs=cs: e.tensor_scalar(st_flat[:, cs], tS[:], isB, None, ALU.mult),
                             r=[("tmpf", b8 % 2), "role"], w=["st_f"])
                    else:
                        P.op("dve", lambda e, tS=tS: e.tensor_scalar(tS[:], tS[:], isB, None, ALU.mult),
                             r=[("tmpf", b8 % 2), "role"], w=[("tmpf", b8 % 2)])
                        P.dma("sp", lambda e, tC=tC, cs=cs: e.dma_start(out=tC, in_=sh_C[1, ti - 1][:, cs]),
                              r=[("hs", ti)], w=aC, key=("ldC", b8 % 2))
                        P.op("dve", lambda e, tS=tS, tC=tC, cs=cs, b8=b8: e.scalar_tensor_tensor(
                            st_flat[:, cs], tC, coef[:, b8:b8 + 1], tS[:], ALU.mult, ALU.add),
                            r=aC + [("tmpf", b8 % 2), "coef"], w=["st_f"])
                pw = pcol[:, CW0:CW0 + 3 * CC].rearrange("p (c t) -> p c t", t=3)
                P.op("dve", lambda e: e.tensor_tensor(fx[:, 0, :], pw[:, :, 1], hin[:, :, 1], ALU.mult), r=["pcol", "hin"], w=["fx"])
                P.op("dve", lambda e: e.tensor_tensor(fx[:, 1, :], pw[:, :, 0], hin[:, :, 0], ALU.mult), r=["pcol", "hin"], w=["fx"])
                P.op("dve", lambda e: e.tensor_tensor(fx[:, 0, :], fx[:, 0, :], fx[:, 1, :], ALU.add), r=["fx"], w=["fx"])
                P.op("dve", lambda e: e.tensor_tensor(fx[:, 1, :], pw[:, :, 0], hin[:, :, 1], ALU.mult), r=["pcol", "hin", "fx"], w=["fx"])
                for t in range(2):
                    P.op("dve", lambda e, t=t: e.tensor_tensor(fx[:, 2 + t, :], fx[:, t, :], acc0[:, :, t], ALU.add),
                         r=["fx", "acc0"], w=["fx"])
                    P.op("dve", lambda e, t=t: e.tensor_tensor(prodT[:, :, t], fx[:, 2 + t, :], cb0[:, :, t], ALU.mult),
                         r=["fx", "cb0"], w=XA(*PRD))
            else:
                if first:
                    P.op("dve", lambda e: e.memset(st_f[:], 0.0), w=["st_f"])
                head_prep(0, True)
            dbg_dump("prodT", xr_bf(*PRD), XA(*PRD))
            dbg_dump("stin", st_flat, ["st_f"])
            head_blocks(0, True)
            for h in range(1, HEADS):
                head_prep(h, True)
                head_blocks(h, True)
            if exchange and ti + 1 < n_tiles:
                own_slot_dma(lambda sl: sh_C[sl, ti], st_flat, r=["st_f"], w=[("shC", ti)], key=("shC",))

                def carry_flag(e, finish, ti=ti):
                    fv = regs["fv"]
                    e.reg_add(fv, regs["nonce"], ti + 1)
                    with e.If_eq(regs["par"], 0):
                        e.reg_save(sh_F[0:1, 4 + ti:5 + ti], fv)
                    with e.Else():
                        e.reg_save(sh_F[1:2, 4 + ti:5 + ti], fv)

                P.op("sp", carry_flag, r=[("shC", ti)], w=[("cf", ti)], custom=True)
            dbg_dump("ogT", xr_bf(*OGT), XA(*OGT))
            for dch in range(KC):
                o = 40 * K + (dch % 2) * 8 * K
                sa = xr_f32(o, o + 2 * K)
                sb_ = xr_f32(o + 2 * K, o + 4 * K)
                m1 = xr_f32(o + 4 * K, o + 6 * K)
                m2 = xr_f32(o + 6 * K, o + 8 * K)
                A_sa, A_sb, A_m1, A_m2 = [XA(o + i * 2 * K, o + (i + 1) * 2 * K) for i in range(4)]
                bga, bgb, bya, byb = next_bank(), next_bank(), next_bank(), next_bank()
                fchunk(64 + dch, bga)
                fchunk(96 + dch, bgb)
                si = wload(wco[dch], CC * 128)
                for cc in range(CC):
                    mm(bya, ring[si][:, cc * 128:(cc + 1) * 128], prodT[:, cc, :], cc == 0, cc == CC - 1,
                       r=[("ring", si)] + XA(16 * K + cc * K, 17 * K + cc * K))
                si = wload(wgo[dch], CC * 128)
                for vc in range(16):
                    mm(byb, ring[si][:, vc * 128:(vc + 1) * 128], ogT[:, vc, :], vc == 0, vc == 15,
                       r=[("ring", si)] + XA(*OGT))
                bm0 = pcol[:, BM0 + dch:BM0 + dch + 1]
                bm1 = pcol[:, BM0 + KC + dch:BM0 + KC + dch + 1]
                P.op("act", lambda e, sa=sa, b=bga, bm0=bm0: e.activation(sa, ps[:, b, :], AF.Sigmoid, bias=bm0),
                     r=[("ps", bga), "pcol"], w=A_sa)
                P.op("act", lambda e, sb_=sb_, b=bgb, bm1=bm1: e.activation(sb_, ps[:, b, :], AF.Sigmoid, bias=bm1),
                     r=[("ps", bgb), "pcol"], w=A_sb)
                P.op("dve", lambda e, m1=m1, sa=sa, b=bya: e.tensor_tensor(m1, sa, ps[:, b, :], ALU.mult),
                     r=A_sa + [("ps", bya)], w=A_m1)
                P.op("dve", lambda e, m2=m2, sb_=sb_, b=byb: e.tensor_tensor(m2, sb_, ps[:, b, :], ALU.mult),
                     r=A_sb + [("ps", byb)], w=A_m2)
                P.op("dve", lambda e, m1=m1, m2=m2, dch=dch: e.tensor_tensor(mergedT[:, dch, :], m1, m2, ALU.add),
                     r=A_m1 + A_m2, w=[("BQ", dch // 2)])
            dbg_dump("mergedT", bq[:, :], BA(0, 32 * K))
            for s in range(NS):
                P.dma("sp", lambda e, s=s: e.dma_start(out=x_sb[:, s, :], in_=xs_d[s]), r=[("xsd", s)], w=XS[s],
                      key=("xrl", s))
            for cg in range(8):
                banks = [next_bank() for _ in range(NS)]
                for kq in range(4):
                    si = wload(wmo[cg, kq], 4096)
                    for k8 in range(8):
                        kc = kq * 8 + k8
                        for s in range(NS):
                            mm(banks[s], mergedT[:, kc, s * 128:(s + 1) * 128], ring[si][:, k8 * 512:(k8 + 1) * 512],
                               kc == 0, kc == KC - 1, r=[("ring", si), ("BQ", kc // 2)])
                for s in range(NS):
                    xs = x_sb[:, s, cg * 512:(cg + 1) * 512]
                    P.op("dve", lambda e, xs=xs, b=banks[s]: e.tensor_tensor(xs, ps[:, b, :], xs, ALU.add),
                         r=[("ps", banks[s]), ("XR", s * 8 + cg)], w=[("XR", s * 8 + cg)])

        for ti in range(n_tiles):
            for s in range(NS):
                r0 = ti * T + s * 128
                P.dma("sp", lambda e, s=s, r0=r0: e.dma_start(out=x_sb[:, s, :], in_=x_in[r0:r0 + 128, :]), w=XS[s],
                      key=("xload", s))
            ffn("ffn1")
            if ti == 0:
                dbg_dump("x1", xr[:, :].bitcast(F32), [a for s in range(NS) for a in XS[s]])
            mixer(ti)
            if ti == 0:
                dbg_dump("x2", xr[:, :].bitcast(F32), [a for s in range(NS) for a in XS[s]])
            ffn("ffn2")
            P.dma("sp", lambda e: e.dma_start(out=gb, in_=g_norm["final"].partition_broadcast(128)),
                  w=BA(0, 16 * K), key=("gbload",))
            norm_stats()
            for s in range(NS):
                rs = stat[:, 4 + s:5 + s]
                norm_recip(s)
                P.op("dve", lambda e, s=s, rs=rs: e.scalar_tensor_tensor(x_sb[:, s, :], x_sb[:, s, :], rs, gb,
                                                                           ALU.mult, ALU.mult),
                     r=XS[s] + BA(0, 16 * K) + [("stat", 4 + s)], w=XS[s])
                r0 = ti * T + s * 128
                P.dma("sp", lambda e, s=s, r0=r0: e.dma_start(out=out_d[r0:r0 + 128, :], in_=x_sb[:, s, :]),
                      r=XS[s], w=[("out", ti, s)], key=("ostore", s), final=True)
        P.build(st)
        nc._prog_stats = (len(P.ops), P.n_sems, P.counts)
    return nc


def _f32(a):
    return np.ascontiguousarray(a, dtype=np.float32)


def prep_weights(inp):
    m = {}
    m["g_ffn1"] = _f32(inp["ffn1_norm_g"][0])
    m["g_mix"] = _f32(inp["mix_norm_g"][0])
    m["g_ffn2"] = _f32(inp["ffn2_norm_g"][0])
    m["g_final"] = _f32(inp["final_norm_g"])

    def fchunks(w):
        C = w.shape[1] // 128
        return _f32(w.reshape(KC, 128, C, 128).transpose(2, 1, 0, 3).reshape(C, 128, KC * 128))

    def rounds(w):
        R = w.shape[0] // 128
        return _f32(w.reshape(R, 128, 8, 512).transpose(2, 0, 1, 3))

    def tgroups(w):
        G = w.shape[1] // 512
        return _f32(w.reshape(4, 8, 128, G, 512).transpose(3, 0, 2, 1, 4).reshape(G, 4, 128, 8 * 512))

    for k in ("ffn1", "ffn2"):
        m[k + "_wg"] = fchunks(inp[k + "_w_gate"][0])
        m[k + "_wu"] = fchunks(inp[k + "_w_up"][0])
        m[k + "_wd"] = rounds(inp[k + "_w_down"][0])
    w_in = inp["w_in"][0]
    nat = fchunks(w_in[:, 0:O_V])
    ga = fchunks(w_in[:, O_GA:O_GA + D])
    gb_ = fchunks(w_in[:, O_GB:O_GB + D])
    m["win_f"] = np.concatenate([nat, ga, gb_], axis=0)
    m["win_t"] = np.concatenate([tgroups(w_in[:, O_V:O_V + DV]), tgroups(w_in[:, O_R:O_R + DV])], axis=0)
    wa = w_in[:, O_A:O_A + RANK]
    m["win_a"] = _f32(wa.reshape(KC, 128, RANK).transpose(1, 0, 2).reshape(128, KC * RANK))

    def ochunks(w):
        return _f32(w.reshape(CC, 128, KC, 128).transpose(2, 1, 0, 3).reshape(KC, 128, CC * 128))

    m["wco"] = ochunks(inp["w_conv_out"][0])
    m["wgo"] = ochunks(inp["w_gla_out"][0])
    wmo = inp["w_mix_out"][0]
    m["wmo"] = _f32(wmo.reshape(4, 8, 128, 8, 512).transpose(3, 0, 2, 1, 4).reshape(8, 4, 128, 8 * 512))
    m["wup"] = _f32(inp["w_alpha_up"][0])
    m["gng"] = _f32(inp["gla_norm_g"][0])
    cw = inp["conv_w"][0]
    pc = np.zeros((128, 3 * CC + CC + 8 + 2 * KC), np.float32)
    pc[:, 0:3 * CC] = cw.reshape(3, CC, 128).transpose(2, 1, 0).reshape(128, 3 * CC)
    pc[:, 3 * CC:4 * CC] = inp["conv_b"][0].reshape(CC, 128).T
    pc[:, 4 * CC:4 * CC + 8] = inp["b_alpha"][0].reshape(8, 128).T
    pc[:, 4 * CC + 8:4 * CC + 8 + KC] = inp["b_merge"][0, 0].reshape(KC, 128).T
    pc[:, 4 * CC + 8 + KC:] = inp["b_merge"][0, 1].reshape(KC, 128).T
    m["pcol"] = pc
    cm = np.zeros((128, 128 * 3 + 512), np.float32)
    cm[:, 0:128] = np.eye(128, dtype=np.float32)
    j = np.arange(128)[:, None]
    i = np.arange(128)[None, :]
    cm[:, 128:256] = (j <= i)
    cm[:, 256:384] = (j > i) & ((j // 64) == (i // 64))
    rm = np.ones(512, np.float32)
    rm[::128] = 0.0
    cm[:, 384:896] = rm[None, :]
    m["cmat"] = cm
    return m


N_CORES = 8
N_TILES = 2


def shard_x(x):
    B = x.shape[0]
    xt = x.reshape(B, SEQ // T, T, D)
    return [_f32(xt[c // 2, (c % 2)::2].reshape(N_TILES * T, D)) for c in range(2 * B)]


def unshard_out(outs, B):
    out = np.empty((B, SEQ // T, T, D), np.float32)
    for c in range(2 * B):
        out[c // 2, (c % 2)::2] = np.asarray(outs[c], dtype=np.float32).reshape(N_TILES, T, D)
    return out.reshape(B, SEQ, D)


def core_inputs(shared, xs, c, nonce):
    mp = dict(shared)
    mp["x"] = xs[c]
    role = np.zeros((128, 2), np.float32)
    role[:, c % 2] = 1.0
    mp["role"] = role
    mp["par"] = np.array([[c % 2]], np.int32)
    mp["nonce"] = np.array([[nonce]], np.int32)
    return mp


def kernel(**inputs):
    inp = {k: np.asarray(v) for k, v in inputs.items()}
    x = inp["x"]
    B = x.shape[0]
    shared = prep_weights(inp)
    nc = build_program(N_TILES, N_TILES)
    xs = shard_x(x)
    nonce = int(np.random.randint(1, 1 << 28)) * 4
    in_maps = [core_inputs(shared, xs, c, nonce) for c in range(N_CORES)]
    res = run_bass_kernel_spmd(nc, in_maps, core_ids=list(range(N_CORES)))
    return unshard_out([res.results[c]["out"] for c in range(N_CORES)], B)
```

```python
import math
from contextlib import ExitStack

import numpy as np
import concourse.bass as bass
import concourse.mybir as mybir
from concourse.bass_utils import run_bass_kernel_spmd

F32 = mybir.dt.float32
BF16 = mybir.dt.bfloat16
I32 = mybir.dt.int32
ALU = mybir.AluOpType
AF = mybir.ActivationFunctionType

D = 4096
KC = D // 128
DFF = 11008
FC = DFF // 128
DCONV = 2048
CC = DCONV // 128
HEADS = 4
DK = 1024
DV = 2048
HK = 256
HV = 512
RANK = 16
EPS = 1e-6
T = 512
NS = 4
SEQ = 2048
O_CB, O_CC, O_CU, O_Q, O_K, O_V, O_R, O_A, O_GA, O_GB = 0, 2048, 4096, 6144, 7168, 8192, 10240, 12288, 12304, 16400
DIN = 20496
NSLOT = 4
FGRP = 16


class _Op:
    __slots__ = ("eng", "emit", "reads", "writes", "dma_key", "deps", "signal", "sigval", "reqs", "custom")

    def __init__(self, eng, emit, reads, writes, dma_key):
        self.eng = eng
        self.emit = emit
        self.reads = reads
        self.writes = writes
        self.dma_key = dma_key
        self.deps = ()
        self.signal = False
        self.sigval = 0
        self.reqs = ()
        self.custom = False


class Prog:
    ENGS = ("pe", "act", "dve", "pool", "sp")

    def __init__(self, nc):
        self.nc = nc
        self.ops = []
        self.last_w = {}
        self.readers = {}
        self.final_dma = []

    def op(self, eng, emit, r=(), w=(), custom=False):
        o = _Op(eng, emit, tuple(r), tuple(w), None)
        o.custom = custom
        self._track(o)
        return o

    def dma(self, queue, emit, r=(), w=(), key=None, final=False, custom=False):
        if key is None:
            assert len(w) == 1
            key = w[0]
        o = _Op(queue, emit, tuple(r), tuple(w), key)
        o.custom = custom
        self._track(o)
        if final:
            self.final_dma.append(o)
        return o

    def _track(self, o):
        idx = len(self.ops)
        deps = set()
        for a in o.reads:
            lw = self.last_w.get(a)
            if lw is not None:
                deps.add(lw)
        for a in o.writes:
            lw = self.last_w.get(a)
            if lw is not None:
                deps.add(lw)
            rl = self.readers.get(a)
            if rl:
                deps.update(rl.values())
        deps.discard(idx)
        o.deps = deps
        for a in o.reads:
            self.readers.setdefault(a, {})[(o.eng, o.dma_key)] = idx
        for a in o.writes:
            self.last_w[a] = idx
            self.readers[a] = {}
        self.ops.append(o)

    def build(self, stack):
        nc = self.nc
        ops = self.ops

        def skip(do, o):
            if do.dma_key is not None:
                return False
            if do.eng == "pe":
                return o.eng == "pe" and o.dma_key is None
            return do.eng in ("sp", "pool") and o.eng == do.eng

        for o in ops:
            for d in o.deps:
                do = ops[d]
                if do.dma_key is None and not skip(do, o):
                    do.signal = True
        cnt = {e: 0 for e in self.ENGS}
        dcnt = {}
        for o in ops:
            if o.dma_key is not None:
                dcnt[o.dma_key] = dcnt.get(o.dma_key, 0) + 1
                o.sigval = 16 * dcnt[o.dma_key]
            elif o.signal:
                cnt[o.eng] += 1
                o.sigval = cnt[o.eng]
        esem = {e: stack.enter_context(nc.semaphore("s_" + e)) for e in ("pe", "act", "dve", "pool")}
        dsem = {}
        for i, k in enumerate(dcnt):
            dsem[k] = stack.enter_context(nc.semaphore("d%d" % i))
        self.n_sems = 4 + len(dsem)
        self.counts = dict(cnt)
        for o in ops:
            req = {}
            for d in o.deps:
                do = ops[d]
                if do.dma_key is not None:
                    s = dsem[do.dma_key]
                    k = ("d", do.dma_key)
                else:
                    if skip(do, o):
                        continue
                    s = esem[do.eng]
                    k = ("e", do.eng)
                if k not in req or req[k][1] < do.sigval:
                    req[k] = (s, do.sigval)
            o.reqs = req
        block = stack.enter_context(nc.Block())
        by_eng = {e: [o for o in ops if o.eng == e] for e in self.ENGS}
        final_dma = self.final_dma

        def run(eng_name, eng):
            known = {}
            for o in by_eng[eng_name]:
                for k, (s, v) in o.reqs.items():
                    if known.get(k, 0) < v:
                        eng.wait_ge(s, v)
                        known[k] = v
                def finish(ins, o=o):
                    if o.dma_key is not None:
                        ins.then_inc(dsem[o.dma_key], 16)
                    elif o.signal:
                        ins.then_inc(esem[eng_name], 1)

                if o.custom:
                    o.emit(eng, finish)
                else:
                    finish(o.emit(eng))
            if eng_name == "sp":
                for o in final_dma:
                    eng.wait_ge(dsem[o.dma_key], o.sigval)

        @block.tensor
        def _(e):
            run("pe", e)

        @block.scalar
        def _(e):
            run("act", e)

        @block.vector
        def _(e):
            run("dve", e)

        @block.gpsimd
        def _(e):
            run("pool", e)

        @block.sync
        def _(e):
            run("sp", e)


def build_program(n_tiles, tiles_per_seq, dbg=None, exchange=True):
    nc = bass.Bass("TRN2", target_bir_lowering=False)
    NTOK = n_tiles * T

    def din(name, shape, dt=F32):
        return nc.dram_tensor(name, list(shape), dt, kind="ExternalInput").ap()

    x_in = din("x", [NTOK, D])
    out_d = nc.dram_tensor("out", [NTOK, D], F32, kind="ExternalOutput").ap()
    xs_d = nc.dram_tensor("xspill", [NS, 128, D], F32).ap()
    g_norm = {k: din("g_" + k, [D]) for k in ("ffn1", "mix", "ffn2", "final")}
    ffn_w = {}
    for k in ("ffn1", "ffn2"):
        ffn_w[k] = (din(k + "_wg", [FC, 128, KC * 128]), din(k + "_wu", [FC, 128, KC * 128]),
                    din(k + "_wd", [8, FC, 128, 512]))
    win_f = din("win_f", [128, 128, KC * 128])
    win_t = din("win_t", [8, 4, 128, 8 * 512])
    win_a = din("win_a", [128, KC * RANK])
    wco = din("wco", [KC, 128, CC * 128])
    wgo = din("wgo", [KC, 128, CC * 128])
    wmo = din("wmo", [8, 4, 128, 8 * 512])
    wup_d = din("wup", [RANK, DK])
    gng_d = din("gng", [HV])
    NPC = 3 * CC + CC + 8 + 2 * KC
    pcol_d = din("pcol", [128, NPC])
    cmat_d = din("cmat", [128, 128 * 3 + 512])
    role_d = din("role", [128, 2])
    par_d = din("par", [1, 1], I32)
    nonce_d = din("nonce", [1, 1], I32)
    NR = max(n_tiles, 1)
    sh_S = nc.dram_tensor("sh_S", [2, NR, 128, HEADS * 2 * HV], F32, addr_space="Shared").ap()
    sh_C = nc.dram_tensor("sh_C", [2, NR, 128, HEADS * 2 * HV], F32, addr_space="Shared").ap()
    sh_D = nc.dram_tensor("sh_D", [2, NR, 128, 8], F32, addr_space="Shared").ap()
    sh_H = nc.dram_tensor("sh_H", [2, NR, 128, 2 * CC], F32, addr_space="Shared").ap()
    sh_F = nc.dram_tensor("sh_F", [2, 8], I32, addr_space="Shared").ap()
    dbg_t = {}
    if dbg:
        for name, (shape, dt_) in dbg.items():
            dbg_t[name] = nc.dram_tensor("dbg_" + name, list(shape), dt_, kind="ExternalOutput").ap()

    with ExitStack() as st:
        P = Prog(nc)
        sb = lambda n, shp, dt: st.enter_context(nc.sbuf_tensor("sb_" + n, shp, dt))
        ring = [sb("ring%d" % i, [128, 4096], BF16) for i in range(NSLOT)]
        hT = sb("hT", [128, KC, T], BF16)
        xr = sb("xr", [128, 32768], BF16)
        bq = sb("bq", [128, 16384], BF16)
        st_f = sb("st_f", [128, HEADS, 2, HV], F32)
        cmat = sb("cmat", [128, 128 * 3 + 512], F32)
        ident = sb("ident", [128, 128], BF16)
        pcol = sb("pcol", [128, NPC], F32)
        negb = sb("negb", [128, 8], F32)
        gng = sb("gng", [128, HV], F32)
        wup = sb("wup", [RANK, DK], F32)
        alT = sb("alT", [RANK, T], F32)
        halo = sb("halo", [128, CC, 2], F32)
        hin = sb("hin", [128, CC, 2], F32)
        hs1 = sb("hs1", [128, CC, 2], F32)
        acc0 = sb("acc0", [128, CC, 2], F32)
        cb0 = sb("cb0", [128, CC, 2], F32)
        fx = sb("fx", [128, 4, CC], F32)
        role = sb("role", [128, 2], F32)
        dloc = sb("dloc", [128, 8], F32)
        coef = sb("coef", [128, 8], F32)
        stat = sb("stat", [128, 16], F32)
        tmpf = [sb("tmpf%d" % i, [128, T], F32) for i in range(2)]
        ps = st.enter_context(nc.psum_tensor("ps", [128, 8, 512], F32))

        x_sb = xr[:, :].bitcast(F32).rearrange("p (s d) -> p s d", s=NS)
        gb = bq[:, 0:8192].bitcast(F32)
        hp = bq[:, 8192:12288]
        hp2 = [bq[:, 8192:12288], bq[:, 12288:16384]]
        actT = bq[:, 0:FGRP * T].rearrange("p (f t) -> p f t", f=FGRP)
        mergedT = bq[:, :].rearrange("p (c t) -> p c t", c=KC)
        maskL = cmat[:, 128:256]
        maskU = cmat[:, 256:384]
        rmask = cmat[:, 384:896]

        def xr_bf(lo, hi):
            return xr[:, lo // 2:hi // 2]

        def xr_f32(lo, hi):
            return xr[:, lo // 2:hi // 2].bitcast(F32)

        def XA(lo, hi):
            return [("XR", p) for p in range(lo // 2048, (hi + 2047) // 2048)]

        def BA(lo, hi):
            return [("BQ", p) for p in range(lo // 2048, (hi + 2047) // 2048)]

        K = 1024
        OGT = (0, 16 * K)
        PRD = (16 * K, 32 * K)
        Q1, Q2, K1, K2 = [(32 * K + i * 2 * K, 34 * K + i * 2 * K) for i in range(4)]
        K1T = (40 * K, 42 * K)
        VH = (42 * K, 46 * K)
        SR = (46 * K, 50 * K)
        OG = (50 * K, 54 * K)
        TA = (54 * K, 56 * K)
        TB = (56 * K, 58 * K)
        CUM = (58 * K, 60 * K)
        SC1 = (60 * K, 60 * K + 512)
        SC2 = (60 * K + 512, 61 * K)
        SCT = (61 * K, 61 * K + 256)
        T1 = (62 * K, 64 * K)
        A_STB = BA(0, 8 * K)
        A_EL = BA(8 * K, 12 * K)
        A_ENL = BA(12 * K, 16 * K)
        ogT = xr_bf(*OGT).rearrange("p (c t) -> p c t", c=16)
        stb = bq[:, 0:4096].rearrange("p (h c v) -> p h c v", h=4, c=2)
        prodT = xr_bf(*PRD).rearrange("p (c t) -> p c t", c=CC)

        state = {"slot": 0, "bank": 0}

        def next_slot():
            i = state["slot"] % NSLOT
            state["slot"] += 1
            return i

        def next_bank():
            i = state["bank"] % 8
            state["bank"] += 1
            return i

        def wload(src_ap, nelem, view=None):
            i = next_slot()
            dst = ring[i][:, 0:nelem]
            if view is not None:
                dst = view(dst)
            P.dma("pool", lambda e, dst=dst, src=src_ap: e.dma_start(out=dst, in_=src), w=[("ring", i)])
            return i

        def mm(bank, lhsT, rhs, start, stop, r, out=None):
            o = ps[:, bank, :] if out is None else out
            P.op("pe", lambda e, o=o, lhsT=lhsT, rhs=rhs, start=start, stop=stop:
                 e.matmul(o, lhsT, rhs, start=start, stop=stop), r=r, w=[("ps", bank)])

        def dbg_dump(name, src_ap, atoms):
            if name in dbg_t:
                P.dma("sp", lambda e, d=dbg_t[name], s=src_ap: e.dma_start(out=d, in_=s), r=atoms,
                      w=[("dbg", name)], final=True)

        P.dma("sp", lambda e: e.dma_start(out=cmat[:], in_=cmat_d), w=["cmat"])
        P.dma("sp", lambda e: e.dma_start(out=pcol[:], in_=pcol_d), w=["pcol"])
        P.dma("sp", lambda e: e.dma_start(out=wup[:], in_=wup_d), w=["wup"])
        P.dma("sp", lambda e: e.dma_start(out=gng[:], in_=gng_d.partition_broadcast(128)), w=["gng"])
        P.op("dve", lambda e: e.tensor_copy(ident[:], cmat[:, 0:128]), r=["cmat"], w=["ident"])
        CW0 = 0
        CB0 = 3 * CC
        BA0 = 4 * CC
        BM0 = 4 * CC + 8
        P.op("dve", lambda e: e.tensor_scalar(negb[:], pcol[:, BA0:BA0 + 8], -1.0, None, ALU.mult),
             r=["pcol"], w=["negb"])

        P.op("dve", lambda e: e.memset(stat[:, 10:11], 1.0), w=[("stat", 10)])
        P.op("dve", lambda e: e.memset(stat[:, 11:12], EPS), w=[("stat", 11)])

        def rstd_ops(ss, rs, n, a_ss, a_rs):
            P.op("act", lambda e: e.activation(rs, ss, AF.Sqrt, bias=stat[:, 11:12], scale=1.0 / n),
                 r=[a_ss, ("stat", 11)], w=[a_rs])
            P.op("dve", lambda e: e.reciprocal(rs, rs), r=[a_rs], w=[a_rs])

        XS = [[("XR", s * 8 + c) for c in range(8)] for s in range(NS)]

        junk = hT[:, 24:32, :].rearrange("p a b -> p (a b)")
        A_junk = [("hT", kc, s) for kc in range(24, 32) for s in range(NS)]

        def norm_stats():
            for s in range(NS):
                ss = stat[:, s:s + 1]
                rs = stat[:, 4 + s:5 + s]
                P.op("dve", lambda e, ss=ss: e.memset(ss, 0.0), w=[("stat", s)])
                P.op("act", lambda e, s=s, ss=ss: e.activation(junk, x_sb[:, s, :], AF.Square, accum_out=ss),
                     r=XS[s], w=A_junk + [("stat", s)])
                P.op("act", lambda e, ss=ss, rs=rs: e.activation(rs, ss, AF.Sqrt, bias=stat[:, 11:12], scale=1.0 / D),
                     r=[("stat", s), ("stat", 11)], w=[("stat", 4 + s)])

        def norm_recip(s):
            rs = stat[:, 4 + s:5 + s]
            P.op("dve", lambda e, rs=rs: e.reciprocal(rs, rs), r=[("stat", 4 + s)], w=[("stat", 4 + s)])

        def norm_to_hT(gkey):
            P.dma("sp", lambda e: e.dma_start(out=gb, in_=g_norm[gkey].partition_broadcast(128)),
                  w=BA(0, 16 * K), key=("gbload",))
            norm_stats()

            def scale(s):
                rs = stat[:, 4 + s:5 + s]
                hp = hp2[s % 2]
                A_hp = BA(16 * K + (s % 2) * 8 * K, 24 * K + (s % 2) * 8 * K)
                norm_recip(s)
                P.op("dve", lambda e, s=s, rs=rs, hp=hp: e.scalar_tensor_tensor(hp, x_sb[:, s, :], rs, gb, ALU.mult, ALU.mult),
                     r=XS[s] + BA(0, 16 * K) + [("stat", 4 + s)], w=A_hp)

            def transp(s):
                hp = hp2[s % 2]
                A_hp = BA(16 * K + (s % 2) * 8 * K, 24 * K + (s % 2) * 8 * K)
                for k4 in range(KC // 4):
                    b = next_bank()
                    pb = ps[:, b, :].bitcast(BF16)
                    for j in range(4):
                        kc = k4 * 4 + j
                        P.op("pe", lambda e, pb=pb, j=j, kc=kc, hp=hp: e.transpose(pb[:, j * 128:(j + 1) * 128],
                                                                            hp[:, kc * 128:(kc + 1) * 128], ident[:]),
                             r=A_hp + ["ident"], w=[("ps", b)])
                    dst = hT[:, k4 * 4:k4 * 4 + 4, s * 128:(s + 1) * 128]
                    src = pb[:, 0:512].rearrange("p (a b) -> p a b", a=4)
                    wat = [("hT", kc, s) for kc in range(k4 * 4, k4 * 4 + 4)]
                    if k4 % 2 == 0:
                        P.op("act", lambda e, dst=dst, src=src: e.copy(dst, src), r=[("ps", b)], w=wat)
                    else:
                        P.op("dve", lambda e, dst=dst, src=src: e.tensor_copy(dst, src), r=[("ps", b)], w=wat)

            scale(0)
            scale(1)
            transp(0)
            scale(2)
            transp(1)
            scale(3)
            transp(2)
            transp(3)

        def hT_atoms(kc):
            return [("hT", kc, s) for s in range(NS)]

        def ffn(key):
            wg_d, wu_d, wd_d = ffn_w[key]
            norm_to_hT(key)
            groups = [(f0, min(FGRP, FC - f0)) for f0 in range(0, FC, FGRP)]
            for (f0, nf) in groups:
                for j in range(nf):
                    fc = f0 + j
                    sg_i = wload(wg_d[fc], 4096)
                    su_i = wload(wu_d[fc], 4096)
                    bg, bu = next_bank(), next_bank()
                    for kc in range(KC):
                        mm(bg, ring[sg_i][:, kc * 128:(kc + 1) * 128], hT[:, kc, :], kc == 0, kc == KC - 1,
                           r=[("ring", sg_i)] + hT_atoms(kc))
                    for kc in range(KC):
                        mm(bu, ring[su_i][:, kc * 128:(kc + 1) * 128], hT[:, kc, :], kc == 0, kc == KC - 1,
                           r=[("ring", su_i)] + hT_atoms(kc))
                    tf = tmpf[fc % 2]
                    P.op("act", lambda e, tf=tf, bg=bg: e.activation(tf[:], ps[:, bg, :], AF.Silu),
                         r=[("ps", bg)], w=[("tmpf", fc % 2)])
                    P.op("dve", lambda e, tf=tf, bu=bu, j=j: e.tensor_tensor(actT[:, j, :], tf[:], ps[:, bu, :], ALU.mult),
                         r=[("tmpf", fc % 2), ("ps", bu)], w=[("BQ", j // 2)])
                for cg in range(8):
                    banks = [next_bank() for _ in range(NS)]
                    for j0 in range(0, nf, 8):
                        n = min(8, nf - j0)
                        si = wload(wd_d[cg, f0 + j0:f0 + j0 + n].rearrange("f p c -> p f c"), n * 512,
                                   view=lambda a, n=n: a.rearrange("p (f c) -> p f c", f=n))
                        for jj in range(n):
                            j = j0 + jj
                            for s in range(NS):
                                mm(banks[s], actT[:, j, s * 128:(s + 1) * 128], ring[si][:, jj * 512:(jj + 1) * 512],
                                   j == 0, j == nf - 1, r=[("ring", si), ("BQ", j // 2)])
                    for s in range(NS):
                        xs = x_sb[:, s, cg * 512:(cg + 1) * 512]
                        P.op("dve", lambda e, xs=xs, b=banks[s]: e.scalar_tensor_tensor(xs, ps[:, b, :], 0.5, xs,
                                                                                         ALU.mult, ALU.add),
                             r=[("ps", banks[s]), ("XR", s * 8 + cg)], w=[("XR", s * 8 + cg)])

        def fchunk(col_chunk, bank):
            si = wload(win_f[col_chunk], 4096)
            for kc in range(KC):
                mm(bank, ring[si][:, kc * 128:(kc + 1) * 128], hT[:, kc, :], kc == 0, kc == KC - 1,
                   r=[("ring", si)] + hT_atoms(kc))

        def tgroup(grp):
            banks = [next_bank() for _ in range(NS)]
            for kq in range(4):
                si = wload(win_t[grp, kq], 4096)
                for k8 in range(8):
                    kc = kq * 8 + k8
                    for s in range(NS):
                        mm(banks[s], hT[:, kc, s * 128:(s + 1) * 128], ring[si][:, k8 * 512:(k8 + 1) * 512],
                           kc == 0, kc == KC - 1, r=[("ring", si), ("hT", kc, s)])
            return banks

        regs = {}

        def sp_init(e, finish):
            regs["par"] = e.alloc_register("r_par")
            regs["nonce"] = e.alloc_register("r_nonce")
            regs["fv"] = e.alloc_register("r_fv")
            regs["pr"] = e.alloc_register("r_pr")
            e.reg_load(regs["par"], par_d[0:1, 0:1])
            e.reg_load(regs["nonce"], nonce_d[0:1, 0:1])

        P.op("sp", sp_init, custom=True)
        P.dma("sp", lambda e: e.dma_start(out=role[:], in_=role_d), w=["role"])
        isA = role[:, 0:1]
        isB = role[:, 1:2]

        def own_slot_dma(dst_of_slot, src, r, w, key):
            def emit(e, finish):
                with e.If_eq(regs["par"], 0):
                    finish(e.dma_start(out=dst_of_slot(0), in_=src))
                with e.Else():
                    finish(e.dma_start(out=dst_of_slot(1), in_=src))
            P.dma("sp", emit, r=r, w=w, key=key, custom=True)

        def mixer(ti):
            first = (ti % tiles_per_seq) == 0
            norm_to_hT("mix")
            for s in range(NS):
                P.dma("sp", lambda e, s=s: e.dma_start(out=xs_d[s], in_=x_sb[:, s, :]), r=XS[s], w=[("xsd", s)])
            si = wload(win_a, KC * RANK)
            ba = next_bank()
            for kc in range(KC):
                mm(ba, ring[si][:, kc * RANK:(kc + 1) * RANK], hT[:, kc, :], kc == 0, kc == KC - 1,
                   r=[("ring", si)] + hT_atoms(kc), out=ps[0:RANK, ba, :])
            P.op("act", lambda e: e.copy(alT[:], ps[0:RANK, ba, :]), r=[("ps", ba)], w=["alT"])

            TCC = (40 * K, 42 * K)
            CCU = (42 * K, 44 * K + 16)
            ACC = (46 * K, 48 * K)
            tcc = xr_f32(*TCC)
            ccu = xr_f32(42 * K, 44 * K + 8)
            acc = xr_f32(*ACC)
            for c in range(CC):
                b_cc, b_cu, b_cb = next_bank(), next_bank(), next_bank()
                fchunk(O_CC // 128 + c, b_cc)
                fchunk(O_CU // 128 + c, b_cu)
                fchunk(O_CB // 128 + c, b_cb)
                P.op("act", lambda e, b=b_cc: e.copy(tcc, ps[:, b, :]), r=[("ps", b_cc)], w=XA(*TCC))
                P.op("dve", lambda e: e.memset(ccu[:, 0:2], 0.0), w=XA(*CCU))
                P.op("dve", lambda e, b=b_cu: e.tensor_tensor(ccu[:, 2:514], tcc, ps[:, b, :], ALU.mult),
                     r=[("ps", b_cu)] + XA(*TCC), w=XA(*CCU))
                P.op("dve", lambda e, c=c: e.tensor_copy(halo[:, c, :], ccu[:, 512:514]), r=XA(*CCU), w=["halo"])
                w0 = pcol[:, CW0 + c * 3 + 0:CW0 + c * 3 + 1]
                w1 = pcol[:, CW0 + c * 3 + 1:CW0 + c * 3 + 2]
                w2 = pcol[:, CW0 + c * 3 + 2:CW0 + c * 3 + 3]
                cb_ = pcol[:, CB0 + c:CB0 + c + 1]
                P.op("dve", lambda e, w2=w2, cb_=cb_: e.tensor_scalar(acc, ccu[:, 2:514], w2, cb_, ALU.mult, ALU.add),
                     r=XA(*CCU) + ["pcol"], w=XA(*ACC))
                P.op("dve", lambda e, w1=w1: e.scalar_tensor_tensor(acc, ccu[:, 1:513], w1, acc, ALU.mult, ALU.add),
                     r=XA(*CCU) + XA(*ACC) + ["pcol"], w=XA(*ACC))
                P.op("dve", lambda e, w0=w0: e.scalar_tensor_tensor(acc, ccu[:, 0:512], w0, acc, ALU.mult, ALU.add),
                     r=XA(*CCU) + XA(*ACC) + ["pcol"], w=XA(*ACC))
                P.op("dve", lambda e, c=c: e.tensor_copy(acc0[:, c, :], acc[:, 0:2]), r=XA(*ACC), w=["acc0"])
                P.op("dve", lambda e, c=c, b=b_cb: e.tensor_copy(cb0[:, c, :], ps[:, b, 0:2]), r=[("ps", b_cb)], w=["cb0"])
                P.op("dve", lambda e, c=c, b=b_cb: e.tensor_tensor(prodT[:, c, :], acc, ps[:, b, :], ALU.mult),
                     r=XA(*ACC) + [("ps", b_cb)], w=XA(16 * K + c * K, 17 * K + c * K))

            q1 = xr_bf(*Q1).rearrange("p (c t) -> p c t", c=2)
            q2 = xr_bf(*Q2).rearrange("p (c t) -> p c t", c=2)
            k1 = xr_bf(*K1).rearrange("p (c t) -> p c t", c=2)
            k2 = xr_bf(*K2).rearrange("p (c t) -> p c t", c=2)
            k1t = xr_bf(*K1T).rearrange("p (b d) -> p b d", b=4)
            vh_all = bq[:, 8192:16384].rearrange("p (h s v) -> p h s v", h=HEADS, s=NS)
            sr = xr_bf(*SR).rearrange("p (s v) -> p s v", s=NS)
            og = xr_bf(*OG).rearrange("p (s v) -> p s v", s=NS)
            eL = bq[:, 4096:6144].bitcast(F32).rearrange("p (c t) -> p c t", c=2)
            eNL = bq[:, 6144:8192].bitcast(F32).rearrange("p (c t) -> p c t", c=2)
            tA = xr_f32(*TA)
            tB = xr_f32(*TB)
            cum = xr_f32(*CUM)
            sc1s = [xr_f32(42 * K + b * 512, 42 * K + (b + 1) * 512) for b in range(4)]
            sc2s = [xr_f32(44 * K + b * 512, 44 * K + (b + 1) * 512) for b in range(4)]
            scTs = [xr_bf(60 * K + b * 256, 60 * K + (b + 1) * 256) for b in range(4)]
            A_S1, A_S2, A_ST = XA(42 * K, 44 * K), XA(44 * K, 46 * K), XA(60 * K, 61 * K)
            t1 = xr_f32(*T1)

            def head_prep(h, full):
                for dc in range(2):
                    c = 2 * h + dc
                    bz = next_bank()
                    P.op("pe", lambda e, bz=bz, c=c: e.matmul(ps[:, bz, :], wup[:, c * 128:(c + 1) * 128], alT[:],
                                                               start=True, stop=True),
                         r=["wup", "alT"], w=[("ps", bz)])
                    P.op("act", lambda e, bz=bz, c=c: e.activation(tA, ps[:, bz, :], AF.Exp, bias=negb[:, c:c + 1], scale=-1.0),
                         r=[("ps", bz), "negb"], w=XA(*TA))
                    P.op("act", lambda e: e.activation(tB, tA, AF.Ln, bias=stat[:, 10:11], scale=1.0),
                         r=XA(*TA) + [("stat", 10)], w=XA(*TB))
                    P.op("dve", lambda e: e.tensor_tensor_scan(cum, rmask, tB, 0.0, ALU.mult, ALU.add),
                         r=XA(*TB) + ["cmat"], w=XA(*CUM))
                    P.op("act", lambda e, dc=dc: e.activation(eL[:, dc, :], cum, AF.Exp, scale=-1.0 / 16.0),
                         r=XA(*CUM), w=A_EL)
                    P.op("act", lambda e, dc=dc: e.activation(eNL[:, dc, :], cum, AF.Exp, scale=1.0 / 16.0),
                         r=XA(*CUM), w=A_ENL)
                    if full:
                        bqk = next_bank()
                        fchunk(O_Q // 128 + c, bqk)
                        P.op("dve", lambda e, b=bqk, dc=dc: e.scalar_tensor_tensor(q1[:, dc, :], ps[:, b, :], HK ** -0.5,
                                                                                     eL[:, dc, :], ALU.mult, ALU.mult),
                             r=[("ps", bqk)] + A_EL, w=XA(*Q1))
                        P.op("dve", lambda e, b=bqk, dc=dc: e.scalar_tensor_tensor(q2[:, dc, :], ps[:, b, :], HK ** -0.5,
                                                                                     eNL[:, dc, :], ALU.mult, ALU.mult),
                             r=[("ps", bqk)] + A_ENL, w=XA(*Q2))
                    bkk = next_bank()
                    fchunk(O_K // 128 + c, bkk)
                    P.op("dve", lambda e, b=bkk, dc=dc: e.tensor_tensor(k1[:, dc, :], ps[:, b, :], eNL[:, dc, :], ALU.mult),
                         r=[("ps", bkk)] + A_ENL, w=XA(*K1))
                    if full:
                        P.op("dve", lambda e, b=bkk, dc=dc: e.tensor_tensor(k2[:, dc, :], ps[:, b, :], eL[:, dc, :], ALU.mult),
                             r=[("ps", bkk)] + A_EL, w=XA(*K2))
                    else:
                        dl = dloc[:, c:c + 1]
                        P.op("dve", lambda e, dl=dl, dc=dc: e.tensor_copy(dl, eL[:, dc, 127:128]), r=A_EL, w=["dloc"])
                        for blk in range(1, 4):
                            P.op("dve", lambda e, dl=dl, dc=dc, blk=blk: e.tensor_tensor(
                                dl, dl, eL[:, dc, blk * 128 + 127:blk * 128 + 128], ALU.mult), r=A_EL + ["dloc"], w=["dloc"])
                for blk in range(4):
                    b = next_bank()
                    pb = ps[:, b, :].bitcast(BF16)
                    for dc in range(2):
                        P.op("pe", lambda e, pb=pb, dc=dc, blk=blk: e.transpose(pb[:, dc * 128:(dc + 1) * 128],
                                                                                 k1[:, dc, blk * 128:(blk + 1) * 128], ident[:]),
                             r=XA(*K1) + ["ident"], w=[("ps", b)])
                    P.op("act", lambda e, pb=pb, blk=blk: e.copy(k1t[:, blk, :], pb[:, 0:256]), r=[("ps", b)], w=XA(*K1T))
                vh = vh_all[:, h]
                A_VH = BA(16 * K + h * 4 * K, 16 * K + (h + 1) * 4 * K)
                if (not full) or (not exchange):
                    banks = tgroup(h)
                    for s in range(NS):
                        P.op("act", lambda e, s=s, b=banks[s]: e.copy(vh[:, s, :], ps[:, b, :]), r=[("ps", banks[s])], w=A_VH)
                if full:
                    banks = tgroup(4 + h)
                    for s in range(NS):
                        P.op("act", lambda e, s=s, b=banks[s]: e.activation(sr[:, s, :], ps[:, b, :], AF.Silu),
                             r=[("ps", banks[s])], w=XA(*SR))

            def state_update(h, blk, bs, full):
                for dc in range(2):
                    sl = eL[:, dc, blk * 128 + 127:blk * 128 + 128]
                    sf = st_f[:, h, dc, :]
                    P.op("dve", lambda e, b=bs[dc], sl=sl: e.tensor_scalar(tA, ps[:, b, :], sl, None, ALU.mult),
                         r=[("ps", bs[dc])] + A_EL, w=XA(*TA))
                    P.op("dve", lambda e, sf=sf, sl=sl: e.scalar_tensor_tensor(sf, sf, sl, tA, ALU.mult, ALU.add),
                         r=XA(*TA) + A_EL + ["st_f"], w=["st_f"])
                    if full and blk < 3:
                        P.op("act", lambda e, sf=sf, dc=dc, blk=blk: e.copy(stb[:, blk + 1, dc, :], sf),
                             r=["st_f"], w=[("BQ", blk + 1)])

            def head_blocks(h, full):
                vh = vh_all[:, h]
                A_VH = BA(16 * K + h * 4 * K, 16 * K + (h + 1) * 4 * K)
                if not full:
                    for blk in range(4):
                        bs = [next_bank(), next_bank()]
                        for dc in range(2):
                            mm(bs[dc], k1t[:, blk, dc * 128:(dc + 1) * 128], vh[:, blk, :], True, True, r=XA(*K1T) + A_VH)
                        state_update(h, blk, bs, False)
                    return
                P.op("act", lambda e, h=h: e.copy(stb[:, 0], st_f[:, h]), r=["st_f"], w=[("BQ", 0)])
                for blk in range(4):
                    tk = slice(blk * 128, (blk + 1) * 128)
                    s1, s2, sT = sc1s[blk], sc2s[blk], scTs[blk]
                    b1 = next_bank()
                    for dc in range(2):
                        mm(b1, k1[:, dc, tk], q1[:, dc, tk], dc == 0, dc == 1, r=XA(*K1) + XA(*Q1), out=ps[:, b1, 0:128])
                    b2 = next_bank()
                    for dc in range(2):
                        mm(b2, k2[:, dc, tk], q2[:, dc, tk], dc == 0, dc == 1, r=XA(*K2) + XA(*Q2), out=ps[:, b2, 0:128])
                    P.op("dve", lambda e, b1=b1, s1=s1: e.tensor_tensor(s1, ps[:, b1, 0:128], maskL, ALU.mult),
                         r=[("ps", b1), "cmat"], w=A_S1)
                    P.op("dve", lambda e, b2=b2, s2=s2: e.tensor_tensor(s2, ps[:, b2, 0:128], maskU, ALU.mult),
                         r=[("ps", b2), "cmat"], w=A_S2)
                    P.op("dve", lambda e, s1=s1, s2=s2, sT=sT: e.tensor_tensor(sT, s1, s2, ALU.add), r=A_S1 + A_S2, w=A_ST)
                for blk in range(4):
                    bs = [next_bank(), next_bank()]
                    for dc in range(2):
                        mm(bs[dc], k1t[:, blk, dc * 128:(dc + 1) * 128], vh[:, blk, :], True, True, r=XA(*K1T) + A_VH)
                    state_update(h, blk, bs, True)
                for blk in range(4):
                    tk = slice(blk * 128, (blk + 1) * 128)
                    bo = next_bank()
                    mm(bo, scTs[blk], vh[:, blk, :], True, False, r=A_ST + A_VH)
                    for dc in range(2):
                        mm(bo, q1[:, dc, tk], stb[:, blk, dc, :], False, dc == 1, r=XA(*Q1) + [("BQ", blk)])
                    ss = stat[:, 8:9]
                    rs = stat[:, 9:10]
                    P.op("dve", lambda e, ss=ss: e.memset(ss, 0.0), w=[("stat", 8)])
                    P.op("act", lambda e, bo=bo, ss=ss: e.activation(t1, ps[:, bo, :], AF.Square, accum_out=ss),
                         r=[("ps", bo)], w=XA(*T1) + [("stat", 8)])
                    rstd_ops(ss, rs, HV, ("stat", 8), ("stat", 9))
                    P.op("dve", lambda e, bo=bo, rs=rs: e.scalar_tensor_tensor(t1, ps[:, bo, :], rs, gng[:], ALU.mult, ALU.mult),
                         r=[("ps", bo), ("stat", 9), "gng"], w=XA(*T1))
                    P.op("dve", lambda e, blk=blk: e.tensor_tensor(og[:, blk, :], t1, sr[:, blk, :], ALU.mult),
                         r=XA(*T1) + XA(*SR), w=XA(*OG))

            def head_ogT(h):
                for vc in range(4):
                    b = next_bank()
                    pb = ps[:, b, :].bitcast(BF16)
                    for blk in range(4):
                        P.op("pe", lambda e, pb=pb, blk=blk, vc=vc: e.transpose(pb[:, blk * 128:(blk + 1) * 128],
                                                                                 og[:, blk, vc * 128:(vc + 1) * 128], ident[:]),
                             r=XA(*OG) + ["ident"], w=[("ps", b)])
                    P.op("act", lambda e, pb=pb, h=h, vc=vc: e.copy(ogT[:, h * 4 + vc, :], pb[:, 0:512]),
                         r=[("ps", b)], w=XA(*OGT))

            st_flat = st_f[:].rearrange("p h c v -> p (h c v)")
            halo_flat = halo[:].rearrange("p c t -> p (c t)")
            if exchange:
                P.op("dve", lambda e: e.memset(st_f[:], 0.0), w=["st_f"])
                for h in range(HEADS):
                    head_prep(h, False)
                    head_blocks(h, False)
                own_slot_dma(lambda sl: sh_S[sl, ti], st_flat, r=["st_f"], w=[("shS", ti)], key=("shS",))
                own_slot_dma(lambda sl: sh_D[sl, ti], dloc[:], r=["dloc"], w=[("shD", ti)], key=("shD",))
                own_slot_dma(lambda sl: sh_H[sl, ti], halo_flat, r=["halo"], w=[("shH", ti)], key=("shH",))

                def handshake(e, finish, ti=ti, first=first):
                    fv, pr = regs["fv"], regs["pr"]
                    e.reg_add(fv, regs["nonce"], ti + 1)
                    with e.If_eq(regs["par"], 0):
                        e.reg_save(sh_F[0:1, ti:ti + 1], fv)
                    with e.Else():
                        e.reg_save(sh_F[1:2, ti:ti + 1], fv)
                    polls = [sh_F[0:1, ti:ti + 1]]
                    if not first:
                        polls.append(sh_F[1:2, 4 + ti - 1:4 + ti])
                    for k, fl in enumerate(polls):
                        if k == 1:
                            e.reg_add(fv, regs["nonce"], ti)
                        e.reg_mov(pr, 1)
                        with e.While(pr):
                            e.reg_load(pr, fl)
                            e.reg_sub(pr, pr, fv)

                deps = [("shS", ti), ("shD", ti), ("shH", ti)] + ([("shC", ti - 1)] if not first else [])
                P.op("sp", handshake, r=deps, w=[("hs", ti)], custom=True)
                head_prep(0, True)
                P.dma("sp", lambda e: e.dma_start(out=dloc[:], in_=sh_D[0, ti]), r=[("hs", ti), ("shD", ti)], w=["dloc"],
                      key=("ldD",))
                P.op("dve", lambda e: e.tensor_scalar(coef[:], dloc[:], isB, isA, ALU.mult, ALU.add),
                     r=["dloc", "role"], w=["coef"])
                P.dma("sp", lambda e: e.dma_start(out=hin[:].rearrange("p c t -> p (c t)"), in_=sh_H[0, ti]),
                      r=[("hs", ti), ("shH", ti)], w=["hin"], key=("ldH",))
                P.op("dve", lambda e: e.tensor_scalar(hin[:], hin[:], isB, None, ALU.mult), r=["hin", "role"], w=["hin"])
                if not first:
                    P.dma("sp", lambda e: e.dma_start(out=hs1[:].rearrange("p c t -> p (c t)"), in_=sh_H[1, ti - 1]),
                          r=[("hs", ti)], w=["hs1"], key=("ldH1",))
                    P.op("dve", lambda e: e.scalar_tensor_tensor(hin[:], hs1[:], isA, hin[:], ALU.mult, ALU.add),
                         r=["hin", "hs1", "role"], w=["hin"])
                for b8 in range(8):
                    tS = tmpf[b8 % 2]
                    tC = xr_f32(50 * K + (b8 % 2) * 2 * K, 52 * K + (b8 % 2) * 2 * K)
                    aC = XA(50 * K + (b8 % 2) * 2 * K, 52 * K + (b8 % 2) * 2 * K)
                    cs = slice(b8 * HV, (b8 + 1) * HV)
                    P.dma("sp", lambda e, tS=tS, cs=cs: e.dma_start(out=tS[:], in_=sh_S[0, ti][:, cs]),
                          r=[("hs", ti), ("shS", ti)], w=[("tmpf", b8 % 2)], key=("ldS", b8 % 2))
                    if first:
                        P.op("dve", lambda e, tS=tS, cs=cs: e.tensor_scalar(st_flat[:, cs], tS[:], isB, None, ALU.mult),
                             r=[("tmpf", b8 % 2), "role"], w=["st_f"])
                    else:
                        P.op("dve", lambda e, tS=tS: e.tensor_scalar(tS[:], tS[:], isB, None, ALU.mult),
                             r=[("tmpf", b8 % 2), "role"], w=[("tmpf", b8 % 2)])
                        P.dma("sp", lambda e, tC=tC, cs=cs: e.dma_start(out=tC, in_=sh_C[1, ti - 1][:, cs]),
                              r=[("hs", ti)], w=aC, key=("ldC", b8 % 2))
                        P.op("dve", lambda e, tS=tS, tC=tC, cs=cs, b8=b8: e.scalar_tensor_tensor(
                            st_flat[:, cs], tC, coef[:, b8:b8 + 1], tS[:], ALU.mult, ALU.add),
                            r=aC + [("tmpf", b8 % 2), "coef"], w=["st_f"])
                pw = pcol[:, CW0:CW0 + 3 * CC].rearrange("p (c t) -> p c t", t=3)
                P.op("dve", lambda e: e.tensor_tensor(fx[:, 0, :], pw[:, :, 1], hin[:, :, 1], ALU.mult), r=["pcol", "hin"], w=["fx"])
                P.op("dve", lambda e: e.tensor_tensor(fx[:, 1, :], pw[:, :, 0], hin[:, :, 0], ALU.mult), r=["pcol", "hin"], w=["fx"])
                P.op("dve", lambda e: e.tensor_tensor(fx[:, 0, :], fx[:, 0, :], fx[:, 1, :], ALU.add), r=["fx"], w=["fx"])
                P.op("dve", lambda e: e.tensor_tensor(fx[:, 1, :], pw[:, :, 0], hin[:, :, 1], ALU.mult), r=["pcol", "hin", "fx"], w=["fx"])
                for t in range(2):
                    P.op("dve", lambda e, t=t: e.tensor_tensor(fx[:, 2 + t, :], fx[:, t, :], acc0[:, :, t], ALU.add),
                         r=["fx", "acc0"], w=["fx"])
                    P.op("dve", lambda e, t=t: e.tensor_tensor(prodT[:, :, t], fx[:, 2 + t, :], cb0[:, :, t], ALU.mult),
                         r=["fx", "cb0"], w=XA(*PRD))
            else:
                if first:
                    P.op("dve", lambda e: e.memset(st_f[:], 0.0), w=["st_f"])
                head_prep(0, True)
            dbg_dump("prodT", xr_bf(*PRD), XA(*PRD))
            dbg_dump("stin", st_flat, ["st_f"])
            head_blocks(0, True)
            for h in range(1, HEADS):
                head_prep(h, True)
                head_ogT(h - 1)
                head_blocks(h, True)
            head_ogT(HEADS - 1)
            if exchange and ti + 1 < n_tiles:
                own_slot_dma(lambda sl: sh_C[sl, ti], st_flat, r=["st_f"], w=[("shC", ti)], key=("shC",))

                def carry_flag(e, finish, ti=ti):
                    fv = regs["fv"]
                    e.reg_add(fv, regs["nonce"], ti + 1)
                    with e.If_eq(regs["par"], 0):
                        e.reg_save(sh_F[0:1, 4 + ti:5 + ti], fv)
                    with e.Else():
                        e.reg_save(sh_F[1:2, 4 + ti:5 + ti], fv)

                P.op("sp", carry_flag, r=[("shC", ti)], w=[("cf", ti)], custom=True)
            dbg_dump("ogT", xr_bf(*OGT), XA(*OGT))
            for dch in range(KC):
                o = 40 * K + (dch % 2) * 8 * K
                sa = xr_f32(o, o + 2 * K)
                sb_ = xr_f32(o + 2 * K, o + 4 * K)
                m1 = xr_f32(o + 4 * K, o + 6 * K)
                m2 = xr_f32(o + 6 * K, o + 8 * K)
                A_sa, A_sb, A_m1, A_m2 = [XA(o + i * 2 * K, o + (i + 1) * 2 * K) for i in range(4)]
                bga, bgb, bya, byb = next_bank(), next_bank(), next_bank(), next_bank()
                fchunk(64 + dch, bga)
                fchunk(96 + dch, bgb)
                si = wload(wco[dch], CC * 128)
                for cc in range(CC):
                    mm(bya, ring[si][:, cc * 128:(cc + 1) * 128], prodT[:, cc, :], cc == 0, cc == CC - 1,
                       r=[("ring", si)] + XA(16 * K + cc * K, 17 * K + cc * K))
                si = wload(wgo[dch], CC * 128)
                for vc in range(16):
                    mm(byb, ring[si][:, vc * 128:(vc + 1) * 128], ogT[:, vc, :], vc == 0, vc == 15,
                       r=[("ring", si)] + XA(*OGT))
                bm0 = pcol[:, BM0 + dch:BM0 + dch + 1]
                bm1 = pcol[:, BM0 + KC + dch:BM0 + KC + dch + 1]
                P.op("act", lambda e, sa=sa, b=bga, bm0=bm0: e.activation(sa, ps[:, b, :], AF.Sigmoid, bias=bm0),
                     r=[("ps", bga), "pcol"], w=A_sa)
                P.op("act", lambda e, sb_=sb_, b=bgb, bm1=bm1: e.activation(sb_, ps[:, b, :], AF.Sigmoid, bias=bm1),
                     r=[("ps", bgb), "pcol"], w=A_sb)
                P.op("dve", lambda e, m1=m1, sa=sa, b=bya: e.tensor_tensor(m1, sa, ps[:, b, :], ALU.mult),
                     r=A_sa + [("ps", bya)], w=A_m1)
                P.op("dve", lambda e, m2=m2, sb_=sb_, b=byb: e.tensor_tensor(m2, sb_, ps[:, b, :], ALU.mult),
                     r=A_sb + [("ps", byb)], w=A_m2)
                P.op("dve", lambda e, m1=m1, m2=m2, dch=dch: e.tensor_tensor(mergedT[:, dch, :], m1, m2, ALU.add),
                     r=A_m1 + A_m2, w=[("BQ", dch // 2)])
            dbg_dump("mergedT", bq[:, :], BA(0, 32 * K))
            for s in range(NS):
                P.dma("sp", lambda e, s=s: e.dma_start(out=x_sb[:, s, :], in_=xs_d[s]), r=[("xsd", s)], w=XS[s],
                      key=("xrl", s))
            for cg in range(8):
                banks = [next_bank() for _ in range(NS)]
                for kq in range(4):
                    si = wload(wmo[cg, kq], 4096)
                    for k8 in range(8):
                        kc = kq * 8 + k8
                        for s in range(NS):
                            mm(banks[s], mergedT[:, kc, s * 128:(s + 1) * 128], ring[si][:, k8 * 512:(k8 + 1) * 512],
                               kc == 0, kc == KC - 1, r=[("ring", si), ("BQ", kc // 2)])
                for s in range(NS):
                    xs = x_sb[:, s, cg * 512:(cg + 1) * 512]
                    P.op("dve", lambda e, xs=xs, b=banks[s]: e.tensor_tensor(xs, ps[:, b, :], xs, ALU.add),
                         r=[("ps", banks[s]), ("XR", s * 8 + cg)], w=[("XR", s * 8 + cg)])

        for ti in range(n_tiles):
            for s in range(NS):
                r0 = ti * T + s * 128
                P.dma("sp", lambda e, s=s, r0=r0: e.dma_start(out=x_sb[:, s, :], in_=x_in[r0:r0 + 128, :]), w=XS[s],
                      key=("xload", s))
            ffn("ffn1")
            if ti == 0:
                dbg_dump("x1", xr[:, :].bitcast(F32), [a for s in range(NS) for a in XS[s]])
            mixer(ti)
            if ti == 0:
                dbg_dump("x2", xr[:, :].bitcast(F32), [a for s in range(NS) for a in XS[s]])
            ffn("ffn2")
            P.dma("sp", lambda e: e.dma_start(out=gb, in_=g_norm["final"].partition_broadcast(128)),
                  w=BA(0, 16 * K), key=("gbload",))
            norm_stats()
            for s in range(NS):
                rs = stat[:, 4 + s:5 + s]
                norm_recip(s)
                P.op("dve", lambda e, s=s, rs=rs: e.scalar_tensor_tensor(x_sb[:, s, :], x_sb[:, s, :], rs, gb,
                                                                           ALU.mult, ALU.mult),
                     r=XS[s] + BA(0, 16 * K) + [("stat", 4 + s)], w=XS[s])
                r0 = ti * T + s * 128
                P.dma("sp", lambda e, s=s, r0=r0: e.dma_start(out=out_d[r0:r0 + 128, :], in_=x_sb[:, s, :]),
                      r=XS[s], w=[("out", ti, s)], key=("ostore", s), final=True)
        P.build(st)
        nc._prog_stats = (len(P.ops), P.n_sems, P.counts)
    return nc


def _f32(a):
    return np.ascontiguousarray(a, dtype=np.float32)


def prep_weights(inp):
    m = {}
    m["g_ffn1"] = _f32(inp["ffn1_norm_g"][0])
    m["g_mix"] = _f32(inp["mix_norm_g"][0])
    m["g_ffn2"] = _f32(inp["ffn2_norm_g"][0])
    m["g_final"] = _f32(inp["final_norm_g"])

    def fchunks(w):
        C = w.shape[1] // 128
        return _f32(w.reshape(KC, 128, C, 128).transpose(2, 1, 0, 3).reshape(C, 128, KC * 128))

    def rounds(w):
        R = w.shape[0] // 128
        return _f32(w.reshape(R, 128, 8, 512).transpose(2, 0, 1, 3))

    def tgroups(w):
        G = w.shape[1] // 512
        return _f32(w.reshape(4, 8, 128, G, 512).transpose(3, 0, 2, 1, 4).reshape(G, 4, 128, 8 * 512))

    for k in ("ffn1", "ffn2"):
        m[k + "_wg"] = fchunks(inp[k + "_w_gate"][0])
        m[k + "_wu"] = fchunks(inp[k + "_w_up"][0])
        m[k + "_wd"] = rounds(inp[k + "_w_down"][0])
    w_in = inp["w_in"][0]
    nat = fchunks(w_in[:, 0:O_V])
    ga = fchunks(w_in[:, O_GA:O_GA + D])
    gb_ = fchunks(w_in[:, O_GB:O_GB + D])
    m["win_f"] = np.concatenate([nat, ga, gb_], axis=0)
    m["win_t"] = np.concatenate([tgroups(w_in[:, O_V:O_V + DV]), tgroups(w_in[:, O_R:O_R + DV])], axis=0)
    wa = w_in[:, O_A:O_A + RANK]
    m["win_a"] = _f32(wa.reshape(KC, 128, RANK).transpose(1, 0, 2).reshape(128, KC * RANK))

    def ochunks(w):
        return _f32(w.reshape(CC, 128, KC, 128).transpose(2, 1, 0, 3).reshape(KC, 128, CC * 128))

    m["wco"] = ochunks(inp["w_conv_out"][0])
    m["wgo"] = ochunks(inp["w_gla_out"][0])
    wmo = inp["w_mix_out"][0]
    m["wmo"] = _f32(wmo.reshape(4, 8, 128, 8, 512).transpose(3, 0, 2, 1, 4).reshape(8, 4, 128, 8 * 512))
    m["wup"] = _f32(inp["w_alpha_up"][0])
    m["gng"] = _f32(inp["gla_norm_g"][0])
    cw = inp["conv_w"][0]
    pc = np.zeros((128, 3 * CC + CC + 8 + 2 * KC), np.float32)
    pc[:, 0:3 * CC] = cw.reshape(3, CC, 128).transpose(2, 1, 0).reshape(128, 3 * CC)
    pc[:, 3 * CC:4 * CC] = inp["conv_b"][0].reshape(CC, 128).T
    pc[:, 4 * CC:4 * CC + 8] = inp["b_alpha"][0].reshape(8, 128).T
    pc[:, 4 * CC + 8:4 * CC + 8 + KC] = inp["b_merge"][0, 0].reshape(KC, 128).T
    pc[:, 4 * CC + 8 + KC:] = inp["b_merge"][0, 1].reshape(KC, 128).T
    m["pcol"] = pc
    cm = np.zeros((128, 128 * 3 + 512), np.float32)
    cm[:, 0:128] = np.eye(128, dtype=np.float32)
    j = np.arange(128)[:, None]
    i = np.arange(128)[None, :]
    cm[:, 128:256] = (j <= i)
    cm[:, 256:384] = (j > i) & ((j // 64) == (i // 64))
    rm = np.ones(512, np.float32)
    rm[::128] = 0.0
    cm[:, 384:896] = rm[None, :]
    m["cmat"] = cm
    return m


N_CORES = 8
N_TILES = 2


def shard_x(x):
    B = x.shape[0]
    xt = x.reshape(B, SEQ // T, T, D)
    return [_f32(xt[c // 2, (c % 2)::2].reshape(N_TILES * T, D)) for c in range(2 * B)]


def unshard_out(outs, B):
    out = np.empty((B, SEQ // T, T, D), np.float32)
    for c in range(2 * B):
        out[c // 2, (c % 2)::2] = np.asarray(outs[c], dtype=np.float32).reshape(N_TILES, T, D)
    return out.reshape(B, SEQ, D)


def core_inputs(shared, xs, c, nonce):
    mp = dict(shared)
    mp["x"] = xs[c]
    role = np.zeros((128, 2), np.float32)
    role[:, c % 2] = 1.0
    mp["role"] = role
    mp["par"] = np.array([[c % 2]], np.int32)
    mp["nonce"] = np.array([[nonce]], np.int32)
    return mp


def kernel(**inputs):
    inp = {k: np.asarray(v) for k, v in inputs.items()}
    x = inp["x"]
    B = x.shape[0]
    shared = prep_weights(inp)
    nc = build_program(N_TILES, N_TILES)
    xs = shard_x(x)
    nonce = int(np.random.randint(1, 1 << 28)) * 4
    in_maps = [core_inputs(shared, xs, c, nonce) for c in range(N_CORES)]
    res = run_bass_kernel_spmd(nc, in_maps, core_ids=list(range(N_CORES)))
    return unshard_out([res.results[c]["out"] for c in range(N_CORES)], B)
```
